# Optimizing a Trainium2 kernel written in Bass

```python
import jax, jax.numpy as jnp
from jax import lax
import numpy as np

D_MODEL = 1024
BATCH = 8
SEQ = 2048
DEPTH = 1
DEC_BATCH = 128
DEC_SEQ = 1
PAST_LEN = 16384
PAGE_SIZE = 128

D_MIX = 2 * D_MODEL
SSD_WIDTH = D_MIX // 2
SC_WIDTH = D_MIX - SSD_WIDTH
SSD_HEAD_DIM = 64
SSD_HEADS = SSD_WIDTH // SSD_HEAD_DIM
SSD_GROUPS = 2
D_STATE = 128
SSD_CONV_W = 4
SSD_CONV_DIM = SSD_WIDTH + 2 * SSD_GROUPS * D_STATE
SSD_CHUNK = 256
SC_CONV_W = 3
SC_GROUP_DIM = 64
SC_GROUPS = SC_WIDTH // SC_GROUP_DIM
D_FF = 2816
D_IN_PROJ = SSD_WIDTH + SSD_CONV_DIM + SSD_HEADS + 3 * SC_WIDTH
NORM_EPS = 1e-6

kernel_name = 'hymba_ssd_shortconv_macaron_step'


def rmsnorm(x, w):
    xf = x.astype(jnp.float32)
    y = xf * lax.rsqrt(jnp.mean(xf * xf, axis=-1, keepdims=True) + NORM_EPS)
    return (y * w.astype(jnp.float32)).astype(x.dtype)


def swiglu(x, w_gate, w_up, w_down):
    return (jax.nn.silu(x @ w_gate) * (x @ w_up)) @ w_down


def causal_dwconv(u, buf, w):
    K = w.shape[0]
    T = u.shape[1]
    up = jnp.concatenate([buf.astype(u.dtype), u], axis=1)
    y = up[:, 0:T] * w[0]
    for k in range(1, K):
        y = y + up[:, k:k + T] * w[k]
    return y, up[:, T:]


def ssd_scan(x, dt, A, B, C, s0):
    b, T, h, p = x.shape
    g, n = B.shape[2], B.shape[3]
    r = h // g
    l = min(SSD_CHUNK, T)
    pad = (-T) % l
    Tp = T + pad
    c = Tp // l
    f32 = jnp.float32
    padt = lambda a: jnp.pad(a.astype(f32), [(0, 0), (0, pad)] + [(0, 0)] * (a.ndim - 2))
    xc = padt(x).reshape(b, c, l, g, r, p)
    dtc = padt(dt).reshape(b, c, l, g, r)
    Bc = padt(B).reshape(b, c, l, g, n)
    Cc = padt(C).reshape(b, c, l, g, n)
    a = dtc * A.astype(f32).reshape(g, r)
    acum = jnp.cumsum(a, axis=2)
    xdt = xc * dtc[..., None]
    seg = acum[:, :, :, None] - acum[:, :, None, :]
    mask = jnp.tril(jnp.ones((l, l), dtype=bool))[None, None, :, :, None, None]
    Lm = jnp.exp(jnp.where(mask, seg, -jnp.inf))
    CB = jnp.einsum('bclgn,bcsgn->bclsg', Cc, Bc)
    y_diag = jnp.einsum('bclsg,bclsgr,bcsgrp->bclgrp', CB, Lm, xdt)
    decay_end = jnp.exp(acum[:, :, -1:] - acum)
    states = jnp.einsum('bclgn,bclgr,bclgrp->bcgrpn', Bc, decay_end, xdt)
    chunk_decay = jnp.exp(acum[:, :, -1])

    def step(S, inp):
        st, dc = inp
        return S * dc[..., None, None] + st, S

    S_final, S_enter = lax.scan(
        step, s0.astype(f32).reshape(b, g, r, p, n),
        (jnp.moveaxis(states, 1, 0), jnp.moveaxis(chunk_decay, 1, 0)))
    S_enter = jnp.moveaxis(S_enter, 0, 1)
    y_off = jnp.einsum('bclgn,bcgrpn,bclgr->bclgrp', Cc, S_enter, jnp.exp(acum))
    y = (y_diag + y_off).reshape(b, Tp, h, p)[:, :T]
    return y, S_final.reshape(b, h, p, n)


def token_mix(h, ssm0, conv0, sconv0, w_in, ssd_conv_w, ssd_conv_b, dt_bias, a_log,
              d_skip, ssd_norm_w, sconv_w, w_out):
    b, T, _ = h.shape
    proj = h @ w_in
    o1 = SSD_WIDTH
    o2 = o1 + SSD_CONV_DIM
    o3 = o2 + SSD_HEADS
    o4 = o3 + SC_WIDTH
    o5 = o4 + SC_WIDTH
    z = proj[..., :o1]
    xbc = proj[..., o1:o2]
    dt_raw = proj[..., o2:o3]
    sc_b = proj[..., o3:o4]
    sc_c = proj[..., o4:o5]
    sc_h = proj[..., o5:]
    xbc_c, conv_new = causal_dwconv(xbc, conv0, ssd_conv_w)
    xbc_c = jax.nn.silu(xbc_c + ssd_conv_b)
    xs = xbc_c[..., :SSD_WIDTH].reshape(b, T, SSD_HEADS, SSD_HEAD_DIM)
    Bm = xbc_c[..., SSD_WIDTH:SSD_WIDTH + SSD_GROUPS * D_STATE].reshape(b, T, SSD_GROUPS, D_STATE)
    Cm = xbc_c[..., SSD_WIDTH + SSD_GROUPS * D_STATE:].reshape(b, T, SSD_GROUPS, D_STATE)
    dt = jax.nn.softplus(dt_raw.astype(jnp.float32) + dt_bias.astype(jnp.float32))
    A = -jnp.exp(a_log.astype(jnp.float32))
    y_ssd, S_new = ssd_scan(xs, dt, A, Bm, Cm, ssm0)
    y_ssd = (y_ssd + d_skip.astype(jnp.float32)[:, None] * xs.astype(jnp.float32)).astype(h.dtype)
    y_ssd = y_ssd.reshape(b, T, SSD_WIDTH) * jax.nn.silu(z)
    y_ssd = rmsnorm(y_ssd.reshape(b, T, SSD_GROUPS, SSD_WIDTH // SSD_GROUPS),
                    ssd_norm_w.reshape(SSD_GROUPS, SSD_WIDTH // SSD_GROUPS)).reshape(b, T, SSD_WIDTH)
    u = sc_c * sc_h
    v, sconv_new = causal_dwconv(u, sconv0, sconv_w)
    y_sc = sc_b * v
    out = jnp.concatenate([y_ssd, y_sc], axis=-1) @ w_out
    return out, S_new.astype(ssm0.dtype), conv_new.astype(conv0.dtype), sconv_new.astype(sconv0.dtype)


def layer(x, ssm0, conv0, sconv0, norm_ffn1_w, ffn1_w_gate, ffn1_w_up, ffn1_w_down,
          norm_mix_w, w_in, ssd_conv_w, ssd_conv_b, dt_bias, a_log, d_skip, ssd_norm_w,
          sconv_w, w_out, norm_ffn2_w, ffn2_w_gate, ffn2_w_up, ffn2_w_down):
    x = x + 0.5 * swiglu(rmsnorm(x, norm_ffn1_w), ffn1_w_gate, ffn1_w_up, ffn1_w_down)
    mix, S, cb, sb = token_mix(rmsnorm(x, norm_mix_w), ssm0, conv0, sconv0, w_in, ssd_conv_w,
                               ssd_conv_b, dt_bias, a_log, d_skip, ssd_norm_w, sconv_w, w_out)
    x = x + mix
    x = x + 0.5 * swiglu(rmsnorm(x, norm_ffn2_w), ffn2_w_gate, ffn2_w_up, ffn2_w_down)
    return x, S, cb, sb


def setup_inputs(seed: int = 0) -> dict:
    key = jax.random.key(seed)
    ks = jax.random.split(key, 32)
    f32 = jnp.float32
    nrm = lambda k, shape, scale: jax.random.normal(k, shape, f32) * scale
    Ld = DEPTH
    dt0 = jnp.exp(jax.random.uniform(ks[10], (Ld, SSD_HEADS), f32, np.log(1e-3), np.log(1e-1)))
    return {
        'x_prompt': nrm(ks[0], (BATCH, SEQ, D_MODEL), 1.0),
        'x_sample': nrm(ks[1], (DEC_BATCH, DEC_SEQ, D_MODEL), 1.0),
        'state_ssm': nrm(ks[2], (Ld, DEC_BATCH, SSD_HEADS, SSD_HEAD_DIM, D_STATE), 0.1),
        'state_ssd_conv': nrm(ks[3], (Ld, DEC_BATCH, SSD_CONV_W - 1, SSD_CONV_DIM), 1.0),
        'state_sconv': nrm(ks[4], (Ld, DEC_BATCH, SC_CONV_W - 1, SC_WIDTH), 1.0),
        'norm_ffn1_w': 1.0 + nrm(ks[5], (Ld, D_MODEL), 0.02),
        'ffn1_w_gate': nrm(ks[6], (Ld, D_MODEL, D_FF), D_MODEL ** -0.5),
        'ffn1_w_up': nrm(ks[7], (Ld, D_MODEL, D_FF), D_MODEL ** -0.5),
        'ffn1_w_down': nrm(ks[8], (Ld, D_FF, D_MODEL), D_FF ** -0.5),
        'norm_mix_w': 1.0 + nrm(ks[9], (Ld, D_MODEL), 0.02),
        'w_in': nrm(ks[11], (Ld, D_MODEL, D_IN_PROJ), D_MODEL ** -0.5),
        'ssd_conv_w': nrm(ks[12], (Ld, SSD_CONV_W, SSD_CONV_DIM), SSD_CONV_W ** -0.5),
        'ssd_conv_b': nrm(ks[13], (Ld, SSD_CONV_DIM), 0.02),
        'dt_bias': dt0 + jnp.log(-jnp.expm1(-dt0)),
        'a_log': jnp.log(jax.random.uniform(ks[14], (Ld, SSD_HEADS), f32, 1.0, 16.0)),
        'd_skip': 1.0 + nrm(ks[15], (Ld, SSD_HEADS), 0.1),
        'ssd_norm_w': 1.0 + nrm(ks[16], (Ld, SSD_WIDTH), 0.02),
        'sconv_w': nrm(ks[17], (Ld, SC_CONV_W, SC_WIDTH), SC_CONV_W ** -0.5),
        'w_out': nrm(ks[18], (Ld, D_MIX, D_MODEL), D_MIX ** -0.5),
        'norm_ffn2_w': 1.0 + nrm(ks[19], (Ld, D_MODEL), 0.02),
        'ffn2_w_gate': nrm(ks[20], (Ld, D_MODEL, D_FF), D_MODEL ** -0.5),
        'ffn2_w_up': nrm(ks[21], (Ld, D_MODEL, D_FF), D_MODEL ** -0.5),
        'ffn2_w_down': nrm(ks[22], (Ld, D_FF, D_MODEL), D_FF ** -0.5),
        'final_norm_w': 1.0 + nrm(ks[23], (D_MODEL,), 0.02),
    }


def reference(x_prompt, x_sample, state_ssm, state_ssd_conv, state_sconv,
              norm_ffn1_w, ffn1_w_gate, ffn1_w_up, ffn1_w_down, norm_mix_w, w_in,
              ssd_conv_w, ssd_conv_b, dt_bias, a_log, d_skip, ssd_norm_w, sconv_w, w_out,
              norm_ffn2_w, ffn2_w_gate, ffn2_w_up, ffn2_w_down, final_norm_w):
    xp = x_prompt
    xs = x_sample
    bp = x_prompt.shape[0]
    sp_list, cp_list, scp_list = [], [], []
    ss_list, cs_list, scs_list = [], [], []
    for i in range(DEPTH):
        w = (norm_ffn1_w[i], ffn1_w_gate[i], ffn1_w_up[i], ffn1_w_down[i], norm_mix_w[i],
             w_in[i], ssd_conv_w[i], ssd_conv_b[i], dt_bias[i], a_log[i], d_skip[i],
             ssd_norm_w[i], sconv_w[i], w_out[i], norm_ffn2_w[i], ffn2_w_gate[i],
             ffn2_w_up[i], ffn2_w_down[i])
        ssm0 = jnp.zeros((bp,) + state_ssm.shape[2:], state_ssm.dtype)
        conv0 = jnp.zeros((bp,) + state_ssd_conv.shape[2:], state_ssd_conv.dtype)
        sconv0 = jnp.zeros((bp,) + state_sconv.shape[2:], state_sconv.dtype)
        xp, S_p, c_p, sc_p = layer(xp, ssm0, conv0, sconv0, *w)
        xs, S_s, c_s, sc_s = layer(xs, state_ssm[i], state_ssd_conv[i], state_sconv[i], *w)
        sp_list.append(S_p); cp_list.append(c_p); scp_list.append(sc_p)
        ss_list.append(S_s); cs_list.append(c_s); scs_list.append(sc_s)
    y_prompt = rmsnorm(xp, final_norm_w)
    y_sample = rmsnorm(xs, final_norm_w)
    return (y_prompt, y_sample,
            jnp.stack(sp_list), jnp.stack(cp_list), jnp.stack(scp_list),
            jnp.stack(ss_list), jnp.stack(cs_list), jnp.stack(scs_list))
```

```python
import numpy as np
import concourse.bass as bass
import concourse.mybir as mybir
from concourse.bass_utils import run_bass_kernel_spmd

F32 = mybir.dt.float32
BF16 = mybir.dt.bfloat16
AF = mybir.ActivationFunctionType
ALU = mybir.AluOpType
AX = mybir.AxisListType

NCORES = 8
D = 1024
KC = 8
DFF = 2816
NJ = 22
DIN = 5648
SEQ = 2048
TT = 512
NTILE = 4
NBLK = 4
NS = 16
EPS = 1e-6
O_Z, O_XBC, O_DT, O_SCB, O_SCC, O_SCH = 0, 1024, 2560, 2576, 3600, 4624

V_NW1, V_NWM, V_NW2, V_NWF = 0, 8, 16, 24
V_CW = 32
V_CB = 80
V_SW = 92
V_DFM = 116
V_SNW = 124
NV = 132
R_D, R_SNW, R_ALOG, R_DTB = 0, 1024, 2048, 2064
R_ALOG4, R_DTB4 = 2080, 2144
NR = 2208


CFG = dict(tiles=[0, 1, 2, 3], ffn1=True, mixer=True, ssd=True, sample=True, outproj=True, ffn2=True, stateout=True)


class Sem:
    def __init__(self, h):
        self.h = h
        self.n = 0


class Prog:
    def __init__(self, nc, sem_handles):
        self.nc = nc
        self.free_sems = list(sem_handles)
        self.all_sems = []
        self.q = {k: [] for k in ("pe", "act", "dve", "pool", "sp")}
        self.esem = {k: self.new_sem() for k in ("pe", "act", "dve")}
        self.seen = {k: {} for k in self.q}
        self.w = {}
        self.r = {}
        self.named = {}
        self.embed = False

    def new_sem(self):
        s = Sem(self.free_sems.pop())
        self.all_sems.append(s)
        return s

    def stream(self, name):
        if name not in self.named:
            self.named[name] = self.new_sem()
        return self.named[name]

    def _waits(self, eng, reads, writes):
        deps = {}
        for k in reads:
            for (s, v) in self.w.get(k, ()):
                deps[s] = max(deps.get(s, 0), v)
        for k in writes:
            for (s, v) in self.w.get(k, ()):
                deps[s] = max(deps.get(s, 0), v)
            for (s, v) in self.r.get(k, ()):
                deps[s] = max(deps.get(s, 0), v)
        waits = []
        for s, v in deps.items():
            if eng == "pe" and s is self.esem["pe"]:
                continue
            if self.seen[eng].get(s, 0) >= v:
                continue
            self.seen[eng][s] = v
            waits.append((s, v))
        return waits

    def _mark(self, t, reads, writes):
        for k in reads:
            self.r.setdefault(k, []).append(t)
        for k in writes:
            self.w[k] = [t]
            self.r[k] = []

    def op(self, eng, fn, reads=(), writes=(), inc=True):
        waits = self._waits(eng, reads, writes)
        s = self.esem[eng]
        if inc:
            s.n += 1
            t = (s, s.n)
            self.q[eng].append((waits, fn, s, 1, self.embed))
        else:
            t = (s, s.n + 1)
            self.q[eng].append((waits, fn, None, 0, self.embed))
        self._mark(t, reads, writes)

    def dma(self, queue, fn, reads, writes, stream):
        waits = self._waits(queue, reads, writes)
        stream.n += 16
        t = (stream, stream.n)
        self.q[queue].append((waits, fn, stream, 16, self.embed))
        self._mark(t, reads, writes)

    def sync(self, eng, keys):
        waits = self._waits(eng, (), keys)
        if waits:
            self.q[eng].append((waits, None, None, 0, False))

    def barrier(self, engines=("pe", "act", "dve", "sp")):
        for e in engines:
            waits = []
            for s in self.all_sems:
                if e == "pe" and s is self.esem["pe"]:
                    continue
                if s.n > self.seen[e].get(s, 0):
                    self.seen[e][s] = s.n
                    waits.append((s, s.n))
            if waits:
                self.q[e].append((waits, None, None, 0, False))

    def replay(self, eng, e):
        for (waits, fn, s, inc, emb) in self.q[eng]:
            if fn is None or not emb:
                for (ws, wv) in waits:
                    e.wait_ge(ws.h, wv)
                if fn is None:
                    continue
                ins = fn(e)
                if s is not None:
                    ins.then_inc(s.h, inc)
                continue
            for (ws, wv) in waits[:-1]:
                e.wait_ge(ws.h, wv)
            ins = fn(e)
            if waits:
                ins._wait_ge(waits[-1][0].h, waits[-1][1])
            if s is not None:
                ins.then_inc(s.h, inc)

    def mm(self, okey, out, lhsT, rhs, reads, start=True, stop=True, force_inc=False):
        self.op("pe", lambda e: e.matmul(out, lhsT, rhs, start=start, stop=stop),
                reads=reads, writes=[okey], inc=(stop or force_inc))

    def tr(self, okey, out, in_, ident, reads, last=True):
        self.op("pe", lambda e: e.transpose(out, in_, ident), reads=reads, writes=[okey], inc=last)

    def act(self, out, in_, func, reads, writes, bias=None, scale=None, accum=None):
        kw = {}
        if bias is not None:
            kw["bias"] = bias
        if scale is not None:
            kw["scale"] = scale
        if accum is not None:
            kw["accum_out"] = accum
        self.op("act", lambda e: e.activation(out=out, in_=in_, func=func, **kw), reads=reads, writes=writes)

    def tt(self, out, a, b, op, reads, writes):
        self.op("dve", lambda e: e.tensor_tensor(out=out, in0=a, in1=b, op=op), reads=reads, writes=writes)

    def ts(self, out, a, s1, op0, reads, writes, s2=None, op1=None):
        if op1 is None:
            self.op("dve", lambda e: e.tensor_scalar(out, a, s1, None, op0), reads=reads, writes=writes)
        else:
            self.op("dve", lambda e: e.tensor_scalar(out, a, s1, s2, op0, op1), reads=reads, writes=writes)

    def stt(self, out, in0, scalar, in1, op0, op1, reads, writes):
        self.op("dve", lambda e: e.scalar_tensor_tensor(out=out, in0=in0, scalar=scalar, in1=in1, op0=op0, op1=op1),
                reads=reads, writes=writes)

    def cp(self, eng, out, in_, reads, writes):
        if eng == "act":
            self.op("act", lambda e: e.copy(out=out, in_=in_), reads=reads, writes=writes)
        else:
            self.op("dve", lambda e: e.tensor_copy(out=out, in_=in_), reads=reads, writes=writes)

    def memset(self, ap, val, writes):
        self.op("dve", lambda e: e.memset(ap, val), reads=(), writes=writes)


class Arena:
    def __init__(self, t, nwords):
        self.t = t
        self.nwords = nwords
        self.off = 0

    def reset(self, off=0):
        self.off = off

    def alloc(self, shape, dtype, parts=128):
        n = 1
        for s in shape[1:]:
            n *= s
        words = n if dtype == F32 else (n + 1) // 2
        assert self.off + words <= self.nwords, ("arena overflow", self.off, words, self.nwords)
        ap = self.t[0:shape[0], self.off:self.off + words]
        self.off += words
        if dtype != F32:
            ap = ap.bitcast(dtype)[:, 0:n]
        if len(shape) == 3:
            ap = ap.rearrange("p (a b) -> p a b", a=shape[1])
        elif len(shape) == 4:
            ap = ap.rearrange("p (a b c) -> p a b c", a=shape[1], b=shape[2])
        return ap


def build_program():
    nc = bass.Bass("TRN2", target_bir_lowering=False)

    def din(name, shape):
        return nc.dram_tensor(name, list(shape), F32, kind="ExternalInput").ap()

    def dout(name, shape):
        return nc.dram_tensor(name, list(shape), F32, kind="ExternalOutput").ap()

    xp = din("xp", [SEQ, D])
    xsm = din("xsm", [NS, D])
    ssm_in = din("ssm_in", [NS, 8, 128, 128])
    cst_in = din("cst_in", [NS, 3, 1536])
    sst_in = din("sst_in", [NS, 2, 1024])
    wg = [din("wg1", [D, DFF]), din("wg2", [D, DFF])]
    wu = [din("wu1", [D, DFF]), din("wu2", [D, DFF])]
    wd = [din("wd1", [DFF, D]), din("wd2", [DFF, D])]
    win = din("win", [D, DIN])
    wout = din("wout", [2 * D, D])
    vecs_d = din("vecs", [128, NV])
    rowc_d = din("rowc", [128, NR])
    cmat_d = din("cmat", [128, 512])

    yp = dout("yp", [SEQ, D])
    ysm = dout("ysm", [NS, D])
    o_ssm_p = dout("o_ssm_p", [8, 128, 128])
    o_cv_p = dout("o_cv_p", [3, 1536])
    o_sc_p = dout("o_sc_p", [2, 1024])
    o_ssm_s = dout("o_ssm_s", [NS, 8, 128, 128])
    o_cv_s = dout("o_cv_s", [NS, 3, 1536])
    o_sc_s = dout("o_sc_s", [NS, 2, 1024])

    wg_v = [w.rearrange("(k p) n -> p k n", p=128) for w in wg]
    wu_v = [w.rearrange("(k p) n -> p k n", p=128) for w in wu]
    wd_v = [w.rearrange("(j p) m -> p j m", p=128) for w in wd]
    win_v = win.rearrange("(k p) n -> p k n", p=128)
    wout_v = wout.rearrange("(c p) m -> p c m", p=128)

    NCOL = TT + NS
    AW = 23800

    from contextlib import ExitStack
    with ExitStack() as es:
        def sb(name, shape, dt):
            return es.enter_context(nc.sbuf_tensor("sb_" + name, list(shape), dt))

        def ps(name, shape, dt):
            return es.enter_context(nc.psum_tensor("ps_" + name, list(shape), dt))

        sems = [es.enter_context(nc.semaphore(f"s{i}")) for i in range(64)]
        g = Prog(nc, sems)

        xT = sb("xT", [128, KC, NCOL], F32)
        hT = sb("hT", [128, KC, NCOL], BF16)
        rstd = sb("rstd", [128, NCOL], F32)
        lnv = sb("lnv", [128, NCOL], F32)
        sqb = [sb(f"sq{i}", [128, NCOL], BF16) for i in range(2)]
        sgb = [sb(f"sg{i}", [128, TT], F32) for i in range(2)]
        wA = [sb(f"wA{i}", [128, KC, 256], BF16) for i in range(5)]
        wB = [sb(f"wB{i}", [128, NJ, 128], BF16) for i in range(3)]
        cmat = sb("cmat", [128, 512], F32)
        cbf = sb("cbf", [128, 256], BF16)
        vecs = sb("vecs", [128, NV], F32)
        rowc = sb("rowc", [128, NR], F32)
        smallc = sb("smallc", [128, 256], F32)
        S_f = sb("S_f", [128, 1024], F32)
        S_bf = sb("S_bf", [128, 1024], BF16)
        hist = sb("hist", [128, 12, 3], F32)
        uhist = sb("uhist", [128, 8, 2], F32)
        Wz_t = sb("Wz", [128, KC, 1024], BF16)
        Wdt_t = sb("Wdt", [128, KC, 16], BF16)
        arena_t = sb("arena", [128, AW], F32)
        ar = Arena(arena_t, AW)

        banks = [ps(f"pb{i}", [128, 512], F32) for i in range(7)]
        psbf = ps("psbf", [128, 1024], BF16)
        bank_i = [0]

        def bank():
            i = bank_i[0] % 7
            bank_i[0] += 1
            return ("pb", i), banks[i]

        ident_f = cmat[:, 0:128]
        U_f = cmat[:, 128:256]
        Ms_f = cmat[:, 256:384]
        ones_f = cmat[:, 384:512]
        ident_b = cbf[:, 0:128]
        ones_b = cbf[:, 128:256]
        eps_t = smallc[:, 0:1]
        one_t = smallc[:, 1:2]
        A_bc4 = smallc[:, 64:128]
        A_bc = smallc[:, 64:80]

        R2_OFF = (16 * NCOL) // 2 + KC * NCOL + (4 * NCOL) // 2 + 4 * NS + 2 * (3 + TT) + 2 * (2 + TT) + 2 * TT
        wA_i = [0]
        wB_i = [0]
        sg_i = [0]
        wA_s = [g.stream(f"wA{i}") for i in range(5)]
        wB_s = [g.stream(f"wB{i}") for i in range(3)]

        def wloadA(src_ap, ncols):
            i = wA_i[0] % 5
            wA_i[0] += 1
            key = ("wA", i)
            dst = wA[i][:, :, 0:ncols]
            if CFG.get("dma_skip", False) and (wA_i[0] % 2 == 1):
                return key, wA[i]
            g.dma("pool", lambda e: e.dma_start(out=dst, in_=src_ap), reads=(), writes=[key], stream=wA_s[i])
            return key, wA[i]

        def wloadB(src_ap, nrow):
            i = wB_i[0] % 3
            wB_i[0] += 1
            key = ("wB", i)
            dst = wB[i][:, 0:nrow, :]
            g.dma("pool", lambda e: e.dma_start(out=dst, in_=src_ap), reads=(), writes=[key], stream=wB_s[i])
            return key, wB[i]

        wB_pref = {0: [], 1: []}

        def prefetch_wB(which, n=3):
            assert not wB_pref[which]
            for m in range(n):
                wB_pref[which].append(wloadB(wd_v[which][:, :, m * 128:(m + 1) * 128], NJ))

        def spdma(name, out, in_, reads, writes):
            g.dma("sp", lambda e: e.dma_start(out=out, in_=in_), reads=reads, writes=writes, stream=g.stream(name))

        spdma("c0", cmat[:, :], cmat_d[:, :], (), ["cmat"])
        spdma("c1", vecs[:, :], vecs_d[:, :], (), ["vecs"])
        spdma("c2", rowc[:, :], rowc_d[:, :], (), ["rowc"])
        g.cp("dve", cbf[:, 0:128], cmat[:, 0:128], ["cmat"], ["cbf"])
        g.cp("dve", cbf[:, 128:256], cmat[:, 384:512], ["cmat"], ["cbf"])
        g.memset(smallc[:, 0:1], EPS, ["smallc"])
        g.memset(smallc[:, 1:2], 1.0, ["smallc"])
        g.act(smallc[:, 128:192], rowc[:, R_ALOG4:R_ALOG4 + 64], AF.Exp, ["rowc", "smallc"], ["smallc"])
        g.ts(A_bc4, smallc[:, 128:192], -1.0, ALU.mult, ["smallc"], ["smallc"])
        g.memset(S_f[:, :], 0.0, ["S_f"])
        g.memset(S_bf[:, :], 0.0, ["S_bf"])
        g.memset(hist[:, :, :], 0.0, [("hist", c) for c in range(12)])
        g.memset(uhist[:, :, :], 0.0, [("uhist", c) for c in range(8)])
        CONSTK = ["cmat", "cbf", "vecs", "rowc", "smallc"]

        def segs_of(t):
            return [(0, TT)] + ([(TT, NS)] if t == NTILE - 1 else [])

        def norm_rstd(segs, voff):
            ncol = segs[-1][0] + segs[-1][1]
            pbs = [bank() for _ in segs]
            for k in range(KC):
                sq = sqb[k % 2]
                g.act(sq[:, 0:ncol], xT[:, k, 0:ncol], AF.Square, [("xT", k)], [("sq", k % 2)])
                for si, (c0, n) in enumerate(segs):
                    g.mm(pbs[si][0], pbs[si][1][:, 0:n], ones_b, sq[:, c0:c0 + n],
                         [("sq", k % 2), "cbf"], start=(k == 0), stop=(k == KC - 1), force_inc=True)
            for si, (c0, n) in enumerate(segs):
                g.act(lnv[:, c0:c0 + n], pbs[si][1][:, 0:n], AF.Ln, [pbs[si][0], "smallc"], [("lnv", si)],
                      bias=eps_t, scale=1.0 / D)
                g.act(rstd[:, c0:c0 + n], lnv[:, c0:c0 + n], AF.Exp, [("lnv", si)], [("rstd", si)], scale=-0.5)

        def norm_to_hT(segs, voff):
            norm_rstd(segs, voff)
            ncol = segs[-1][0] + segs[-1][1]
            rk = [("rstd", si) for si in range(len(segs))]
            for k in range(KC):
                g.stt(hT[:, k, 0:ncol], xT[:, k, 0:ncol], vecs[:, voff + k:voff + k + 1], rstd[:, 0:ncol],
                      ALU.mult, ALU.mult, [("xT", k), "vecs"] + rk, [("hT", k)])

        def ffn(t, which, voff):
            g.embed = CFG.get("embed_ffn", True)
            segs = segs_of(t)
            ncol = segs[-1][0] + segs[-1][1]
            ar.reset(R2_OFF)
            acts = ar.alloc([128, NJ, NCOL], BF16)
            norm_to_hT(segs, voff)
            hk = [("hT", k) for k in range(KC)]
            g.sync("dve", [("io", 0), ("io", 1), ("io", 2, 0), ("io", 2, 1), ("io", 3, 0), ("io", 3, 1)])
            for s in range(NJ // 2):
                kg, tg = wloadA(wg_v[which][:, :, s * 256:(s + 1) * 256], 256)
                ku, tu = wloadA(wu_v[which][:, :, s * 256:(s + 1) * 256], 256)
                for jj in range(2):
                    j = 2 * s + jj
                    for si, (c0, n) in enumerate(segs):
                        bg = bank()
                        bu = bank()
                        for k in range(KC):
                            g.mm(bg[0], bg[1][:, 0:n], tg[:, k, jj * 128:(jj + 1) * 128], hT[:, k, c0:c0 + n],
                                 [kg, ("hT", k)], start=(k == 0), stop=(k == KC - 1))
                        for k in range(KC):
                            if CFG.get("skip_up", False) and k < KC - 1:
                                continue
                            g.mm(bu[0], bu[1][:, 0:n], tu[:, k, jj * 128:(jj + 1) * 128], hT[:, k, c0:c0 + n],
                                 [ku, ("hT", k)], start=(k == 0 or CFG.get("skip_up", False)), stop=(k == KC - 1))
                        sg_i[0] += 1
                        sgi = sg_i[0] % 2
                        g.act(sgb[sgi][:, 0:n], bg[1][:, 0:n], AF.Silu, [bg[0]], [("sg", sgi)])
                        g.tt(acts[:, j, c0:c0 + n], sgb[sgi][:, 0:n], bu[1][:, 0:n], ALU.mult,
                             [("sg", sgi), bu[0]], [("acts", j, si)])
            g.act(smallc[:, 200:201], one_t, AF.Exp, ["smallc"], ["tblwarm2"])
            for m in range(KC):
                if wB_pref[which]:
                    kd, td = wB_pref[which].pop(0)
                else:
                    kd, td = wloadB(wd_v[which][:, :, m * 128:(m + 1) * 128], NJ)
                for si, (c0, n) in enumerate(segs):
                    bo = bank()
                    for j in range(NJ):
                        g.mm(bo[0], bo[1][:, 0:n], td[:, j, :], acts[:, j, c0:c0 + n],
                             [kd, ("acts", j, si)], start=(j == 0), stop=(j == NJ - 1))
                    g.stt(xT[:, m, c0:c0 + n], bo[1][:, 0:n], 0.5, xT[:, m, c0:c0 + n], ALU.mult, ALU.add,
                          [bo[0], ("xT", m)], [("xT", m)])
            if (which == 0 and t == NTILE - 1) or CFG.get("all_barriers", False):
                g.barrier()

        def load_tile(t):
            g.embed = CFG.get("embed_io", True)
            ar.reset(R2_OFF)
            xrow = [ar.alloc([128, D], F32) for _ in range(2)]
            for b in range(NBLK):
                r0 = t * TT + b * 128
                xr = xrow[b % 2]
                spdma(f"xrow{b % 2}", xr[:, :], xp[r0:r0 + 128, :], (), [("io", b % 2)])
                for hf in range(0 if (CFG.get("load_dma_only", False) and t == 3) else 2):
                    bk = bank()
                    for q in range(4):
                        k = hf * 4 + q
                        g.tr(bk[0], bk[1][:, q * 128:(q + 1) * 128], xr[:, k * 128:(k + 1) * 128], ident_f,
                             [("io", b % 2), "cmat"], last=(q == 3))
                    g.cp("act", xT[:, hf * 4:(hf + 1) * 4, b * 128:(b + 1) * 128],
                         bk[1][:, :].rearrange("p (q c) -> p q c", q=4), [bk[0]],
                         [("xT", hf * 4 + q) for q in range(4)])
            if t == NTILE - 1 and not CFG.get("skip_xs", False):
                xsr = ar.alloc([NS, D], F32)
                spdma("xsr", xsr[:, :], xsm[:, :], (), [("io", 2, 0), ("io", 2, 1)])
                for hf in range(2):
                    bk = bank()
                    for q in range(4):
                        k = hf * 4 + q
                        g.tr(bk[0], bk[1][:, q * NS:(q + 1) * NS], xsr[:, k * 128:(k + 1) * 128], ident_f[0:NS, 0:NS],
                             [("io", 2, 0), ("io", 2, 1), "cmat"], last=(q == 3))
                    g.cp("act", xT[:, hf * 4:(hf + 1) * 4, TT:TT + NS],
                         bk[1][:, 0:4 * NS].rearrange("p (q c) -> p q c", q=4), [bk[0]],
                         [("xT", hf * 4 + q) for q in range(4)])
            if CFG.get("all_barriers", False):
                g.barrier()

        def store_tile(t):
            g.embed = CFG.get("embed_io", True)
            segs = segs_of(t)
            ar.reset(R2_OFF)
            yTb = [ar.alloc([128, KC, 128], F32) for _ in range(2)]
            yrow = [ar.alloc([128, D], F32) for _ in range(2)]
            if CFG.get("st_norm", True):
                norm_rstd(segs, V_NWF)
            rk = [("rstd", si) for si in range(len(segs))]
            nblk = NBLK + (1 if t == NTILE - 1 else 0)
            for b in range(nblk):
                c0 = b * 128
                n = 128 if b < NBLK else NS
                yt = yTb[b % 2]
                yr = yrow[b % 2]
                for k in range(KC):
                    g.stt(yt[:, k, 0:n], xT[:, k, c0:c0 + n], vecs[:, V_NWF + k:V_NWF + k + 1], rstd[:, c0:c0 + n],
                          ALU.mult, ALU.mult, [("xT", k), "vecs"] + rk, [("io", b % 2)])
                for hf in range(2 if CFG.get("st_tr", True) else 0):
                    bk = bank()
                    for q in range(4):
                        k = hf * 4 + q
                        g.tr(bk[0], bk[1][0:n, q * 128:(q + 1) * 128], yt[:, k, 0:n], ident_f,
                             [("io", b % 2), "cmat"], last=(q == 3))
                    g.cp("act", yr[0:n, hf * 512:(hf + 1) * 512], bk[1][0:n, :], [bk[0]], [("io", 2 + b % 2, hf)])
                if not CFG.get("st_dma", True):
                    continue
                if b < NBLK:
                    r0 = t * TT + b * 128
                    spdma(f"yst{b % 2}", yp[r0:r0 + 128, :], yr[:, :], [("io", 2 + b % 2, 0), ("io", 2 + b % 2, 1)],
                          [("ypd", t, b)])
                else:
                    spdma(f"yst{b % 2}", ysm[:, :], yr[0:NS, :], [("io", 2 + b % 2, 0), ("io", 2 + b % 2, 1)],
                          [("ysd",)])
            if CFG.get("all_barriers", False):
                g.barrier()

        def mixer(t):
            g.embed = CFG.get("embed_mix", False)
            segs = segs_of(t)
            last = (t == NTILE - 1)
            ncol = segs[-1][0] + segs[-1][1]
            norm_to_hT(segs, V_NWM)
            hk = [("hT", k) for k in range(KC)]
            ar.reset(0)
            Wz = Wz_t[:, :, :]
            Wdt = Wdt_t[:, :, :]
            ymixT = ar.alloc([128, 16, NCOL], BF16)
            xsT = ar.alloc([128, KC, NCOL], F32)
            BCT = ar.alloc([128, 4, NCOL], BF16)
            BCs_f = ar.alloc([128, 4, NS], F32)
            rawb = [ar.alloc([128, 3 + TT], F32) for _ in range(2)]
            ubuf = [ar.alloc([128, 2 + TT], F32) for _ in range(2)]
            accb = [ar.alloc([128, TT], F32) for _ in range(2)]
            R2 = ar.off
            assert R2 == R2_OFF, (R2, R2_OFF)

            g.dma("pool", lambda e: e.dma_start(out=Wz, in_=win_v[:, :, O_Z:O_Z + 1024]), (), ["Wz"], g.stream("Wz"))
            g.dma("pool", lambda e: e.dma_start(out=Wdt, in_=win_v[:, :, O_DT:O_DT + 16]), (), ["Wdt"], g.stream("Wdt"))

            if last:
                strow = ar.alloc([48, 1536], F32)
                scrow = ar.alloc([32, 1024], F32)
                hs = ar.alloc([128, 12, 48], F32)
                us = ar.alloc([128, 8, 32], F32)
                smp = ar.alloc([128, 8, NS], F32)
                smp2 = ar.alloc([128, 8, NS], F32)
                cvrow = ar.alloc([NS, 1536], F32)
                urow = ar.alloc([NS, 1024], F32)
                spdma("strow", strow[:, :], cst_in.rearrange("b k c -> (b k) c"), (), ["strow"])
                spdma("scrow", scrow[:, :], sst_in.rearrange("b k c -> (b k) c"), (), ["scrow"])
                for c in range(12):
                    bk = bank()
                    g.tr(bk[0], bk[1][:, 0:48], strow[:, c * 128:(c + 1) * 128], ident_f[0:48, 0:48], ["strow", "cmat"])
                    g.cp("act", hs[:, c, :], bk[1][:, 0:48], [bk[0]], [("hs", c)])
                for c in range(8):
                    bk = bank()
                    g.tr(bk[0], bk[1][:, 0:32], scrow[:, c * 128:(c + 1) * 128], ident_f[0:32, 0:32], ["scrow", "cmat"])
                    g.cp("act", us[:, c, :], bk[1][:, 0:32], [bk[0]], [("us", c)])
                spdma("cvpass", o_cv_s[:, 0:2, :], cst_in[:, 1:3, :], (), ["o_cv_s01"])
                spdma("scpass", o_sc_s[:, 0:1, :], sst_in[:, 1:2, :], (), ["o_sc_s0"])

            for qd in range(4):
                kb, tb = wloadA(win_v[:, :, O_SCB + qd * 256:O_SCB + (qd + 1) * 256], 256)
                kc_, tc_ = wloadA(win_v[:, :, O_SCC + qd * 256:O_SCC + (qd + 1) * 256], 256)
                kh, th = wloadA(win_v[:, :, O_SCH + qd * 256:O_SCH + (qd + 1) * 256], 256)
                for jj in range(2):
                    c = qd * 2 + jj
                    for si, (c0, n) in enumerate(segs):
                        pb_, pc_, ph_ = bank(), bank(), bank()
                        for (pk, tw, kw) in ((pb_, tb, kb), (pc_, tc_, kc_), (ph_, th, kh)):
                            for k in range(KC):
                                g.mm(pk[0], pk[1][:, 0:n], tw[:, k, jj * 128:(jj + 1) * 128], hT[:, k, c0:c0 + n],
                                     [kw, ("hT", k)], start=(k == 0), stop=(k == KC - 1))
                        sg_i[0] += 1
                        sgi = sg_i[0] % 2
                        ct = sgb[sgi]
                        g.cp("act", ct[:, 0:n], pc_[1][:, 0:n], [pc_[0]], [("sg", sgi)])
                        w0 = vecs[:, V_SW + c * 3 + 0:V_SW + c * 3 + 1]
                        w1 = vecs[:, V_SW + c * 3 + 1:V_SW + c * 3 + 2]
                        w2 = vecs[:, V_SW + c * 3 + 2:V_SW + c * 3 + 3]
                        if si == 0:
                            ub = ubuf[c % 2]
                            uk = ("ubuf", c % 2)
                            g.cp("dve", ub[:, 0:2], uhist[:, c, :], [("uhist", c)], [uk])
                            g.tt(ub[:, 2:2 + TT], ct[:, 0:TT], ph_[1][:, 0:TT], ALU.mult, [("sg", sgi), ph_[0]], [uk])
                            ac = accb[c % 2]
                            ak = ("accb", c % 2)
                            g.ts(ac[:, :], ub[:, 0:TT], w0, ALU.mult, [uk, "vecs"], [ak])
                            g.stt(ac[:, :], ub[:, 1:1 + TT], w1, ac[:, :], ALU.mult, ALU.add, [uk, ak], [ak])
                            g.stt(ac[:, :], ub[:, 2:2 + TT], w2, ac[:, :], ALU.mult, ALU.add, [uk, ak], [ak])
                            g.tt(ymixT[:, 8 + c, 0:TT], ac[:, :], pb_[1][:, 0:TT], ALU.mult, [ak, pb_[0]],
                                 [("ymix", 8 + c, "p")])
                            g.cp("dve", uhist[:, c, :], ub[:, TT:TT + 2], [uk], [("uhist", c)])
                        else:
                            u_s = smp[:, c, :]
                            v_s = smp2[:, c, :]
                            g.tt(u_s, ct[:, 0:NS], ph_[1][:, 0:NS], ALU.mult, [("sg", sgi), ph_[0]], [("smp", c)])
                            usv = us[:, c, :].rearrange("p (b k) -> p b k", k=2)
                            g.ts(v_s, usv[:, :, 0], w0, ALU.mult, [("us", c), "vecs"], [("smp2", c)])
                            g.stt(v_s, usv[:, :, 1], w1, v_s, ALU.mult, ALU.add, [("us", c), ("smp2", c)], [("smp2", c)])
                            g.stt(v_s, u_s, w2, v_s, ALU.mult, ALU.add, [("smp", c), ("smp2", c)], [("smp2", c)])
                            g.tt(ymixT[:, 8 + c, TT:TT + NS], v_s, pb_[1][:, 0:NS], ALU.mult, [("smp2", c), pb_[0]],
                                 [("ymix", 8 + c, "s")])
                            bk = bank()
                            g.tr(bk[0], bk[1][0:NS, 0:128], u_s, ident_f, [("smp", c), "cmat"])
                            g.cp("act", urow[:, c * 128:(c + 1) * 128], bk[1][0:NS, 0:128], [bk[0]], [("urow", c)])
            if last:
                spdma("urow", o_sc_s[:, 1:2, :], urow[:, :].rearrange("b (o c) -> b o c", o=1),
                      [("urow", c) for c in range(8)], ["o_sc_s1"])

            pend_b = [None]
            for s in range(6):
                kx, tx = wloadA(win_v[:, :, O_XBC + s * 256:O_XBC + (s + 1) * 256], 256)
                for jj in range(2):
                    c = s * 2 + jj
                    cw = [vecs[:, V_CW + c * 4 + i:V_CW + c * 4 + i + 1] for i in range(4)]
                    cb = vecs[:, V_CB + c:V_CB + c + 1]
                    for si, (c0, n) in enumerate(segs):
                        px = bank()
                        for k in range(KC):
                            g.mm(px[0], px[1][:, 0:n], tx[:, k, jj * 128:(jj + 1) * 128], hT[:, k, c0:c0 + n],
                                 [kx, ("hT", k)], start=(k == 0), stop=(k == KC - 1))
                        if si == 0:
                            rb = rawb[c % 2]
                            rk_ = ("rawb", c % 2)
                            g.cp("dve", rb[:, 0:3], hist[:, c, :], [("hist", c)], [rk_])
                            g.cp("act", rb[:, 3:3 + TT], px[1][:, 0:TT], [px[0]], [rk_])
                            ac = accb[c % 2]
                            ak = ("accb", c % 2)
                            g.ts(ac[:, :], rb[:, 0:TT], cw[0], ALU.mult, [rk_, "vecs"], [ak])
                            for i in range(1, 4):
                                g.stt(ac[:, :], rb[:, i:i + TT], cw[i], ac[:, :], ALU.mult, ALU.add, [rk_, ak], [ak])
                            g.cp("dve", hist[:, c, :], rb[:, TT:TT + 3], [rk_], [("hist", c)])

                            def stage_b(c=c, ac=ac, ak=ak, cb=cb):
                                if c < 8:
                                    g.act(xsT[:, c, 0:TT], ac[:, :], AF.Silu, [ak, "vecs"], [("xsT", c, "p")], bias=cb)
                                else:
                                    g.act(BCT[:, c - 8, 0:TT], ac[:, :], AF.Silu, [ak, "vecs"], [("BCT", c - 8)], bias=cb)
                            if pend_b[0] is not None:
                                pend_b[0]()
                            pend_b[0] = stage_b
                        else:
                            rs = ar_small_raw[c]
                            g.cp("act", rs, px[1][:, 0:NS], [px[0]], [("rs", c)])
                            hv = hs[:, c, :].rearrange("p (b k) -> p b k", k=3)
                            a_s = ar_small_acc[c]
                            g.ts(a_s, hv[:, :, 0], cw[0], ALU.mult, [("hs", c), "vecs"], [("as", c)])
                            g.stt(a_s, hv[:, :, 1], cw[1], a_s, ALU.mult, ALU.add, [("hs", c), ("as", c)], [("as", c)])
                            g.stt(a_s, hv[:, :, 2], cw[2], a_s, ALU.mult, ALU.add, [("hs", c), ("as", c)], [("as", c)])
                            g.stt(a_s, rs, cw[3], a_s, ALU.mult, ALU.add, [("rs", c), ("as", c)], [("as", c)])
                            if c < 8:
                                g.act(xsT[:, c, TT:TT + NS], a_s, AF.Silu, [("as", c), "vecs"], [("xsT", c, "s")], bias=cb)
                            else:
                                g.act(BCs_f[:, c - 8, :], a_s, AF.Silu, [("as", c), "vecs"], [("BCs", c - 8)], bias=cb)
                            bk = bank()
                            g.tr(bk[0], bk[1][0:NS, 0:128], rs, ident_f, [("rs", c), "cmat"])
                            g.cp("act", cvrow[:, c * 128:(c + 1) * 128], bk[1][0:NS, 0:128], [bk[0]], [("cvrow", c)])
            if pend_b[0] is not None:
                pend_b[0]()
            if CFG.get("wb_prefetch", True) and CFG["ffn2"]:
                prefetch_wB(1)
            if last:
                spdma("cvrow", o_cv_s[:, 2:3, :], cvrow[:, :].rearrange("b (o c) -> b o c", o=1),
                      [("cvrow", c) for c in range(12)], ["o_cv_s2"])
            if last or CFG.get("all_barriers", False):
                g.barrier()
            return dict(Wz=Wz, Wdt=Wdt, ymixT=ymixT, xsT=xsT, BCT=BCT, BCs_f=BCs_f, R2=R2)

        ar_small = sb("ar_small", [128, 24, NS], F32)
        ar_small_raw = [ar_small[:, c, :] for c in range(12)]
        ar_small_acc = [ar_small[:, 12 + c, :] for c in range(12)]

        def ssd_blocks(t, mx):
            g.embed = CFG.get("embed_mix", False)
            Wz, Wdt, ymixT, xsT, BCT = mx["Wz"], mx["Wdt"], mx["ymixT"], mx["xsT"], mx["BCT"]
            ar.reset(mx["R2"])
            zs = ar.alloc([128, 1024], F32)
            sm = ar.alloc([128, 16, 64], F32)
            xs_tm = ar.alloc([128, 1024], F32)
            xdt = ar.alloc([128, 1024], BF16)
            xdtd = ar.alloc([128, 1024], BF16)
            B_tm = ar.alloc([128, 2, 128], BF16)
            CBm = ar.alloc([128, 2, 128], F32)
            rseg = ar.alloc([128, 8, 128], F32)
            Lm = ar.alloc([128, 8, 128], F32)
            G = ar.alloc([128, 16, 128], BF16)
            y1 = ar.alloc([128, 1024], F32)
            yg = ar.alloc([128, 1024], F32)
            yn = ar.alloc([128, 1024], BF16)
            junk2 = ar.alloc([128, 2, 512], F32)
            hk = [("hT", k) for k in range(KC)]
            dtr, dtt, dta, dte, dtl, dtv, av, acum, totb, eacum, dend, etot, dtde, ss2, ln2, rs2 = [sm[:, i, 0:16] for i in range(16)]
            rsegq = [rseg[:, 0:4, :], rseg[:, 4:8, :]]
            Lmq = [Lm[:, 0:4, :], Lm[:, 4:8, :]]

            def z_mm(b):
                cb0_ = b * 128
                bzs = []
                for hf in range(2):
                    bz = bank()
                    for k in range(KC):
                        g.mm(bz[0], bz[1][:, :], hT[:, k, cb0_:cb0_ + 128], Wz[:, k, hf * 512:(hf + 1) * 512],
                             [("hT", k), "Wz"], start=(k == 0), stop=(k == KC - 1))
                    bzs.append(bz)
                return bzs

            def z_act(bzs):
                for hf in range(2):
                    g.act(zs[:, hf * 512:(hf + 1) * 512], bzs[hf][1][:, :], AF.Silu, [bzs[hf][0]], [("zs", hf)])
                g.act(sm[:, 12, 32:33], one_t, AF.Exp, ["smallc"], ["tblwarm"])

            z_act(z_mm(0))
            for b in range(NBLK):
                cb0 = b * 128
                bd = bank()
                for k in range(KC):
                    g.mm(bd[0], bd[1][:, 0:16], hT[:, k, cb0:cb0 + 128], Wdt[:, k, :], [("hT", k), "Wdt"],
                         start=(k == 0), stop=(k == KC - 1))
                g.tt(dtt, bd[1][:, 0:16], rowc[:, R_DTB:R_DTB + 16], ALU.add, [bd[0], "rowc"], ["dtt"])
                g.act(dta, dtt, AF.Abs, ["dtt"], ["dta"])
                g.act(dte, dta, AF.Exp, ["dta"], ["dte"], scale=-1.0)
                g.act(dtl, dte, AF.Ln, ["dte", "smallc"], ["dtl"], bias=one_t)
                for hf in range(2):
                    bx = bank()
                    for q in range(4):
                        c = hf * 4 + q
                        g.tr(bx[0], bx[1][:, q * 128:(q + 1) * 128], xsT[:, c, cb0:cb0 + 128], ident_f,
                             [("xsT", c, "p"), "cmat"], last=(q == 3))
                    sl = slice(hf * 512, (hf + 1) * 512)
                    g.cp("act", xs_tm[:, sl], bx[1][:, :], [bx[0]], [("xs_tm", hf)])
                for gi in range(2):
                    g.tr("psbf", psbf[:, gi * 128:(gi + 1) * 128], BCT[:, gi, cb0:cb0 + 128], ident_b,
                         [("BCT", gi), "cbf"], last=(gi == 1))
                g.cp("act", B_tm[:, :, :], psbf[:, 0:256].rearrange("p (a b) -> p a b", a=2), ["psbf"], ["B_tm"])
                g.stt(dtv, dtt, 0.0, dtl, ALU.max, ALU.add, ["dtt", "dtl"], ["dtv"])
                g.tt(av, dtv, A_bc, ALU.mult, ["dtv", "smallc"], ["av"])
                ba = bank()
                g.mm(ba[0], ba[1][:, 0:16], U_f, av, ["cmat", "av"])
                g.mm(ba[0], ba[1][:, 16:32], ones_f, av, ["cmat", "av"])
                g.cp("dve", acum, ba[1][:, 0:16], [ba[0]], ["acum"])
                g.act(eacum, ba[1][:, 0:16], AF.Exp, [ba[0]], ["eacum"])
                g.act(etot, ba[1][:, 16:32], AF.Exp, [ba[0]], ["etot"])
                g.tt(dend, ba[1][:, 16:32], acum, ALU.subtract, [ba[0], "acum"], ["dend"])
                g.act(dend, dend, AF.Exp, ["dend"], ["dend"])
                g.tt(dtde, dtv, dend, ALU.mult, ["dtv", "dend"], ["dtde"])
                bc = bank()
                for gi in range(2):
                    g.mm(bc[0], bc[1][:, gi * 128:(gi + 1) * 128], BCT[:, gi, cb0:cb0 + 128], BCT[:, 2 + gi, cb0:cb0 + 128],
                         [("BCT", gi), ("BCT", 2 + gi)])
                for gi in range(2):
                    g.tt(CBm[:, gi, :], bc[1][:, gi * 128:(gi + 1) * 128], U_f, ALU.mult, [bc[0], "cmat"], [("CBm", gi)])
                def seg_a(qq):
                    rq, lq = rsegq[qq % 2], Lmq[qq % 2]
                    g.tt(rq, U_f.unsqueeze(1).to_broadcast([128, 4, 128]),
                         av[:, qq * 4:(qq + 1) * 4].unsqueeze(2).to_broadcast([128, 4, 128]), ALU.mult,
                         ["cmat", "av"], [("rseg", qq % 2)])
                    bs = bank()
                    g.mm(bs[0], bs[1][:, :], Ms_f, rq.rearrange("p a b -> p (a b)"), ["cmat", ("rseg", qq % 2)])
                    g.act(lq.rearrange("p a b -> p (a b)"), bs[1][:, :], AF.Exp, [bs[0]], [("Lm", qq % 2)])

                def seg_b(qq):
                    lq = Lmq[qq % 2]
                    g.tt(G[:, qq * 4:(qq + 1) * 4, :], lq, CBm[:, qq // 2, :].unsqueeze(1).to_broadcast([128, 4, 128]), ALU.mult,
                         [("Lm", qq % 2), ("CBm", qq // 2)], [("G", qq)])

                def xmul(hf):
                    sl = slice(hf * 512, (hf + 1) * 512)
                    g.tt(xdt[:, sl].rearrange("p (h d) -> p h d", h=8), xs_tm[:, sl].rearrange("p (h d) -> p h d", h=8),
                         dtv[:, hf * 8:(hf + 1) * 8].unsqueeze(2).to_broadcast([128, 8, 64]), ALU.mult,
                         [("xs_tm", hf), "dtv"], [("xdt", hf)])
                    g.tt(xdtd[:, sl].rearrange("p (h d) -> p h d", h=8), xs_tm[:, sl].rearrange("p (h d) -> p h d", h=8),
                         dtde[:, hf * 8:(hf + 1) * 8].unsqueeze(2).to_broadcast([128, 8, 64]), ALU.mult,
                         [("xs_tm", hf), "dtde"], [("xdtd", hf)])
                    g.tt(yg[:, sl], xs_tm[:, sl], rowc[:, R_D + hf * 512:R_D + (hf + 1) * 512], ALU.mult,
                         [("xs_tm", hf), "rowc"], [("yg", hf)])

                seg_a(0)
                seg_a(1)
                xmul(0)
                seg_b(0)
                seg_a(2)
                xmul(1)
                seg_b(1)
                seg_a(3)
                seg_b(2)
                seg_b(3)
                by = [bank(), bank()]
                for h in range(16):
                    g.mm(by[h // 8][0], by[h // 8][1][:, (h % 8) * 64:(h % 8 + 1) * 64], G[:, h, :],
                         xdt[:, h * 64:(h + 1) * 64], [("G", h // 4), ("xdt", h // 8)])
                g.cp("act", S_bf[:, :], S_f[:, :], ["S_f"], ["S_bf"])
                bo = [bank(), bank()]
                for gi in range(2):
                    g.mm(bo[gi][0], bo[gi][1][:, :], BCT[:, 2 + gi, cb0:cb0 + 128], S_bf[:, gi * 512:(gi + 1) * 512],
                         [("BCT", 2 + gi), "S_bf"])
                for gi in range(2):
                    sl = slice(gi * 512, (gi + 1) * 512)
                    g.cp("act", y1[:, sl], bo[gi][1][:, :], [bo[gi][0]], [("y1", gi)])
                    g.tt(y1[:, sl].rearrange("p (h d) -> p h d", h=8), y1[:, sl].rearrange("p (h d) -> p h d", h=8),
                         eacum[:, gi * 8:(gi + 1) * 8].unsqueeze(2).to_broadcast([128, 8, 64]), ALU.mult,
                         [("y1", gi), "eacum"], [("y1", gi)])
                    g.tt(y1[:, sl], y1[:, sl], by[gi][1][:, :], ALU.add, [("y1", gi), by[gi][0]], [("y1", gi)])
                    g.tt(yg[:, sl], yg[:, sl], y1[:, sl], ALU.add, [("yg", gi), ("y1", gi)], [("yg", gi)])
                    g.tt(yg[:, sl], yg[:, sl], zs[:, sl], ALU.mult, [("yg", gi), ("zs", gi)], [("yg", gi)])
                bzn = z_mm(b + 1) if b + 1 < NBLK else None
                for gi in range(2):
                    sl = slice(gi * 512, (gi + 1) * 512)
                    g.act(junk2[:, gi, :], yg[:, sl], AF.Square, [("yg", gi)], [("junk", gi)])
                    g.op("dve", lambda e, gi=gi, ss2=ss2: e.tensor_reduce(out=ss2[:, gi:gi + 1], in_=junk2[:, gi, :], axis=AX.X, op=ALU.add),
                         reads=[("junk", gi)], writes=[("ss2", gi)])
                g.act(ln2[:, 0:2], ss2[:, 0:2], AF.Ln, [("ss2", 0), ("ss2", 1), "smallc"], ["ln2"], bias=eps_t, scale=1.0 / 512)
                g.act(rs2[:, 0:2], ln2[:, 0:2], AF.Exp, ["ln2"], ["rs2"], scale=-0.5)
                for gi in range(2):
                    sl = slice(gi * 512, (gi + 1) * 512)
                    g.stt(yn[:, sl], yg[:, sl], rs2[:, gi:gi + 1], rowc[:, R_SNW + gi * 512:R_SNW + (gi + 1) * 512],
                          ALU.mult, ALU.mult, [("yg", gi), "rs2", "rowc"], [("yn", gi)])
                if bzn is not None:
                    z_act(bzn)
                for c in range(8):
                    g.tr("psbf", psbf[:, c * 128:(c + 1) * 128], yn[:, c * 128:(c + 1) * 128], ident_b,
                         [("yn", c // 4), "cbf"], last=(c == 7))
                g.cp("act", ymixT[:, 0:8, cb0:cb0 + 128], psbf[:, :].rearrange("p (a b) -> p a b", a=8), ["psbf"],
                     [("ymix", c, "p", b) for c in range(8)])
                bn = [bank(), bank()]
                for gi in range(2):
                    g.mm(bn[gi][0], bn[gi][1][:, :], B_tm[:, gi, :], xdtd[:, gi * 512:(gi + 1) * 512],
                         ["B_tm", ("xdtd", gi)])
                for gi in range(2):
                    sl = slice(gi * 512, (gi + 1) * 512)
                    g.tt(S_f[:, sl].rearrange("p (h d) -> p h d", h=8), S_f[:, sl].rearrange("p (h d) -> p h d", h=8),
                         etot[:, gi * 8:(gi + 1) * 8].unsqueeze(2).to_broadcast([128, 8, 64]), ALU.mult,
                         ["S_f", "etot"], ["S_f"])
                    g.tt(S_f[:, sl], S_f[:, sl], bn[gi][1][:, :], ALU.add, ["S_f", bn[gi][0]], ["S_f"])
            g.barrier()

        def ssd_sample(mx):
            g.embed = CFG.get("embed_mix", False)
            Wz, Wdt, ymixT, xsT, BCs_f = mx["Wz"], mx["Wdt"], mx["ymixT"], mx["xsT"], mx["BCs_f"]
            ar.reset(mx["R2"])
            Sb = [ar.alloc([128, 8, 128], F32) for _ in range(3)]
            tmpb = ar.alloc([128, 8, 128], F32)
            tmp2 = ar.alloc([128, 8, 128], F32)
            BCbb = [ar.alloc([128, 512], F32) for _ in range(2)]
            dtE = ar.alloc([NS, 2, 1024], F32)
            dtF = ar.alloc([128, 2, 8, NS], F32)
            dtxF = ar.alloc([128, 8, NS], F32)
            ysT = ar.alloc([128, 8, NS], F32)
            zT = ar.alloc([128, 8, NS], F32)
            ygT = ar.alloc([128, 8, NS], F32)
            sqT = ar.alloc([128, 8, NS], F32)
            rsb = ar.alloc([128, 2, NS], F32)
            BC_tm = ar.alloc([NS, 4, 128], F32)
            sel = [ar.alloc([NS, 128], F32) for _ in range(2)]
            smt = ar.alloc([NS, 8, 16], F32)
            dtt, dta, dte, dtl, dtv, av, dA = [smt[:, i, :] for i in range(7)]
            sc0, sc1 = TT, TT + NS
            bd = bank()
            for k in range(KC):
                g.mm(bd[0], bd[1][0:NS, 0:16], hT[:, k, sc0:sc1], Wdt[:, k, :], [("hT", k), "Wdt"],
                     start=(k == 0), stop=(k == KC - 1))
            g.tt(dtt, bd[1][0:NS, 0:16], rowc[0:NS, R_DTB:R_DTB + 16], ALU.add, [bd[0], "rowc"], ["s_dtt"])
            g.act(dta, dtt, AF.Abs, ["s_dtt"], ["s_dta"])
            g.act(dte, dta, AF.Exp, ["s_dta"], ["s_dte"], scale=-1.0)
            g.act(dtl, dte, AF.Ln, ["s_dte", "smallc"], ["s_dtl"], bias=one_t[0:NS, :])
            g.stt(dtv, dtt, 0.0, dtl, ALU.max, ALU.add, ["s_dtt", "s_dtl"], ["s_dtv"])
            g.tt(av, dtv, A_bc[0:NS, :], ALU.mult, ["s_dtv", "smallc"], ["s_av"])
            g.act(dA, av, AF.Exp, ["s_av"], ["s_dA"])
            g.cp("dve", dtE[:, 0, :].rearrange("p (h d) -> p h d", h=16), dtv.unsqueeze(2).to_broadcast([NS, 16, 64]),
                 ["s_dtv"], [("dtE", 0)])
            g.cp("dve", dtE[:, 1, :].rearrange("p (h d) -> p h d", h=16), dA.unsqueeze(2).to_broadcast([NS, 16, 64]),
                 ["s_dA"], [("dtE", 1)])
            for w_ in range(2):
                for hf in range(2):
                    bk = bank()
                    for q in range(4):
                        c = hf * 4 + q
                        g.tr(bk[0], bk[1][:, q * NS:(q + 1) * NS], dtE[:, w_, c * 128:(c + 1) * 128], ident_f[0:NS, 0:NS],
                             [("dtE", w_), "cmat"], last=(q == 3))
                    g.cp("act", dtF[:, w_, hf * 4:(hf + 1) * 4, :], bk[1][:, 0:4 * NS].rearrange("p (q c) -> p q c", q=4),
                         [bk[0]], [("dtF", w_, hf)])
            dk = [("dtF", w_, hf) for w_ in range(2) for hf in range(2)]
            xk = [("xsT", c, "s") for c in range(8)]
            g.tt(dtxF[:, :, :], xsT[:, :, sc0:sc1], dtF[:, 0, :, :], ALU.mult, xk + dk, ["dtxF"])
            bk = bank()
            for i in range(4):
                g.tr(bk[0], bk[1][0:NS, i * 128:(i + 1) * 128], BCs_f[:, i, :], ident_f, [("BCs", i), "cmat"], last=(i == 3))
            g.cp("act", BC_tm[:, :, :], bk[1][0:NS, :].rearrange("p (a b) -> p a b", a=4), [bk[0]], ["BC_tm"])
            NSL = 3

            def s_load(b):
                spdma(f"ssmin{b % NSL}", Sb[b % NSL][:, :, :], ssm_in[b].rearrange("c q n -> q c n"), (), [("Sb", b % NSL)])

            def s_stage_a(b):
                sl_ = sel[b % 2]
                g.cp("dve", sl_[:, :], ident_f[0:NS, b:b + 1].to_broadcast([NS, 128]), ["cmat"], [("sel", b % 2)])
                bb = bank()
                g.mm(bb[0], bb[1][:, :], sl_[:, :], BC_tm[:, :, :].rearrange("p a b -> p (a b)"), [("sel", b % 2), "BC_tm"])
                g.cp("act", BCbb[b % 2][:, :], bb[1][:, :], [bb[0]], [("BCb", b % 2)])

            def s_stage_b(b):
                S = Sb[b % NSL]
                sk = ("Sb", b % NSL)
                BCb = BCbb[b % 2]
                g.tt(S[:, :, :], S[:, :, :], dtF[:, 1, :, b:b + 1].to_broadcast([128, 8, 128]), ALU.mult, [sk] + dk, [sk])
                Bv = BCb[:, 0:256].rearrange("p (g n) -> p g n", g=2).unsqueeze(2).to_broadcast([128, 2, 4, 128])
                Cv = BCb[:, 256:512].rearrange("p (g n) -> p g n", g=2).unsqueeze(2).to_broadcast([128, 2, 4, 128])
                g.tt(tmpb[:, :, :].rearrange("p (g q) n -> p g q n", g=2), Bv,
                     dtxF[:, :, b:b + 1].to_broadcast([128, 8, 128]).rearrange("p (g q) n -> p g q n", g=2), ALU.mult,
                     [("BCb", b % 2), "dtxF"], ["tmpb"])
                g.tt(S[:, :, :], S[:, :, :], tmpb[:, :, :], ALU.add, [sk, "tmpb"], [sk])
                spdma(f"ssmout{b % NSL}", o_ssm_s[b].rearrange("c q n -> q c n"), S[:, :, :], [sk], [("o_ssm_s", b)])
                g.tt(tmp2[:, :, :].rearrange("p (g q) n -> p g q n", g=2), Cv,
                     S[:, :, :].rearrange("p (g q) n -> p g q n", g=2), ALU.mult, [("BCb", b % 2), sk], ["tmp2"])
                g.op("dve", lambda e, b=b: e.tensor_reduce(out=ysT[:, :, b], in_=tmp2[:, :, :], axis=AX.X, op=ALU.add),
                     reads=["tmp2"], writes=[("ysT", b)])
                if b + NSL < NS:
                    s_load(b + NSL)

            for b in range(NSL):
                s_load(b)
            s_stage_a(0)
            for b in range(NS):
                if b + 1 < NS:
                    s_stage_a(b + 1)
                s_stage_b(b)
            yk = [("ysT", b) for b in range(NS)]
            g.tt(tmpb[:, 0, 0:8 * NS].rearrange("p (c b) -> p c b", c=8), xsT[:, :, sc0:sc1],
                 vecs[:, V_DFM:V_DFM + 8].unsqueeze(2).to_broadcast([128, 8, NS]), ALU.mult, xk + ["vecs", "tmpb"], ["tmpb"])
            g.tt(ysT[:, :, :], ysT[:, :, :], tmpb[:, 0, 0:8 * NS].rearrange("p (c b) -> p c b", c=8), ALU.add,
                 yk + ["tmpb"], ["ysTall"])
            for hf in range(2):
                bk = bank()
                for q in range(4):
                    c = hf * 4 + q
                    for k in range(KC):
                        g.mm(bk[0], bk[1][:, q * NS:(q + 1) * NS], Wz[:, k, c * 128:(c + 1) * 128], hT[:, k, sc0:sc1],
                             ["Wz", ("hT", k)], start=(k == 0), stop=(k == KC - 1))
                g.act(zT[:, hf * 4:(hf + 1) * 4, :], bk[1][:, 0:4 * NS].rearrange("p (q c) -> p q c", q=4), AF.Silu,
                      [bk[0]], [("zT", hf)])
            g.tt(ygT[:, :, :], ysT[:, :, :], zT[:, :, :], ALU.mult, ["ysTall", ("zT", 0), ("zT", 1)], ["ygT"])
            g.tt(sqT[:, :, :], ygT[:, :, :], ygT[:, :, :], ALU.mult, ["ygT"], ["sqT"])
            bk = bank()
            for gi in range(2):
                for q in range(4):
                    g.mm(bk[0], bk[1][:, gi * NS:(gi + 1) * NS], ones_f, sqT[:, gi * 4 + q, :], ["cmat", "sqT"],
                         start=(q == 0), stop=(q == 3))
            g.act(rsb[:, :, :], bk[1][:, 0:2 * NS].rearrange("p (g b) -> p g b", g=2), AF.Ln, [bk[0], "smallc"], ["rsb"],
                  bias=eps_t, scale=1.0 / 512)
            g.act(rsb[:, :, :], rsb[:, :, :], AF.Exp, ["rsb"], ["rsb"], scale=-0.5)
            for c in range(8):
                g.stt(ymixT[:, c, sc0:sc1], ygT[:, c, :], vecs[:, V_SNW + c:V_SNW + c + 1], rsb[:, c // 4, :],
                      ALU.mult, ALU.mult, ["ygT", "vecs", "rsb"], [("ymix", c, "s")])
            g.barrier()

        def out_proj(t, mx):
            g.embed = CFG.get("embed_ffn", True)
            ymixT = mx["ymixT"]
            segs = segs_of(t)
            for m in range(KC):
                i = wA_i[0] % 5
                wA_i[0] += 1
                key = ("wA", i)
                dst = wA[i][:, :, :].rearrange("p k c -> p (k c)")[:, 0:2048].rearrange("p (c m) -> p c m", c=16)
                src = wout_v[:, :, m * 128:(m + 1) * 128]
                g.dma("pool", lambda e, dst=dst, src=src: e.dma_start(out=dst, in_=src), (), [key], wA_s[i])
                for si, (c0, n) in enumerate(segs):
                    bo = bank()
                    for c in range(16):
                        if si == 1:
                            yk_ = [("ymix", c, "s")]
                        elif c >= 8:
                            yk_ = [("ymix", c, "p")]
                        else:
                            yk_ = [("ymix", c, "p", b_) for b_ in range(NBLK)]
                        g.mm(bo[0], bo[1][:, 0:n], dst[:, c, :], ymixT[:, c, c0:c0 + n], [key] + yk_, start=(c == 0), stop=(c == 15))
                    g.tt(xT[:, m, c0:c0 + n], bo[1][:, 0:n], xT[:, m, c0:c0 + n], ALU.add, [bo[0], ("xT", m)], [("xT", m)])
            if CFG.get("all_barriers", False):
                g.barrier()

        def state_outputs():
            g.embed = False
            ar.reset(0)
            Sout = ar.alloc([128, 8, 128], F32)
            crow = ar.alloc([3, 1536], F32)
            srow = ar.alloc([2, 1024], F32)
            for hf in range(2):
                bk = bank()
                for q in range(4):
                    c = hf * 4 + q
                    g.tr(bk[0], bk[1][:, q * 128:(q + 1) * 128], S_f[:, c * 128:(c + 1) * 128], ident_f, ["S_f", "cmat"],
                         last=(q == 3))
                g.cp("act", Sout[:, hf * 4:(hf + 1) * 4, :], bk[1][:, :].rearrange("p (q c) -> p q c", q=4), [bk[0]],
                     [("Sout", hf)])
            spdma("sout", o_ssm_p.rearrange("c q n -> q c n"), Sout[:, :, :], [("Sout", 0), ("Sout", 1)], ["o_ssm_p"])
            for c in range(12):
                bk = bank()
                g.tr(bk[0], bk[1][0:3, 0:128], hist[:, c, :], ident_f, [("hist", c), "cmat"])
                g.cp("act", crow[:, c * 128:(c + 1) * 128], bk[1][0:3, 0:128], [bk[0]], [("crow", c)])
            spdma("crow", o_cv_p[:, :], crow[:, :], [("crow", c) for c in range(12)], ["o_cv_p"])
            for c in range(8):
                bk = bank()
                g.tr(bk[0], bk[1][0:2, 0:128], uhist[:, c, :], ident_f, [("uhist", c), "cmat"])
                g.cp("act", srow[:, c * 128:(c + 1) * 128], bk[1][0:2, 0:128], [bk[0]], [("srow", c)])
            spdma("srow", o_sc_p[:, :], srow[:, :], [("srow", c) for c in range(8)], ["o_sc_p"])

        ph = [0]
        if CFG.get("wb_prefetch", True):
            prefetch_wB(0)

        def ok():
            ph[0] += 1
            return ph[0] <= CFG.get("max_phase", 10 ** 9)

        for t in CFG["tiles"]:
            if ok():
                load_tile(t)
            if CFG["ffn1"]:
                for _ in range(CFG.get("ffn_rep", 1)):
                    if ok():
                        ffn(t, 0, V_NW1)
            if CFG["mixer"]:
                if ok():
                    mx = mixer(t)
                if CFG["ssd"]:
                    if ok():
                        ssd_blocks(t, mx)
                    if t == NTILE - 1 and CFG["sample"]:
                        if ok():
                            ssd_sample(mx)
                if CFG["outproj"]:
                    if ok():
                        out_proj(t, mx)
            if CFG["ffn2"]:
                for _ in range(CFG.get("ffn_rep", 1)):
                    if ok():
                        ffn(t, 1, V_NW2)
                if CFG.get("wb_prefetch", True) and t != CFG["tiles"][-1]:
                    prefetch_wB(0)
            if CFG.get("store", True):
                if ok():
                    store_tile(t)
        if CFG["stateout"] and ok():
            state_outputs()
        g.barrier(engines=("sp",))

        with nc.Block() as block:
            @block.tensor
            def _(e):
                g.replay("pe", e)

            @block.scalar
            def _(e):
                g.replay("act", e)

            @block.vector
            def _(e):
                g.replay("dve", e)

            @block.gpsimd
            def _(e):
                g.replay("pool", e)

            @block.sync
            def _(e):
                g.replay("sp", e)
    return nc


_NC_CACHE = {}


def _host_layout(inp):
    f = np.float32
    def fm(v, n):
        return np.ascontiguousarray(np.asarray(v, f).reshape(n, 128).T)
    vecs = np.zeros((128, NV), f)
    vecs[:, V_NW1:V_NW1 + 8] = fm(inp["norm_ffn1_w"][0], 8)
    vecs[:, V_NWM:V_NWM + 8] = fm(inp["norm_mix_w"][0], 8)
    vecs[:, V_NW2:V_NW2 + 8] = fm(inp["norm_ffn2_w"][0], 8)
    vecs[:, V_NWF:V_NWF + 8] = fm(inp["final_norm_w"], 8)
    cw = np.asarray(inp["ssd_conv_w"][0], f)
    vecs[:, V_CW:V_CW + 48] = cw.reshape(4, 12, 128).transpose(2, 1, 0).reshape(128, 48)
    vecs[:, V_CB:V_CB + 12] = fm(inp["ssd_conv_b"][0], 12)
    sw = np.asarray(inp["sconv_w"][0], f)
    vecs[:, V_SW:V_SW + 24] = sw.reshape(3, 8, 128).transpose(2, 1, 0).reshape(128, 24)
    dsk = np.repeat(np.asarray(inp["d_skip"][0], f), 64)
    vecs[:, V_DFM:V_DFM + 8] = fm(dsk, 8)
    vecs[:, V_SNW:V_SNW + 8] = fm(inp["ssd_norm_w"][0], 8)
    rowc = np.zeros((128, NR), f)
    rowc[:, R_D:R_D + 1024] = dsk[None, :]
    rowc[:, R_SNW:R_SNW + 1024] = np.asarray(inp["ssd_norm_w"][0], f)[None, :]
    rowc[:, R_ALOG:R_ALOG + 16] = np.asarray(inp["a_log"][0], f)[None, :]
    rowc[:, R_DTB:R_DTB + 16] = np.asarray(inp["dt_bias"][0], f)[None, :]
    rowc[:, R_ALOG4:R_ALOG4 + 64] = np.tile(np.asarray(inp["a_log"][0], f), 4)[None, :]
    rowc[:, R_DTB4:R_DTB4 + 64] = np.tile(np.asarray(inp["dt_bias"][0], f), 4)[None, :]
    cm = np.zeros((128, 512), f)
    i = np.arange(128)
    cm[:, 0:128] = np.eye(128, dtype=f)
    cm[:, 128:256] = (i[:, None] <= i[None, :]).astype(f)
    cm[:, 256:384] = (i[:, None] > i[None, :]).astype(f)
    cm[:, 384:512] = 1.0
    return vecs, rowc, cm


def kernel(**inp):
    f = np.float32
    if "nc" not in _NC_CACHE:
        _NC_CACHE["nc"] = build_program()
    nc = _NC_CACHE["nc"]
    vecs, rowc, cm = _host_layout(inp)
    shared = {
        "wg1": np.ascontiguousarray(inp["ffn1_w_gate"][0], f), "wu1": np.ascontiguousarray(inp["ffn1_w_up"][0], f),
        "wd1": np.ascontiguousarray(inp["ffn1_w_down"][0], f),
        "wg2": np.ascontiguousarray(inp["ffn2_w_gate"][0], f), "wu2": np.ascontiguousarray(inp["ffn2_w_up"][0], f),
        "wd2": np.ascontiguousarray(inp["ffn2_w_down"][0], f),
        "win": np.ascontiguousarray(inp["w_in"][0], f), "wout": np.ascontiguousarray(inp["w_out"][0], f),
        "vecs": vecs, "rowc": rowc, "cmat": cm,
    }
    in_maps = []
    for c in range(NCORES):
        m = dict(shared)
        sl = slice(c * NS, (c + 1) * NS)
        m["xp"] = np.ascontiguousarray(inp["x_prompt"][c], f)
        m["xsm"] = np.ascontiguousarray(inp["x_sample"][sl, 0, :], f)
        m["ssm_in"] = np.ascontiguousarray(inp["state_ssm"][0, sl], f).reshape(NS, 8, 128, 128)
        m["cst_in"] = np.ascontiguousarray(inp["state_ssd_conv"][0, sl], f)
        m["sst_in"] = np.ascontiguousarray(inp["state_sconv"][0, sl], f)
        in_maps.append(m)
    res = run_bass_kernel_spmd(nc, in_maps, core_ids=list(range(NCORES)))
    R = res.results
    y_prompt = np.stack([R[c]["yp"] for c in range(NCORES)], 0).astype(f)
    y_sample = np.concatenate([R[c]["ysm"] for c in range(NCORES)], 0).reshape(NCORES * NS, 1, D).astype(f)
    ssm_p = np.stack([R[c]["o_ssm_p"].reshape(16, 64, 128) for c in range(NCORES)], 0)[None].astype(f)
    cv_p = np.stack([R[c]["o_cv_p"] for c in range(NCORES)], 0)[None].astype(f)
    sc_p = np.stack([R[c]["o_sc_p"] for c in range(NCORES)], 0)[None].astype(f)
    ssm_s = np.concatenate([R[c]["o_ssm_s"].reshape(NS, 16, 64, 128) for c in range(NCORES)], 0)[None].astype(f)
    cv_s = np.concatenate([R[c]["o_cv_s"] for c in range(NCORES)], 0)[None].astype(f)
    sc_s = np.concatenate([R[c]["o_sc_s"] for c in range(NCORES)], 0)[None].astype(f)
    return (y_prompt, y_sample, ssm_p, cv_p, sc_p, ssm_s, cv_s, sc_s)
```

```python
import numpy as np
import concourse.bass as bass
import concourse.mybir as mybir
from concourse.bass_utils import run_bass_kernel_spmd

F32 = mybir.dt.float32
BF16 = mybir.dt.bfloat16
AF = mybir.ActivationFunctionType
ALU = mybir.AluOpType
AX = mybir.AxisListType

NCORES = 8
D = 1024
KC = 8
DFF = 2816
NJ = 22
DIN = 5648
SEQ = 2048
TT = 512
NTILE = 4
NBLK = 4
NS = 16
EPS = 1e-6
O_Z, O_XBC, O_DT, O_SCB, O_SCC, O_SCH = 0, 1024, 2560, 2576, 3600, 4624

V_NW1, V_NWM, V_NW2, V_NWF = 0, 8, 16, 24
V_CW = 32
V_CB = 80
V_SW = 92
V_DFM = 116
V_SNW = 124
NV = 132
R_D, R_SNW, R_ALOG, R_DTB = 0, 1024, 2048, 2064
R_ALOG4, R_DTB4 = 2080, 2144
NR = 2208


CFG = dict(tiles=[0, 1, 2, 3], ffn1=True, mixer=True, ssd=True, sample=True, outproj=True, ffn2=True, stateout=True)


class Sem:
    def __init__(self, h):
        self.h = h
        self.n = 0


class Prog:
    def __init__(self, nc, sem_handles):
        self.nc = nc
        self.free_sems = list(sem_handles)
        self.all_sems = []
        self.q = {k: [] for k in ("pe", "act", "dve", "pool", "sp")}
        self.esem = {k: self.new_sem() for k in ("pe", "act", "dve")}
        self.seen = {k: {} for k in self.q}
        self.w = {}
        self.r = {}
        self.named = {}
        self.embed = False

    def new_sem(self):
        s = Sem(self.free_sems.pop())
        self.all_sems.append(s)
        return s

    def stream(self, name):
        if name not in self.named:
            self.named[name] = self.new_sem()
        return self.named[name]

    def _waits(self, eng, reads, writes):
        deps = {}
        for k in reads:
            for (s, v) in self.w.get(k, ()):
                deps[s] = max(deps.get(s, 0), v)
        for k in writes:
            for (s, v) in self.w.get(k, ()):
                deps[s] = max(deps.get(s, 0), v)
            for (s, v) in self.r.get(k, ()):
                deps[s] = max(deps.get(s, 0), v)
        waits = []
        for s, v in deps.items():
            if eng == "pe" and s is self.esem["pe"]:
                continue
            if self.seen[eng].get(s, 0) >= v:
                continue
            self.seen[eng][s] = v
            waits.append((s, v))
        return waits

    def _mark(self, t, reads, writes):
        for k in reads:
            self.r.setdefault(k, []).append(t)
        for k in writes:
            self.w[k] = [t]
            self.r[k] = []

    def op(self, eng, fn, reads=(), writes=(), inc=True):
        waits = self._waits(eng, reads, writes)
        s = self.esem[eng]
        if inc:
            s.n += 1
            t = (s, s.n)
            self.q[eng].append((waits, fn, s, 1, self.embed))
        else:
            t = (s, s.n + 1)
            self.q[eng].append((waits, fn, None, 0, self.embed))
        self._mark(t, reads, writes)

    def dma(self, queue, fn, reads, writes, stream):
        waits = self._waits(queue, reads, writes)
        stream.n += 16
        t = (stream, stream.n)
        self.q[queue].append((waits, fn, stream, 16, self.embed))
        self._mark(t, reads, writes)

    def sync(self, eng, keys):
        waits = self._waits(eng, (), keys)
        if waits:
            self.q[eng].append((waits, None, None, 0, False))

    def barrier(self, engines=("pe", "act", "dve", "sp")):
        for e in engines:
            waits = []
            for s in self.all_sems:
                if e == "pe" and s is self.esem["pe"]:
                    continue
                if s.n > self.seen[e].get(s, 0):
                    self.seen[e][s] = s.n
                    waits.append((s, s.n))
            if waits:
                self.q[e].append((waits, None, None, 0, False))

    def replay(self, eng, e):
        for (waits, fn, s, inc, emb) in self.q[eng]:
            if fn is None or not emb:
                for (ws, wv) in waits:
                    e.wait_ge(ws.h, wv)
                if fn is None:
                    continue
                ins = fn(e)
                if s is not None:
                    ins.then_inc(s.h, inc)
                continue
            for (ws, wv) in waits[:-1]:
                e.wait_ge(ws.h, wv)
            ins = fn(e)
            if waits:
                ins._wait_ge(waits[-1][0].h, waits[-1][1])
            if s is not None:
                ins.then_inc(s.h, inc)

    def mm(self, okey, out, lhsT, rhs, reads, start=True, stop=True, force_inc=False):
        self.op("pe", lambda e: e.matmul(out, lhsT, rhs, start=start, stop=stop),
                reads=reads, writes=[okey], inc=(stop or force_inc))

    def tr(self, okey, out, in_, ident, reads, last=True):
        self.op("pe", lambda e: e.transpose(out, in_, ident), reads=reads, writes=[okey], inc=last)

    def act(self, out, in_, func, reads, writes, bias=None, scale=None, accum=None):
        kw = {}
        if bias is not None:
            kw["bias"] = bias
        if scale is not None:
            kw["scale"] = scale
        if accum is not None:
            kw["accum_out"] = accum
        self.op("act", lambda e: e.activation(out=out, in_=in_, func=func, **kw), reads=reads, writes=writes)

    def tt(self, out, a, b, op, reads, writes):
        self.op("dve", lambda e: e.tensor_tensor(out=out, in0=a, in1=b, op=op), reads=reads, writes=writes)

    def ts(self, out, a, s1, op0, reads, writes, s2=None, op1=None):
        if op1 is None:
            self.op("dve", lambda e: e.tensor_scalar(out, a, s1, None, op0), reads=reads, writes=writes)
        else:
            self.op("dve", lambda e: e.tensor_scalar(out, a, s1, s2, op0, op1), reads=reads, writes=writes)

    def stt(self, out, in0, scalar, in1, op0, op1, reads, writes):
        self.op("dve", lambda e: e.scalar_tensor_tensor(out=out, in0=in0, scalar=scalar, in1=in1, op0=op0, op1=op1),
                reads=reads, writes=writes)

    def cp(self, eng, out, in_, reads, writes):
        if eng == "act":
            self.op("act", lambda e: e.copy(out=out, in_=in_), reads=reads, writes=writes)
        else:
            self.op("dve", lambda e: e.tensor_copy(out=out, in_=in_), reads=reads, writes=writes)

    def memset(self, ap, val, writes):
        self.op("dve", lambda e: e.memset(ap, val), reads=(), writes=writes)


class Arena:
    def __init__(self, t, nwords):
        self.t = t
        self.nwords = nwords
        self.off = 0

    def reset(self, off=0):
        self.off = off

    def alloc(self, shape, dtype, parts=128):
        n = 1
        for s in shape[1:]:
            n *= s
        words = n if dtype == F32 else (n + 1) // 2
        assert self.off + words <= self.nwords, ("arena overflow", self.off, words, self.nwords)
        ap = self.t[0:shape[0], self.off:self.off + words]
        self.off += words
        if dtype != F32:
            ap = ap.bitcast(dtype)[:, 0:n]
        if len(shape) == 3:
            ap = ap.rearrange("p (a b) -> p a b", a=shape[1])
        elif len(shape) == 4:
            ap = ap.rearrange("p (a b c) -> p a b c", a=shape[1], b=shape[2])
        return ap


def build_program():
    nc = bass.Bass("TRN2", target_bir_lowering=False)

    def din(name, shape):
        return nc.dram_tensor(name, list(shape), F32, kind="ExternalInput").ap()

    def dout(name, shape):
        return nc.dram_tensor(name, list(shape), F32, kind="ExternalOutput").ap()

    xp = din("xp", [SEQ, D])
    xsm = din("xsm", [NS, D])
    ssm_in = din("ssm_in", [NS, 8, 128, 128])
    cst_in = din("cst_in", [NS, 3, 1536])
    sst_in = din("sst_in", [NS, 2, 1024])
    wg = [din("wg1", [D, DFF]), din("wg2", [D, DFF])]
    wu = [din("wu1", [D, DFF]), din("wu2", [D, DFF])]
    wd = [din("wd1", [DFF, D]), din("wd2", [DFF, D])]
    win = din("win", [D, DIN])
    wout = din("wout", [2 * D, D])
    vecs_d = din("vecs", [128, NV])
    rowc_d = din("rowc", [128, NR])
    cmat_d = din("cmat", [128, 512])

    yp = dout("yp", [SEQ, D])
    ysm = dout("ysm", [NS, D])
    o_ssm_p = dout("o_ssm_p", [8, 128, 128])
    o_cv_p = dout("o_cv_p", [3, 1536])
    o_sc_p = dout("o_sc_p", [2, 1024])
    o_ssm_s = dout("o_ssm_s", [NS, 8, 128, 128])
    o_cv_s = dout("o_cv_s", [NS, 3, 1536])
    o_sc_s = dout("o_sc_s", [NS, 2, 1024])

    wg_v = [w.rearrange("(k p) n -> p k n", p=128) for w in wg]
    wu_v = [w.rearrange("(k p) n -> p k n", p=128) for w in wu]
    wd_v = [w.rearrange("(j p) m -> p j m", p=128) for w in wd]
    win_v = win.rearrange("(k p) n -> p k n", p=128)
    wout_v = wout.rearrange("(c p) m -> p c m", p=128)

    NCOL = TT + NS
    AW = 23800

    from contextlib import ExitStack
    with ExitStack() as es:
        def sb(name, shape, dt):
            return es.enter_context(nc.sbuf_tensor("sb_" + name, list(shape), dt))

        def ps(name, shape, dt):
            return es.enter_context(nc.psum_tensor("ps_" + name, list(shape), dt))

        sems = [es.enter_context(nc.semaphore(f"s{i}")) for i in range(64)]
        g = Prog(nc, sems)

        xT = sb("xT", [128, KC, NCOL], F32)
        hT = sb("hT", [128, KC, NCOL], BF16)
        rstd = sb("rstd", [128, NCOL], F32)
        lnv = sb("lnv", [128, NCOL], F32)
        sqb = [sb(f"sq{i}", [128, NCOL], BF16) for i in range(2)]
        sgb = [sb(f"sg{i}", [128, TT], F32) for i in range(2)]
        wA = [sb(f"wA{i}", [128, KC, 256], BF16) for i in range(5)]
        wB = [sb(f"wB{i}", [128, NJ, 128], BF16) for i in range(3)]
        cmat = sb("cmat", [128, 512], F32)
        cbf = sb("cbf", [128, 256], BF16)
        vecs = sb("vecs", [128, NV], F32)
        rowc = sb("rowc", [128, NR], F32)
        smallc = sb("smallc", [128, 256], F32)
        S_f = sb("S_f", [128, 1024], F32)
        S_bf = sb("S_bf", [128, 1024], BF16)
        hist = sb("hist", [128, 12, 3], F32)
        uhist = sb("uhist", [128, 8, 2], F32)
        Wz_t = sb("Wz", [128, KC, 1024], BF16)
        Wdt_t = sb("Wdt", [128, KC, 16], BF16)
        arena_t = sb("arena", [128, AW], F32)
        ar = Arena(arena_t, AW)

        banks = [ps(f"pb{i}", [128, 512], F32) for i in range(7)]
        psbf = ps("psbf", [128, 1024], BF16)
        bank_i = [0]

        def bank():
            i = bank_i[0] % 7
            bank_i[0] += 1
            return ("pb", i), banks[i]

        ident_f = cmat[:, 0:128]
        U_f = cmat[:, 128:256]
        Ms_f = cmat[:, 256:384]
        ones_f = cmat[:, 384:512]
        ident_b = cbf[:, 0:128]
        ones_b = cbf[:, 128:256]
        eps_t = smallc[:, 0:1]
        one_t = smallc[:, 1:2]
        A_bc4 = smallc[:, 64:128]
        A_bc = smallc[:, 64:80]

        R2_OFF = (16 * NCOL) // 2 + KC * NCOL + (4 * NCOL) // 2 + 4 * NS + 2 * (3 + TT) + 2 * (2 + TT) + 2 * TT
        wA_i = [0]
        wB_i = [0]
        sg_i = [0]
        wA_s = [g.stream(f"wA{i}") for i in range(5)]
        wB_s = [g.stream(f"wB{i}") for i in range(3)]

        def wloadA(src_ap, ncols):
            i = wA_i[0] % 5
            wA_i[0] += 1
            key = ("wA", i)
            dst = wA[i][:, :, 0:ncols]
            if CFG.get("dma_skip", False) and (wA_i[0] % 2 == 1):
                return key, wA[i]
            g.dma("pool", lambda e: e.dma_start(out=dst, in_=src_ap), reads=(), writes=[key], stream=wA_s[i])
            return key, wA[i]

        def wloadB(src_ap, nrow):
            i = wB_i[0] % 3
            wB_i[0] += 1
            key = ("wB", i)
            dst = wB[i][:, 0:nrow, :]
            g.dma("pool", lambda e: e.dma_start(out=dst, in_=src_ap), reads=(), writes=[key], stream=wB_s[i])
            return key, wB[i]

        wB_pref = {0: [], 1: []}

        def prefetch_wB(which, n=3):
            assert not wB_pref[which]
            for m in range(n):
                wB_pref[which].append(wloadB(wd_v[which][:, :, m * 128:(m + 1) * 128], NJ))

        def spdma(name, out, in_, reads, writes):
            g.dma("sp", lambda e: e.dma_start(out=out, in_=in_), reads=reads, writes=writes, stream=g.stream(name))

        spdma("c0", cmat[:, :], cmat_d[:, :], (), ["cmat"])
        spdma("c1", vecs[:, :], vecs_d[:, :], (), ["vecs"])
        spdma("c2", rowc[:, :], rowc_d[:, :], (), ["rowc"])
        g.cp("dve", cbf[:, 0:128], cmat[:, 0:128], ["cmat"], ["cbf"])
        g.cp("dve", cbf[:, 128:256], cmat[:, 384:512], ["cmat"], ["cbf"])
        g.memset(smallc[:, 0:1], EPS, ["smallc"])
        g.memset(smallc[:, 1:2], 1.0, ["smallc"])
        g.act(smallc[:, 128:192], rowc[:, R_ALOG4:R_ALOG4 + 64], AF.Exp, ["rowc", "smallc"], ["smallc"])
        g.ts(A_bc4, smallc[:, 128:192], -1.0, ALU.mult, ["smallc"], ["smallc"])
        g.memset(S_f[:, :], 0.0, ["S_f"])
        g.memset(S_bf[:, :], 0.0, ["S_bf"])
        g.memset(hist[:, :, :], 0.0, [("hist", c) for c in range(12)])
        g.memset(uhist[:, :, :], 0.0, [("uhist", c) for c in range(8)])
        CONSTK = ["cmat", "cbf", "vecs", "rowc", "smallc"]

        def segs_of(t):
            return [(0, TT)] + ([(TT, NS)] if t == NTILE - 1 else [])

        def norm_rstd(segs, voff):
            ncol = segs[-1][0] + segs[-1][1]
            pbs = [bank() for _ in segs]
            for k in range(KC):
                sq = sqb[k % 2]
                g.act(sq[:, 0:ncol], xT[:, k, 0:ncol], AF.Square, [("xT", k)], [("sq", k % 2)])
                for si, (c0, n) in enumerate(segs):
                    g.mm(pbs[si][0], pbs[si][1][:, 0:n], ones_b, sq[:, c0:c0 + n],
                         [("sq", k % 2), "cbf"], start=(k == 0), stop=(k == KC - 1), force_inc=True)
            for si, (c0, n) in enumerate(segs):
                g.act(lnv[:, c0:c0 + n], pbs[si][1][:, 0:n], AF.Ln, [pbs[si][0], "smallc"], [("lnv", si)],
                      bias=eps_t, scale=1.0 / D)
                g.act(rstd[:, c0:c0 + n], lnv[:, c0:c0 + n], AF.Exp, [("lnv", si)], [("rstd", si)], scale=-0.5)

        def norm_to_hT(segs, voff):
            norm_rstd(segs, voff)
            ncol = segs[-1][0] + segs[-1][1]
            rk = [("rstd", si) for si in range(len(segs))]
            for k in range(KC):
                g.stt(hT[:, k, 0:ncol], xT[:, k, 0:ncol], vecs[:, voff + k:voff + k + 1], rstd[:, 0:ncol],
                      ALU.mult, ALU.mult, [("xT", k), "vecs"] + rk, [("hT", k)])

        def ffn(t, which, voff):
            g.embed = CFG.get("embed_ffn", True)
            segs = segs_of(t)
            ncol = segs[-1][0] + segs[-1][1]
            ar.reset(R2_OFF)
            acts = ar.alloc([128, NJ, NCOL], BF16)
            norm_to_hT(segs, voff)
            hk = [("hT", k) for k in range(KC)]
            g.sync("dve", [("io", 0), ("io", 1), ("io", 2, 0), ("io", 2, 1), ("io", 3, 0), ("io", 3, 1)])
            for s in range(NJ // 2):
                kg, tg = wloadA(wg_v[which][:, :, s * 256:(s + 1) * 256], 256)
                ku, tu = wloadA(wu_v[which][:, :, s * 256:(s + 1) * 256], 256)
                for jj in range(2):
                    j = 2 * s + jj
                    for si, (c0, n) in enumerate(segs):
                        bg = bank()
                        bu = bank()
                        for k in range(KC):
                            g.mm(bg[0], bg[1][:, 0:n], tg[:, k, jj * 128:(jj + 1) * 128], hT[:, k, c0:c0 + n],
                                 [kg, ("hT", k)], start=(k == 0), stop=(k == KC - 1))
                        for k in range(KC):
                            if CFG.get("skip_up", False) and k < KC - 1:
                                continue
                            g.mm(bu[0], bu[1][:, 0:n], tu[:, k, jj * 128:(jj + 1) * 128], hT[:, k, c0:c0 + n],
                                 [ku, ("hT", k)], start=(k == 0 or CFG.get("skip_up", False)), stop=(k == KC - 1))
                        sg_i[0] += 1
                        sgi = sg_i[0] % 2
                        g.act(sgb[sgi][:, 0:n], bg[1][:, 0:n], AF.Silu, [bg[0]], [("sg", sgi)])
                        g.tt(acts[:, j, c0:c0 + n], sgb[sgi][:, 0:n], bu[1][:, 0:n], ALU.mult,
                             [("sg", sgi), bu[0]], [("acts", j, si)])
            g.act(smallc[:, 200:201], one_t, AF.Exp, ["smallc"], ["tblwarm2"])
            for m in range(KC):
                if wB_pref[which]:
                    kd, td = wB_pref[which].pop(0)
                else:
                    kd, td = wloadB(wd_v[which][:, :, m * 128:(m + 1) * 128], NJ)
                for si, (c0, n) in enumerate(segs):
                    bo = bank()
                    for j in range(NJ):
                        g.mm(bo[0], bo[1][:, 0:n], td[:, j, :], acts[:, j, c0:c0 + n],
                             [kd, ("acts", j, si)], start=(j == 0), stop=(j == NJ - 1))
                    g.stt(xT[:, m, c0:c0 + n], bo[1][:, 0:n], 0.5, xT[:, m, c0:c0 + n], ALU.mult, ALU.add,
                          [bo[0], ("xT", m)], [("xT", m)])
            if (which == 0 and t == NTILE - 1) or CFG.get("all_barriers", False):
                g.barrier()

        def load_tile(t):
            g.embed = CFG.get("embed_io", True)
            ar.reset(R2_OFF)
            xrow = [ar.alloc([128, D], F32) for _ in range(2)]
            for b in range(NBLK):
                r0 = t * TT + b * 128
                xr = xrow[b % 2]
                spdma(f"xrow{b % 2}", xr[:, :], xp[r0:r0 + 128, :], (), [("io", b % 2)])
                for hf in range(0 if (CFG.get("load_dma_only", False) and t == 3) else 2):
                    bk = bank()
                    for q in range(4):
                        k = hf * 4 + q
                        g.tr(bk[0], bk[1][:, q * 128:(q + 1) * 128], xr[:, k * 128:(k + 1) * 128], ident_f,
                             [("io", b % 2), "cmat"], last=(q == 3))
                    g.cp("act", xT[:, hf * 4:(hf + 1) * 4, b * 128:(b + 1) * 128],
                         bk[1][:, :].rearrange("p (q c) -> p q c", q=4), [bk[0]],
                         [("xT", hf * 4 + q) for q in range(4)])
            if t == NTILE - 1 and not CFG.get("skip_xs", False):
                xsr = ar.alloc([NS, D], F32)
                spdma("xsr", xsr[:, :], xsm[:, :], (), [("io", 2, 0), ("io", 2, 1)])
                for hf in range(2):
                    bk = bank()
                    for q in range(4):
                        k = hf * 4 + q
                        g.tr(bk[0], bk[1][:, q * NS:(q + 1) * NS], xsr[:, k * 128:(k + 1) * 128], ident_f[0:NS, 0:NS],
                             [("io", 2, 0), ("io", 2, 1), "cmat"], last=(q == 3))
                    g.cp("act", xT[:, hf * 4:(hf + 1) * 4, TT:TT + NS],
                         bk[1][:, 0:4 * NS].rearrange("p (q c) -> p q c", q=4), [bk[0]],
                         [("xT", hf * 4 + q) for q in range(4)])
            if CFG.get("all_barriers", False):
                g.barrier()

        def store_tile(t):
            g.embed = CFG.get("embed_io", True)
            segs = segs_of(t)
            ar.reset(R2_OFF)
            yTb = [ar.alloc([128, KC, 128], F32) for _ in range(2)]
            yrow = [ar.alloc([128, D], F32) for _ in range(2)]
            if CFG.get("st_norm", True):
                norm_rstd(segs, V_NWF)
            rk = [("rstd", si) for si in range(len(segs))]
            nblk = NBLK + (1 if t == NTILE - 1 else 0)
            for b in range(nblk):
                c0 = b * 128
                n = 128 if b < NBLK else NS
                yt = yTb[b % 2]
                yr = yrow[b % 2]
                for k in range(KC):
                    g.stt(yt[:, k, 0:n], xT[:, k, c0:c0 + n], vecs[:, V_NWF + k:V_NWF + k + 1], rstd[:, c0:c0 + n],
                          ALU.mult, ALU.mult, [("xT", k), "vecs"] + rk, [("io", b % 2)])
                for hf in range(2 if CFG.get("st_tr", True) else 0):
                    bk = bank()
                    for q in range(4):
                        k = hf * 4 + q
                        g.tr(bk[0], bk[1][0:n, q * 128:(q + 1) * 128], yt[:, k, 0:n], ident_f,
                             [("io", b % 2), "cmat"], last=(q == 3))
                    g.cp("act", yr[0:n, hf * 512:(hf + 1) * 512], bk[1][0:n, :], [bk[0]], [("io", 2 + b % 2, hf)])
                if not CFG.get("st_dma", True):
                    continue
                if b < NBLK:
                    r0 = t * TT + b * 128
                    spdma(f"yst{b % 2}", yp[r0:r0 + 128, :], yr[:, :], [("io", 2 + b % 2, 0), ("io", 2 + b % 2, 1)],
                          [("ypd", t, b)])
                else:
                    spdma(f"yst{b % 2}", ysm[:, :], yr[0:NS, :], [("io", 2 + b % 2, 0), ("io", 2 + b % 2, 1)],
                          [("ysd",)])
            if CFG.get("all_barriers", False):
                g.barrier()

        def mixer(t):
            g.embed = CFG.get("embed_mix", False)
            segs = segs_of(t)
            last = (t == NTILE - 1)
            ncol = segs[-1][0] + segs[-1][1]
            norm_to_hT(segs, V_NWM)
            hk = [("hT", k) for k in range(KC)]
            ar.reset(0)
            Wz = Wz_t[:, :, :]
            Wdt = Wdt_t[:, :, :]
            ymixT = ar.alloc([128, 16, NCOL], BF16)
            xsT = ar.alloc([128, KC, NCOL], F32)
            BCT = ar.alloc([128, 4, NCOL], BF16)
            BCs_f = ar.alloc([128, 4, NS], F32)
            rawb = [ar.alloc([128, 3 + TT], F32) for _ in range(2)]
            ubuf = [ar.alloc([128, 2 + TT], F32) for _ in range(2)]
            accb = [ar.alloc([128, TT], F32) for _ in range(2)]
            R2 = ar.off
            assert R2 == R2_OFF, (R2, R2_OFF)

            g.dma("pool", lambda e: e.dma_start(out=Wz, in_=win_v[:, :, O_Z:O_Z + 1024]), (), ["Wz"], g.stream("Wz"))
            g.dma("pool", lambda e: e.dma_start(out=Wdt, in_=win_v[:, :, O_DT:O_DT + 16]), (), ["Wdt"], g.stream("Wdt"))

            if last:
                strow = ar.alloc([48, 1536], F32)
                scrow = ar.alloc([32, 1024], F32)
                hs = ar.alloc([128, 12, 48], F32)
                us = ar.alloc([128, 8, 32], F32)
                smp = ar.alloc([128, 8, NS], F32)
                smp2 = ar.alloc([128, 8, NS], F32)
                cvrow = ar.alloc([NS, 1536], F32)
                urow = ar.alloc([NS, 1024], F32)
                spdma("strow", strow[:, :], cst_in.rearrange("b k c -> (b k) c"), (), ["strow"])
                spdma("scrow", scrow[:, :], sst_in.rearrange("b k c -> (b k) c"), (), ["scrow"])
                for c in range(12):
                    bk = bank()
                    g.tr(bk[0], bk[1][:, 0:48], strow[:, c * 128:(c + 1) * 128], ident_f[0:48, 0:48], ["strow", "cmat"])
                    g.cp("act", hs[:, c, :], bk[1][:, 0:48], [bk[0]], [("hs", c)])
                for c in range(8):
                    bk = bank()
                    g.tr(bk[0], bk[1][:, 0:32], scrow[:, c * 128:(c + 1) * 128], ident_f[0:32, 0:32], ["scrow", "cmat"])
                    g.cp("act", us[:, c, :], bk[1][:, 0:32], [bk[0]], [("us", c)])
                spdma("cvpass", o_cv_s[:, 0:2, :], cst_in[:, 1:3, :], (), ["o_cv_s01"])
                spdma("scpass", o_sc_s[:, 0:1, :], sst_in[:, 1:2, :], (), ["o_sc_s0"])

            for qd in range(4):
                kb, tb = wloadA(win_v[:, :, O_SCB + qd * 256:O_SCB + (qd + 1) * 256], 256)
                kc_, tc_ = wloadA(win_v[:, :, O_SCC + qd * 256:O_SCC + (qd + 1) * 256], 256)
                kh, th = wloadA(win_v[:, :, O_SCH + qd * 256:O_SCH + (qd + 1) * 256], 256)
                for jj in range(2):
                    c = qd * 2 + jj
                    for si, (c0, n) in enumerate(segs):
                        pb_, pc_, ph_ = bank(), bank(), bank()
                        for (pk, tw, kw) in ((pb_, tb, kb), (pc_, tc_, kc_), (ph_, th, kh)):
                            for k in range(KC):
                                g.mm(pk[0], pk[1][:, 0:n], tw[:, k, jj * 128:(jj + 1) * 128], hT[:, k, c0:c0 + n],
                                     [kw, ("hT", k)], start=(k == 0), stop=(k == KC - 1))
                        sg_i[0] += 1
                        sgi = sg_i[0] % 2
                        ct = sgb[sgi]
                        g.cp("act", ct[:, 0:n], pc_[1][:, 0:n], [pc_[0]], [("sg", sgi)])
                        w0 = vecs[:, V_SW + c * 3 + 0:V_SW + c * 3 + 1]
                        w1 = vecs[:, V_SW + c * 3 + 1:V_SW + c * 3 + 2]
                        w2 = vecs[:, V_SW + c * 3 + 2:V_SW + c * 3 + 3]
                        if si == 0:
                            ub = ubuf[c % 2]
                            uk = ("ubuf", c % 2)
                            g.cp("dve", ub[:, 0:2], uhist[:, c, :], [("uhist", c)], [uk])
                            g.tt(ub[:, 2:2 + TT], ct[:, 0:TT], ph_[1][:, 0:TT], ALU.mult, [("sg", sgi), ph_[0]], [uk])
                            ac = accb[c % 2]
                            ak = ("accb", c % 2)
                            g.ts(ac[:, :], ub[:, 0:TT], w0, ALU.mult, [uk, "vecs"], [ak])
                            g.stt(ac[:, :], ub[:, 1:1 + TT], w1, ac[:, :], ALU.mult, ALU.add, [uk, ak], [ak])
                            g.stt(ac[:, :], ub[:, 2:2 + TT], w2, ac[:, :], ALU.mult, ALU.add, [uk, ak], [ak])
                            g.tt(ymixT[:, 8 + c, 0:TT], ac[:, :], pb_[1][:, 0:TT], ALU.mult, [ak, pb_[0]],
                                 [("ymix", 8 + c, "p")])
                            g.cp("dve", uhist[:, c, :], ub[:, TT:TT + 2], [uk], [("uhist", c)])
                        else:
                            u_s = smp[:, c, :]
                            v_s = smp2[:, c, :]
                            g.tt(u_s, ct[:, 0:NS], ph_[1][:, 0:NS], ALU.mult, [("sg", sgi), ph_[0]], [("smp", c)])
                            usv = us[:, c, :].rearrange("p (b k) -> p b k", k=2)
                            g.ts(v_s, usv[:, :, 0], w0, ALU.mult, [("us", c), "vecs"], [("smp2", c)])
                            g.stt(v_s, usv[:, :, 1], w1, v_s, ALU.mult, ALU.add, [("us", c), ("smp2", c)], [("smp2", c)])
                            g.stt(v_s, u_s, w2, v_s, ALU.mult, ALU.add, [("smp", c), ("smp2", c)], [("smp2", c)])
                            g.tt(ymixT[:, 8 + c, TT:TT + NS], v_s, pb_[1][:, 0:NS], ALU.mult, [("smp2", c), pb_[0]],
                                 [("ymix", 8 + c, "s")])
                            bk = bank()
                            g.tr(bk[0], bk[1][0:NS, 0:128], u_s, ident_f, [("smp", c), "cmat"])
                            g.cp("act", urow[:, c * 128:(c + 1) * 128], bk[1][0:NS, 0:128], [bk[0]], [("urow", c)])
            if last:
                spdma("urow", o_sc_s[:, 1:2, :], urow[:, :].rearrange("b (o c) -> b o c", o=1),
                      [("urow", c) for c in range(8)], ["o_sc_s1"])

            pend_b = [None]
            for s in range(6):
                kx, tx = wloadA(win_v[:, :, O_XBC + s * 256:O_XBC + (s + 1) * 256], 256)
                for jj in range(2):
                    c = s * 2 + jj
                    cw = [vecs[:, V_CW + c * 4 + i:V_CW + c * 4 + i + 1] for i in range(4)]
                    cb = vecs[:, V_CB + c:V_CB + c + 1]
                    for si, (c0, n) in enumerate(segs):
                        px = bank()
                        for k in range(KC):
                            g.mm(px[0], px[1][:, 0:n], tx[:, k, jj * 128:(jj + 1) * 128], hT[:, k, c0:c0 + n],
                                 [kx, ("hT", k)], start=(k == 0), stop=(k == KC - 1))
                        if si == 0:
                            rb = rawb[c % 2]
                            rk_ = ("rawb", c % 2)
                            g.cp("act", rb[:, 0:3], hist[:, c, :], [("hist", c)], [rk_])
                            g.cp("act", rb[:, 3:3 + TT], px[1][:, 0:TT], [px[0]], [rk_])
                            ac = accb[c % 2]
                            ak = ("accb", c % 2)
                            g.act(ac[:, :], rb[:, 0:TT], AF.Identity, [rk_, "vecs"], [ak], scale=cw[0])
                            for i in range(1, 4):
                                g.stt(ac[:, :], rb[:, i:i + TT], cw[i], ac[:, :], ALU.mult, ALU.add, [rk_, ak], [ak])
                            g.cp("dve", hist[:, c, :], rb[:, TT:TT + 3], [rk_], [("hist", c)])

                            def stage_b(c=c, ac=ac, ak=ak, cb=cb):
                                if c < 8:
                                    g.act(xsT[:, c, 0:TT], ac[:, :], AF.Silu, [ak, "vecs"], [("xsT", c, "p")], bias=cb)
                                else:
                                    g.act(BCT[:, c - 8, 0:TT], ac[:, :], AF.Silu, [ak, "vecs"], [("BCT", c - 8)], bias=cb)
                            if pend_b[0] is not None:
                                pend_b[0]()
                            pend_b[0] = stage_b
                        else:
                            rs = ar_small_raw[c]
                            g.cp("act", rs, px[1][:, 0:NS], [px[0]], [("rs", c)])
                            hv = hs[:, c, :].rearrange("p (b k) -> p b k", k=3)
                            a_s = ar_small_acc[c]
                            g.ts(a_s, hv[:, :, 0], cw[0], ALU.mult, [("hs", c), "vecs"], [("as", c)])
                            g.stt(a_s, hv[:, :, 1], cw[1], a_s, ALU.mult, ALU.add, [("hs", c), ("as", c)], [("as", c)])
                            g.stt(a_s, hv[:, :, 2], cw[2], a_s, ALU.mult, ALU.add, [("hs", c), ("as", c)], [("as", c)])
                            g.stt(a_s, rs, cw[3], a_s, ALU.mult, ALU.add, [("rs", c), ("as", c)], [("as", c)])
                            if c < 8:
                                g.act(xsT[:, c, TT:TT + NS], a_s, AF.Silu, [("as", c), "vecs"], [("xsT", c, "s")], bias=cb)
                            else:
                                g.act(BCs_f[:, c - 8, :], a_s, AF.Silu, [("as", c), "vecs"], [("BCs", c - 8)], bias=cb)
                            bk = bank()
                            g.tr(bk[0], bk[1][0:NS, 0:128], rs, ident_f, [("rs", c), "cmat"])
                            g.cp("act", cvrow[:, c * 128:(c + 1) * 128], bk[1][0:NS, 0:128], [bk[0]], [("cvrow", c)])
            if pend_b[0] is not None:
                pend_b[0]()
            if CFG.get("wb_prefetch", True) and CFG["ffn2"]:
                prefetch_wB(1)
            if last:
                spdma("cvrow", o_cv_s[:, 2:3, :], cvrow[:, :].rearrange("b (o c) -> b o c", o=1),
                      [("cvrow", c) for c in range(12)], ["o_cv_s2"])
            if last or CFG.get("all_barriers", False):
                g.barrier()
            return dict(Wz=Wz, Wdt=Wdt, ymixT=ymixT, xsT=xsT, BCT=BCT, BCs_f=BCs_f, R2=R2)

        ar_small = sb("ar_small", [128, 24, NS], F32)
        ar_small_raw = [ar_small[:, c, :] for c in range(12)]
        ar_small_acc = [ar_small[:, 12 + c, :] for c in range(12)]

        def ssd_blocks(t, mx):
            g.embed = CFG.get("embed_mix", False)
            Wz, Wdt, ymixT, xsT, BCT = mx["Wz"], mx["Wdt"], mx["ymixT"], mx["xsT"], mx["BCT"]
            ar.reset(mx["R2"])
            zs = ar.alloc([128, 1024], F32)
            sm = ar.alloc([128, 16, 64], F32)
            xs_tm = ar.alloc([128, 1024], F32)
            xdt = ar.alloc([128, 1024], BF16)
            xdtd = ar.alloc([128, 1024], BF16)
            B_tm = ar.alloc([128, 2, 128], BF16)
            CBm = ar.alloc([128, 2, 128], F32)
            rseg = ar.alloc([128, 8, 128], F32)
            Lm = ar.alloc([128, 8, 128], F32)
            G = ar.alloc([128, 16, 128], BF16)
            y1 = ar.alloc([128, 1024], F32)
            yg = ar.alloc([128, 1024], F32)
            yn = ar.alloc([128, 1024], BF16)
            junk2 = ar.alloc([128, 2, 512], F32)
            hk = [("hT", k) for k in range(KC)]
            dtr, dtt, dta, dte, dtl, dtv, av, acum, totb, eacum, dend, etot, dtde, ss2, ln2, rs2 = [sm[:, i, 0:16] for i in range(16)]
            rsegq = [rseg[:, 0:4, :], rseg[:, 4:8, :]]
            Lmq = [Lm[:, 0:4, :], Lm[:, 4:8, :]]

            def z_mm(b):
                cb0_ = b * 128
                bzs = []
                for hf in range(2):
                    bz = bank()
                    for k in range(KC):
                        g.mm(bz[0], bz[1][:, :], hT[:, k, cb0_:cb0_ + 128], Wz[:, k, hf * 512:(hf + 1) * 512],
                             [("hT", k), "Wz"], start=(k == 0), stop=(k == KC - 1))
                    bzs.append(bz)
                return bzs

            def z_act(bzs):
                for hf in range(2):
                    g.act(zs[:, hf * 512:(hf + 1) * 512], bzs[hf][1][:, :], AF.Silu, [bzs[hf][0]], [("zs", hf)])
                g.act(sm[:, 12, 32:33], one_t, AF.Exp, ["smallc"], ["tblwarm"])

            z_act(z_mm(0))
            for b in range(NBLK):
                cb0 = b * 128
                bd = bank()
                for k in range(KC):
                    g.mm(bd[0], bd[1][:, 0:16], hT[:, k, cb0:cb0 + 128], Wdt[:, k, :], [("hT", k), "Wdt"],
                         start=(k == 0), stop=(k == KC - 1))
                g.tt(dtt, bd[1][:, 0:16], rowc[:, R_DTB:R_DTB + 16], ALU.add, [bd[0], "rowc"], ["dtt"])
                g.act(dta, dtt, AF.Abs, ["dtt"], ["dta"])
                g.act(dte, dta, AF.Exp, ["dta"], ["dte"], scale=-1.0)
                g.act(dtl, dte, AF.Ln, ["dte", "smallc"], ["dtl"], bias=one_t)
                for hf in range(2):
                    bx = bank()
                    for q in range(4):
                        c = hf * 4 + q
                        g.tr(bx[0], bx[1][:, q * 128:(q + 1) * 128], xsT[:, c, cb0:cb0 + 128], ident_f,
                             [("xsT", c, "p"), "cmat"], last=(q == 3))
                    sl = slice(hf * 512, (hf + 1) * 512)
                    g.cp("act", xs_tm[:, sl], bx[1][:, :], [bx[0]], [("xs_tm", hf)])
                for gi in range(2):
                    g.tr("psbf", psbf[:, gi * 128:(gi + 1) * 128], BCT[:, gi, cb0:cb0 + 128], ident_b,
                         [("BCT", gi), "cbf"], last=(gi == 1))
                g.cp("act", B_tm[:, :, :], psbf[:, 0:256].rearrange("p (a b) -> p a b", a=2), ["psbf"], ["B_tm"])
                g.stt(dtv, dtt, 0.0, dtl, ALU.max, ALU.add, ["dtt", "dtl"], ["dtv"])
                g.tt(av, dtv, A_bc, ALU.mult, ["dtv", "smallc"], ["av"])
                ba = bank()
                g.mm(ba[0], ba[1][:, 0:16], U_f, av, ["cmat", "av"])
                g.mm(ba[0], ba[1][:, 16:32], ones_f, av, ["cmat", "av"])
                g.cp("dve", acum, ba[1][:, 0:16], [ba[0]], ["acum"])
                g.act(eacum, ba[1][:, 0:16], AF.Exp, [ba[0]], ["eacum"])
                g.act(etot, ba[1][:, 16:32], AF.Exp, [ba[0]], ["etot"])
                g.tt(dend, ba[1][:, 16:32], acum, ALU.subtract, [ba[0], "acum"], ["dend"])
                g.act(dend, dend, AF.Exp, ["dend"], ["dend"])
                g.tt(dtde, dtv, dend, ALU.mult, ["dtv", "dend"], ["dtde"])
                bc = bank()
                for gi in range(2):
                    g.mm(bc[0], bc[1][:, gi * 128:(gi + 1) * 128], BCT[:, gi, cb0:cb0 + 128], BCT[:, 2 + gi, cb0:cb0 + 128],
                         [("BCT", gi), ("BCT", 2 + gi)])
                for gi in range(2):
                    g.tt(CBm[:, gi, :], bc[1][:, gi * 128:(gi + 1) * 128], U_f, ALU.mult, [bc[0], "cmat"], [("CBm", gi)])
                def seg_a(qq):
                    rq, lq = rsegq[qq % 2], Lmq[qq % 2]
                    g.tt(rq, U_f.unsqueeze(1).to_broadcast([128, 4, 128]),
                         av[:, qq * 4:(qq + 1) * 4].unsqueeze(2).to_broadcast([128, 4, 128]), ALU.mult,
                         ["cmat", "av"], [("rseg", qq % 2)])
                    bs = bank()
                    g.mm(bs[0], bs[1][:, :], Ms_f, rq.rearrange("p a b -> p (a b)"), ["cmat", ("rseg", qq % 2)])
                    g.act(lq.rearrange("p a b -> p (a b)"), bs[1][:, :], AF.Exp, [bs[0]], [("Lm", qq % 2)])

                def seg_b(qq):
                    lq = Lmq[qq % 2]
                    g.tt(G[:, qq * 4:(qq + 1) * 4, :], lq, CBm[:, qq // 2, :].unsqueeze(1).to_broadcast([128, 4, 128]), ALU.mult,
                         [("Lm", qq % 2), ("CBm", qq // 2)], [("G", qq)])

                def xmul(hf):
                    sl = slice(hf * 512, (hf + 1) * 512)
                    g.tt(xdt[:, sl].rearrange("p (h d) -> p h d", h=8), xs_tm[:, sl].rearrange("p (h d) -> p h d", h=8),
                         dtv[:, hf * 8:(hf + 1) * 8].unsqueeze(2).to_broadcast([128, 8, 64]), ALU.mult,
                         [("xs_tm", hf), "dtv"], [("xdt", hf)])
                    g.tt(xdtd[:, sl].rearrange("p (h d) -> p h d", h=8), xs_tm[:, sl].rearrange("p (h d) -> p h d", h=8),
                         dtde[:, hf * 8:(hf + 1) * 8].unsqueeze(2).to_broadcast([128, 8, 64]), ALU.mult,
                         [("xs_tm", hf), "dtde"], [("xdtd", hf)])
                    g.tt(yg[:, sl], xs_tm[:, sl], rowc[:, R_D + hf * 512:R_D + (hf + 1) * 512], ALU.mult,
                         [("xs_tm", hf), "rowc"], [("yg", hf)])

                seg_a(0)
                seg_a(1)
                xmul(0)
                seg_b(0)
                seg_a(2)
                xmul(1)
                seg_b(1)
                seg_a(3)
                seg_b(2)
                seg_b(3)
                by = [bank(), bank()]
                for h in range(16):
                    g.mm(by[h // 8][0], by[h // 8][1][:, (h % 8) * 64:(h % 8 + 1) * 64], G[:, h, :],
                         xdt[:, h * 64:(h + 1) * 64], [("G", h // 4), ("xdt", h // 8)])
                g.cp("act", S_bf[:, :], S_f[:, :], ["S_f"], ["S_bf"])
                bo = [bank(), bank()]
                for gi in range(2):
                    g.mm(bo[gi][0], bo[gi][1][:, :], BCT[:, 2 + gi, cb0:cb0 + 128], S_bf[:, gi * 512:(gi + 1) * 512],
                         [("BCT", 2 + gi), "S_bf"])
                for gi in range(2):
                    sl = slice(gi * 512, (gi + 1) * 512)
                    g.cp("act", y1[:, sl], bo[gi][1][:, :], [bo[gi][0]], [("y1", gi)])
                    g.tt(y1[:, sl].rearrange("p (h d) -> p h d", h=8), y1[:, sl].rearrange("p (h d) -> p h d", h=8),
                         eacum[:, gi * 8:(gi + 1) * 8].unsqueeze(2).to_broadcast([128, 8, 64]), ALU.mult,
                         [("y1", gi), "eacum"], [("y1", gi)])
                    g.tt(y1[:, sl], y1[:, sl], by[gi][1][:, :], ALU.add, [("y1", gi), by[gi][0]], [("y1", gi)])
                    g.tt(yg[:, sl], yg[:, sl], y1[:, sl], ALU.add, [("yg", gi), ("y1", gi)], [("yg", gi)])
                    g.tt(yg[:, sl], yg[:, sl], zs[:, sl], ALU.mult, [("yg", gi), ("zs", gi)], [("yg", gi)])
                bzn = z_mm(b + 1) if b + 1 < NBLK else None
                for gi in range(2):
                    sl = slice(gi * 512, (gi + 1) * 512)
                    g.act(junk2[:, gi, :], yg[:, sl], AF.Square, [("yg", gi)], [("junk", gi)])
                    g.op("dve", lambda e, gi=gi, ss2=ss2: e.tensor_reduce(out=ss2[:, gi:gi + 1], in_=junk2[:, gi, :], axis=AX.X, op=ALU.add),
                         reads=[("junk", gi)], writes=[("ss2", gi)])
                g.act(ln2[:, 0:2], ss2[:, 0:2], AF.Ln, [("ss2", 0), ("ss2", 1), "smallc"], ["ln2"], bias=eps_t, scale=1.0 / 512)
                g.act(rs2[:, 0:2], ln2[:, 0:2], AF.Exp, ["ln2"], ["rs2"], scale=-0.5)
                for gi in range(2):
                    sl = slice(gi * 512, (gi + 1) * 512)
                    g.stt(yn[:, sl], yg[:, sl], rs2[:, gi:gi + 1], rowc[:, R_SNW + gi * 512:R_SNW + (gi + 1) * 512],
                          ALU.mult, ALU.mult, [("yg", gi), "rs2", "rowc"], [("yn", gi)])
                if bzn is not None:
                    z_act(bzn)
                for c in range(8):
                    g.tr("psbf", psbf[:, c * 128:(c + 1) * 128], yn[:, c * 128:(c + 1) * 128], ident_b,
                         [("yn", c // 4), "cbf"], last=(c == 7))
                g.cp("act", ymixT[:, 0:8, cb0:cb0 + 128], psbf[:, :].rearrange("p (a b) -> p a b", a=8), ["psbf"],
                     [("ymix", c, "p", b) for c in range(8)])
                bn = [bank(), bank()]
                for gi in range(2):
                    g.mm(bn[gi][0], bn[gi][1][:, :], B_tm[:, gi, :], xdtd[:, gi * 512:(gi + 1) * 512],
                         ["B_tm", ("xdtd", gi)])
                for gi in range(2):
                    sl = slice(gi * 512, (gi + 1) * 512)
                    g.tt(S_f[:, sl].rearrange("p (h d) -> p h d", h=8), S_f[:, sl].rearrange("p (h d) -> p h d", h=8),
                         etot[:, gi * 8:(gi + 1) * 8].unsqueeze(2).to_broadcast([128, 8, 64]), ALU.mult,
                         ["S_f", "etot"], ["S_f"])
                    g.tt(S_f[:, sl], S_f[:, sl], bn[gi][1][:, :], ALU.add, ["S_f", bn[gi][0]], ["S_f"])
            g.barrier()

        def ssd_sample(mx):
            g.embed = CFG.get("embed_mix", False)
            Wz, Wdt, ymixT, xsT, BCs_f = mx["Wz"], mx["Wdt"], mx["ymixT"], mx["xsT"], mx["BCs_f"]
            ar.reset(mx["R2"])
            Sb = [ar.alloc([128, 8, 128], F32) for _ in range(3)]
            tmpb = ar.alloc([128, 8, 128], F32)
            tmp2 = ar.alloc([128, 8, 128], F32)
            BCbb = [ar.alloc([128, 512], F32) for _ in range(2)]
            dtE = ar.alloc([NS, 2, 1024], F32)
            dtF = ar.alloc([128, 2, 8, NS], F32)
            dtxF = ar.alloc([128, 8, NS], F32)
            ysT = ar.alloc([128, 8, NS], F32)
            zT = ar.alloc([128, 8, NS], F32)
            ygT = ar.alloc([128, 8, NS], F32)
            sqT = ar.alloc([128, 8, NS], F32)
            rsb = ar.alloc([128, 2, NS], F32)
            BC_tm = ar.alloc([NS, 4, 128], F32)
            sel = [ar.alloc([NS, 128], F32) for _ in range(2)]
            smt = ar.alloc([NS, 8, 16], F32)
            dtt, dta, dte, dtl, dtv, av, dA = [smt[:, i, :] for i in range(7)]
            sc0, sc1 = TT, TT + NS
            bd = bank()
            for k in range(KC):
                g.mm(bd[0], bd[1][0:NS, 0:16], hT[:, k, sc0:sc1], Wdt[:, k, :], [("hT", k), "Wdt"],
                     start=(k == 0), stop=(k == KC - 1))
            g.tt(dtt, bd[1][0:NS, 0:16], rowc[0:NS, R_DTB:R_DTB + 16], ALU.add, [bd[0], "rowc"], ["s_dtt"])
            g.act(dta, dtt, AF.Abs, ["s_dtt"], ["s_dta"])
            g.act(dte, dta, AF.Exp, ["s_dta"], ["s_dte"], scale=-1.0)
            g.act(dtl, dte, AF.Ln, ["s_dte", "smallc"], ["s_dtl"], bias=one_t[0:NS, :])
            g.stt(dtv, dtt, 0.0, dtl, ALU.max, ALU.add, ["s_dtt", "s_dtl"], ["s_dtv"])
            g.tt(av, dtv, A_bc[0:NS, :], ALU.mult, ["s_dtv", "smallc"], ["s_av"])
            g.act(dA, av, AF.Exp, ["s_av"], ["s_dA"])
            g.cp("dve", dtE[:, 0, :].rearrange("p (h d) -> p h d", h=16), dtv.unsqueeze(2).to_broadcast([NS, 16, 64]),
                 ["s_dtv"], [("dtE", 0)])
            g.cp("dve", dtE[:, 1, :].rearrange("p (h d) -> p h d", h=16), dA.unsqueeze(2).to_broadcast([NS, 16, 64]),
                 ["s_dA"], [("dtE", 1)])
            for w_ in range(2):
                for hf in range(2):
                    bk = bank()
                    for q in range(4):
                        c = hf * 4 + q
                        g.tr(bk[0], bk[1][:, q * NS:(q + 1) * NS], dtE[:, w_, c * 128:(c + 1) * 128], ident_f[0:NS, 0:NS],
                             [("dtE", w_), "cmat"], last=(q == 3))
                    g.cp("act", dtF[:, w_, hf * 4:(hf + 1) * 4, :], bk[1][:, 0:4 * NS].rearrange("p (q c) -> p q c", q=4),
                         [bk[0]], [("dtF", w_, hf)])
            dk = [("dtF", w_, hf) for w_ in range(2) for hf in range(2)]
            xk = [("xsT", c, "s") for c in range(8)]
            g.tt(dtxF[:, :, :], xsT[:, :, sc0:sc1], dtF[:, 0, :, :], ALU.mult, xk + dk, ["dtxF"])
            bk = bank()
            for i in range(4):
                g.tr(bk[0], bk[1][0:NS, i * 128:(i + 1) * 128], BCs_f[:, i, :], ident_f, [("BCs", i), "cmat"], last=(i == 3))
            g.cp("act", BC_tm[:, :, :], bk[1][0:NS, :].rearrange("p (a b) -> p a b", a=4), [bk[0]], ["BC_tm"])
            NSL = 3

            def s_load(b):
                spdma(f"ssmin{b % NSL}", Sb[b % NSL][:, :, :], ssm_in[b].rearrange("c q n -> q c n"), (), [("Sb", b % NSL)])

            def s_stage_a(b):
                sl_ = sel[b % 2]
                g.cp("dve", sl_[:, :], ident_f[0:NS, b:b + 1].to_broadcast([NS, 128]), ["cmat"], [("sel", b % 2)])
                bb = bank()
                g.mm(bb[0], bb[1][:, :], sl_[:, :], BC_tm[:, :, :].rearrange("p a b -> p (a b)"), [("sel", b % 2), "BC_tm"])
                g.cp("act", BCbb[b % 2][:, :], bb[1][:, :], [bb[0]], [("BCb", b % 2)])

            def s_stage_b(b):
                S = Sb[b % NSL]
                sk = ("Sb", b % NSL)
                BCb = BCbb[b % 2]
                g.tt(S[:, :, :], S[:, :, :], dtF[:, 1, :, b:b + 1].to_broadcast([128, 8, 128]), ALU.mult, [sk] + dk, [sk])
                Bv = BCb[:, 0:256].rearrange("p (g n) -> p g n", g=2).unsqueeze(2).to_broadcast([128, 2, 4, 128])
                Cv = BCb[:, 256:512].rearrange("p (g n) -> p g n", g=2).unsqueeze(2).to_broadcast([128, 2, 4, 128])
                g.tt(tmpb[:, :, :].rearrange("p (g q) n -> p g q n", g=2), Bv,
                     dtxF[:, :, b:b + 1].to_broadcast([128, 8, 128]).rearrange("p (g q) n -> p g q n", g=2), ALU.mult,
                     [("BCb", b % 2), "dtxF"], ["tmpb"])
                g.tt(S[:, :, :], S[:, :, :], tmpb[:, :, :], ALU.add, [sk, "tmpb"], [sk])
                spdma(f"ssmout{b % NSL}", o_ssm_s[b].rearrange("c q n -> q c n"), S[:, :, :], [sk], [("o_ssm_s", b)])
                g.tt(tmp2[:, :, :].rearrange("p (g q) n -> p g q n", g=2), Cv,
                     S[:, :, :].rearrange("p (g q) n -> p g q n", g=2), ALU.mult, [("BCb", b % 2), sk], ["tmp2"])
                g.op("dve", lambda e, b=b: e.tensor_reduce(out=ysT[:, :, b], in_=tmp2[:, :, :], axis=AX.X, op=ALU.add),
                     reads=["tmp2"], writes=[("ysT", b)])
                if b + NSL < NS:
                    s_load(b + NSL)

            for b in range(NSL):
                s_load(b)
            s_stage_a(0)
            for b in range(NS):
                if b + 1 < NS:
                    s_stage_a(b + 1)
                s_stage_b(b)
            yk = [("ysT", b) for b in range(NS)]
            g.tt(tmpb[:, 0, 0:8 * NS].rearrange("p (c b) -> p c b", c=8), xsT[:, :, sc0:sc1],
                 vecs[:, V_DFM:V_DFM + 8].unsqueeze(2).to_broadcast([128, 8, NS]), ALU.mult, xk + ["vecs", "tmpb"], ["tmpb"])
            g.tt(ysT[:, :, :], ysT[:, :, :], tmpb[:, 0, 0:8 * NS].rearrange("p (c b) -> p c b", c=8), ALU.add,
                 yk + ["tmpb"], ["ysTall"])
            for hf in range(2):
                bk = bank()
                for q in range(4):
                    c = hf * 4 + q
                    for k in range(KC):
                        g.mm(bk[0], bk[1][:, q * NS:(q + 1) * NS], Wz[:, k, c * 128:(c + 1) * 128], hT[:, k, sc0:sc1],
                             ["Wz", ("hT", k)], start=(k == 0), stop=(k == KC - 1))
                g.act(zT[:, hf * 4:(hf + 1) * 4, :], bk[1][:, 0:4 * NS].rearrange("p (q c) -> p q c", q=4), AF.Silu,
                      [bk[0]], [("zT", hf)])
            g.tt(ygT[:, :, :], ysT[:, :, :], zT[:, :, :], ALU.mult, ["ysTall", ("zT", 0), ("zT", 1)], ["ygT"])
            g.tt(sqT[:, :, :], ygT[:, :, :], ygT[:, :, :], ALU.mult, ["ygT"], ["sqT"])
            bk = bank()
            for gi in range(2):
                for q in range(4):
                    g.mm(bk[0], bk[1][:, gi * NS:(gi + 1) * NS], ones_f, sqT[:, gi * 4 + q, :], ["cmat", "sqT"],
                         start=(q == 0), stop=(q == 3))
            g.act(rsb[:, :, :], bk[1][:, 0:2 * NS].rearrange("p (g b) -> p g b", g=2), AF.Ln, [bk[0], "smallc"], ["rsb"],
                  bias=eps_t, scale=1.0 / 512)
            g.act(rsb[:, :, :], rsb[:, :, :], AF.Exp, ["rsb"], ["rsb"], scale=-0.5)
            for c in range(8):
                g.stt(ymixT[:, c, sc0:sc1], ygT[:, c, :], vecs[:, V_SNW + c:V_SNW + c + 1], rsb[:, c // 4, :],
                      ALU.mult, ALU.mult, ["ygT", "vecs", "rsb"], [("ymix", c, "s")])
            g.barrier()

        def out_proj(t, mx):
            g.embed = CFG.get("embed_ffn", True)
            ymixT = mx["ymixT"]
            segs = segs_of(t)
            for m in range(KC):
                i = wA_i[0] % 5
                wA_i[0] += 1
                key = ("wA", i)
                dst = wA[i][:, :, :].rearrange("p k c -> p (k c)")[:, 0:2048].rearrange("p (c m) -> p c m", c=16)
                src = wout_v[:, :, m * 128:(m + 1) * 128]
                g.dma("pool", lambda e, dst=dst, src=src: e.dma_start(out=dst, in_=src), (), [key], wA_s[i])
                for si, (c0, n) in enumerate(segs):
                    bo = bank()
                    for c in range(16):
                        if si == 1:
                            yk_ = [("ymix", c, "s")]
                        elif c >= 8:
                            yk_ = [("ymix", c, "p")]
                        else:
                            yk_ = [("ymix", c, "p", b_) for b_ in range(NBLK)]
                        g.mm(bo[0], bo[1][:, 0:n], dst[:, c, :], ymixT[:, c, c0:c0 + n], [key] + yk_, start=(c == 0), stop=(c == 15))
                    g.tt(xT[:, m, c0:c0 + n], bo[1][:, 0:n], xT[:, m, c0:c0 + n], ALU.add, [bo[0], ("xT", m)], [("xT", m)])
            if CFG.get("all_barriers", False):
                g.barrier()

        def state_outputs():
            g.embed = False
            ar.reset(0)
            Sout = ar.alloc([128, 8, 128], F32)
            crow = ar.alloc([3, 1536], F32)
            srow = ar.alloc([2, 1024], F32)
            for hf in range(2):
                bk = bank()
                for q in range(4):
                    c = hf * 4 + q
                    g.tr(bk[0], bk[1][:, q * 128:(q + 1) * 128], S_f[:, c * 128:(c + 1) * 128], ident_f, ["S_f", "cmat"],
                         last=(q == 3))
                g.cp("act", Sout[:, hf * 4:(hf + 1) * 4, :], bk[1][:, :].rearrange("p (q c) -> p q c", q=4), [bk[0]],
                     [("Sout", hf)])
            spdma("sout", o_ssm_p.rearrange("c q n -> q c n"), Sout[:, :, :], [("Sout", 0), ("Sout", 1)], ["o_ssm_p"])
            for c in range(12):
                bk = bank()
                g.tr(bk[0], bk[1][0:3, 0:128], hist[:, c, :], ident_f, [("hist", c), "cmat"])
                g.cp("act", crow[:, c * 128:(c + 1) * 128], bk[1][0:3, 0:128], [bk[0]], [("crow", c)])
            spdma("crow", o_cv_p[:, :], crow[:, :], [("crow", c) for c in range(12)], ["o_cv_p"])
            for c in range(8):
                bk = bank()
                g.tr(bk[0], bk[1][0:2, 0:128], uhist[:, c, :], ident_f, [("uhist", c), "cmat"])
                g.cp("act", srow[:, c * 128:(c + 1) * 128], bk[1][0:2, 0:128], [bk[0]], [("srow", c)])
            spdma("srow", o_sc_p[:, :], srow[:, :], [("srow", c) for c in range(8)], ["o_sc_p"])

        ph = [0]
        if CFG.get("wb_prefetch", True):
            prefetch_wB(0)

        def ok():
            ph[0] += 1
            return ph[0] <= CFG.get("max_phase", 10 ** 9)

        for t in CFG["tiles"]:
            if ok():
                load_tile(t)
            if CFG["ffn1"]:
                for _ in range(CFG.get("ffn_rep", 1)):
                    if ok():
                        ffn(t, 0, V_NW1)
            if CFG["mixer"]:
                if ok():
                    mx = mixer(t)
                if CFG["ssd"]:
                    if ok():
                        ssd_blocks(t, mx)
                    if t == NTILE - 1 and CFG["sample"]:
                        if ok():
                            ssd_sample(mx)
                if CFG["outproj"]:
                    if ok():
                        out_proj(t, mx)
            if CFG["ffn2"]:
                for _ in range(CFG.get("ffn_rep", 1)):
                    if ok():
                        ffn(t, 1, V_NW2)
                if CFG.get("wb_prefetch", True) and t != CFG["tiles"][-1]:
                    prefetch_wB(0)
            if CFG.get("store", True):
                if ok():
                    store_tile(t)
        if CFG["stateout"] and ok():
            state_outputs()
        g.barrier(engines=("sp",))

        with nc.Block() as block:
            @block.tensor
            def _(e):
                g.replay("pe", e)

            @block.scalar
            def _(e):
                g.replay("act", e)

            @block.vector
            def _(e):
                g.replay("dve", e)

            @block.gpsimd
            def _(e):
                g.replay("pool", e)

            @block.sync
            def _(e):
                g.replay("sp", e)
    return nc


_NC_CACHE = {}


def _host_layout(inp):
    f = np.float32
    def fm(v, n):
        return np.ascontiguousarray(np.asarray(v, f).reshape(n, 128).T)
    vecs = np.zeros((128, NV), f)
    vecs[:, V_NW1:V_NW1 + 8] = fm(inp["norm_ffn1_w"][0], 8)
    vecs[:, V_NWM:V_NWM + 8] = fm(inp["norm_mix_w"][0], 8)
    vecs[:, V_NW2:V_NW2 + 8] = fm(inp["norm_ffn2_w"][0], 8)
    vecs[:, V_NWF:V_NWF + 8] = fm(inp["final_norm_w"], 8)
    cw = np.asarray(inp["ssd_conv_w"][0], f)
    vecs[:, V_CW:V_CW + 48] = cw.reshape(4, 12, 128).transpose(2, 1, 0).reshape(128, 48)
    vecs[:, V_CB:V_CB + 12] = fm(inp["ssd_conv_b"][0], 12)
    sw = np.asarray(inp["sconv_w"][0], f)
    vecs[:, V_SW:V_SW + 24] = sw.reshape(3, 8, 128).transpose(2, 1, 0).reshape(128, 24)
    dsk = np.repeat(np.asarray(inp["d_skip"][0], f), 64)
    vecs[:, V_DFM:V_DFM + 8] = fm(dsk, 8)
    vecs[:, V_SNW:V_SNW + 8] = fm(inp["ssd_norm_w"][0], 8)
    rowc = np.zeros((128, NR), f)
    rowc[:, R_D:R_D + 1024] = dsk[None, :]
    rowc[:, R_SNW:R_SNW + 1024] = np.asarray(inp["ssd_norm_w"][0], f)[None, :]
    rowc[:, R_ALOG:R_ALOG + 16] = np.asarray(inp["a_log"][0], f)[None, :]
    rowc[:, R_DTB:R_DTB + 16] = np.asarray(inp["dt_bias"][0], f)[None, :]
    rowc[:, R_ALOG4:R_ALOG4 + 64] = np.tile(np.asarray(inp["a_log"][0], f), 4)[None, :]
    rowc[:, R_DTB4:R_DTB4 + 64] = np.tile(np.asarray(inp["dt_bias"][0], f), 4)[None, :]
    cm = np.zeros((128, 512), f)
    i = np.arange(128)
    cm[:, 0:128] = np.eye(128, dtype=f)
    cm[:, 128:256] = (i[:, None] <= i[None, :]).astype(f)
    cm[:, 256:384] = (i[:, None] > i[None, :]).astype(f)
    cm[:, 384:512] = 1.0
    return vecs, rowc, cm


def kernel(**inp):
    f = np.float32
    if "nc" not in _NC_CACHE:
        _NC_CACHE["nc"] = build_program()
    nc = _NC_CACHE["nc"]
    vecs, rowc, cm = _host_layout(inp)
    shared = {
        "wg1": np.ascontiguousarray(inp["ffn1_w_gate"][0], f), "wu1": np.ascontiguousarray(inp["ffn1_w_up"][0], f),
        "wd1": np.ascontiguousarray(inp["ffn1_w_down"][0], f),
        "wg2": np.ascontiguousarray(inp["ffn2_w_gate"][0], f), "wu2": np.ascontiguousarray(inp["ffn2_w_up"][0], f),
        "wd2": np.ascontiguousarray(inp["ffn2_w_down"][0], f),
        "win": np.ascontiguousarray(inp["w_in"][0], f), "wout": np.ascontiguousarray(inp["w_out"][0], f),
        "vecs": vecs, "rowc": rowc, "cmat": cm,
    }
    in_maps = []
    for c in range(NCORES):
        m = dict(shared)
        sl = slice(c * NS, (c + 1) * NS)
        m["xp"] = np.ascontiguousarray(inp["x_prompt"][c], f)
        m["xsm"] = np.ascontiguousarray(inp["x_sample"][sl, 0, :], f)
        m["ssm_in"] = np.ascontiguousarray(inp["state_ssm"][0, sl], f).reshape(NS, 8, 128, 128)
        m["cst_in"] = np.ascontiguousarray(inp["state_ssd_conv"][0, sl], f)
        m["sst_in"] = np.ascontiguousarray(inp["state_sconv"][0, sl], f)
        in_maps.append(m)
    res = run_bass_kernel_spmd(nc, in_maps, core_ids=list(range(NCORES)))
    R = res.results
    y_prompt = np.stack([R[c]["yp"] for c in range(NCORES)], 0).astype(f)
    y_sample = np.concatenate([R[c]["ysm"] for c in range(NCORES)], 0).reshape(NCORES * NS, 1, D).astype(f)
    ssm_p = np.stack([R[c]["o_ssm_p"].reshape(16, 64, 128) for c in range(NCORES)], 0)[None].astype(f)
    cv_p = np.stack([R[c]["o_cv_p"] for c in range(NCORES)], 0)[None].astype(f)
    sc_p = np.stack([R[c]["o_sc_p"] for c in range(NCORES)], 0)[None].astype(f)
    ssm_s = np.concatenate([R[c]["o_ssm_s"].reshape(NS, 16, 64, 128) for c in range(NCORES)], 0)[None].astype(f)
    cv_s = np.concatenate([R[c]["o_cv_s"] for c in range(NCORES)], 0)[None].astype(f)
    sc_s = np.concatenate([R[c]["o_sc_s"] for c in range(NCORES)], 0)[None].astype(f)
    return (y_prompt, y_sample, ssm_p, cv_p, sc_p, ssm_s, cv_s, sc_s)
```

```python
import numpy as np
import concourse.bass as bass
import concourse.mybir as mybir
from concourse.bass_utils import run_bass_kernel_spmd

F32 = mybir.dt.float32
BF16 = mybir.dt.bfloat16
AF = mybir.ActivationFunctionType
ALU = mybir.AluOpType
AX = mybir.AxisListType

NCORES = 8
D = 1024
KC = 8
DFF = 2816
NJ = 22
DIN = 5648
SEQ = 2048
TT = 512
NTILE = 4
NBLK = 4
NS = 16
EPS = 1e-6
O_Z, O_XBC, O_DT, O_SCB, O_SCC, O_SCH = 0, 1024, 2560, 2576, 3600, 4624

V_NW1, V_NWM, V_NW2, V_NWF = 0, 8, 16, 24
V_CW = 32
V_CB = 80
V_SW = 92
V_DFM = 116
V_SNW = 124
NV = 132
R_D, R_SNW, R_ALOG, R_DTB = 0, 1024, 2048, 2064
R_ALOG4, R_DTB4 = 2080, 2144
NR = 2208


CFG = dict(tiles=[0, 1, 2, 3], ffn1=True, mixer=True, ssd=True, sample=True, outproj=True, ffn2=True, stateout=True)


class Sem:
    def __init__(self, h):
        self.h = h
        self.n = 0


class Prog:
    def __init__(self, nc, sem_handles):
        self.nc = nc
        self.free_sems = list(sem_handles)
        self.all_sems = []
        self.q = {k: [] for k in ("pe", "act", "dve", "pool", "sp")}
        self.esem = {k: self.new_sem() for k in ("pe", "act", "dve")}
        self.seen = {k: {} for k in self.q}
        self.w = {}
        self.r = {}
        self.named = {}
        self.embed = False

    def new_sem(self):
        s = Sem(self.free_sems.pop())
        self.all_sems.append(s)
        return s

    def stream(self, name):
        if name not in self.named:
            self.named[name] = self.new_sem()
        return self.named[name]

    def _waits(self, eng, reads, writes):
        deps = {}
        for k in reads:
            for (s, v) in self.w.get(k, ()):
                deps[s] = max(deps.get(s, 0), v)
        for k in writes:
            for (s, v) in self.w.get(k, ()):
                deps[s] = max(deps.get(s, 0), v)
            for (s, v) in self.r.get(k, ()):
                deps[s] = max(deps.get(s, 0), v)
        waits = []
        for s, v in deps.items():
            if eng == "pe" and s is self.esem["pe"]:
                continue
            if self.seen[eng].get(s, 0) >= v:
                continue
            self.seen[eng][s] = v
            waits.append((s, v))
        return waits

    def _mark(self, t, reads, writes):
        for k in reads:
            self.r.setdefault(k, []).append(t)
        for k in writes:
            self.w[k] = [t]
            self.r[k] = []

    def op(self, eng, fn, reads=(), writes=(), inc=True):
        waits = self._waits(eng, reads, writes)
        s = self.esem[eng]
        if inc:
            s.n += 1
            t = (s, s.n)
            self.q[eng].append((waits, fn, s, 1, self.embed))
        else:
            t = (s, s.n + 1)
            self.q[eng].append((waits, fn, None, 0, self.embed))
        self._mark(t, reads, writes)

    def dma(self, queue, fn, reads, writes, stream):
        waits = self._waits(queue, reads, writes)
        stream.n += 16
        t = (stream, stream.n)
        self.q[queue].append((waits, fn, stream, 16, self.embed))
        self._mark(t, reads, writes)

    def sync(self, eng, keys):
        waits = self._waits(eng, (), keys)
        if waits:
            self.q[eng].append((waits, None, None, 0, False))

    def barrier(self, engines=("pe", "act", "dve", "sp")):
        for e in engines:
            waits = []
            for s in self.all_sems:
                if e == "pe" and s is self.esem["pe"]:
                    continue
                if s.n > self.seen[e].get(s, 0):
                    self.seen[e][s] = s.n
                    waits.append((s, s.n))
            if waits:
                self.q[e].append((waits, None, None, 0, False))

    def replay(self, eng, e):
        for (waits, fn, s, inc, emb) in self.q[eng]:
            if fn is None or not emb:
                for (ws, wv) in waits:
                    e.wait_ge(ws.h, wv)
                if fn is None:
                    continue
                ins = fn(e)
                if s is not None:
                    ins.then_inc(s.h, inc)
                continue
            for (ws, wv) in waits[:-1]:
                e.wait_ge(ws.h, wv)
            ins = fn(e)
            if waits:
                ins._wait_ge(waits[-1][0].h, waits[-1][1])
            if s is not None:
                ins.then_inc(s.h, inc)

    def mm(self, okey, out, lhsT, rhs, reads, start=True, stop=True, force_inc=False):
        self.op("pe", lambda e: e.matmul(out, lhsT, rhs, start=start, stop=stop),
                reads=reads, writes=[okey], inc=(stop or force_inc))

    def tr(self, okey, out, in_, ident, reads, last=True):
        self.op("pe", lambda e: e.transpose(out, in_, ident), reads=reads, writes=[okey], inc=last)

    def act(self, out, in_, func, reads, writes, bias=None, scale=None, accum=None):
        kw = {}
        if bias is not None:
            kw["bias"] = bias
        if scale is not None:
            kw["scale"] = scale
        if accum is not None:
            kw["accum_out"] = accum
        self.op("act", lambda e: e.activation(out=out, in_=in_, func=func, **kw), reads=reads, writes=writes)

    def tt(self, out, a, b, op, reads, writes):
        self.op("dve", lambda e: e.tensor_tensor(out=out, in0=a, in1=b, op=op), reads=reads, writes=writes)

    def ts(self, out, a, s1, op0, reads, writes, s2=None, op1=None):
        if op1 is None:
            self.op("dve", lambda e: e.tensor_scalar(out, a, s1, None, op0), reads=reads, writes=writes)
        else:
            self.op("dve", lambda e: e.tensor_scalar(out, a, s1, s2, op0, op1), reads=reads, writes=writes)

    def stt(self, out, in0, scalar, in1, op0, op1, reads, writes):
        self.op("dve", lambda e: e.scalar_tensor_tensor(out=out, in0=in0, scalar=scalar, in1=in1, op0=op0, op1=op1),
                reads=reads, writes=writes)

    def cp(self, eng, out, in_, reads, writes):
        if eng == "act":
            self.op("act", lambda e: e.copy(out=out, in_=in_), reads=reads, writes=writes)
        else:
            self.op("dve", lambda e: e.tensor_copy(out=out, in_=in_), reads=reads, writes=writes)

    def memset(self, ap, val, writes):
        self.op("dve", lambda e: e.memset(ap, val), reads=(), writes=writes)


class Arena:
    def __init__(self, t, nwords):
        self.t = t
        self.nwords = nwords
        self.off = 0

    def reset(self, off=0):
        self.off = off

    def alloc(self, shape, dtype, parts=128):
        n = 1
        for s in shape[1:]:
            n *= s
        words = n if dtype == F32 else (n + 1) // 2
        assert self.off + words <= self.nwords, ("arena overflow", self.off, words, self.nwords)
        ap = self.t[0:shape[0], self.off:self.off + words]
        self.off += words
        if dtype != F32:
            ap = ap.bitcast(dtype)[:, 0:n]
        if len(shape) == 3:
            ap = ap.rearrange("p (a b) -> p a b", a=shape[1])
        elif len(shape) == 4:
            ap = ap.rearrange("p (a b c) -> p a b c", a=shape[1], b=shape[2])
        return ap


def build_program():
    nc = bass.Bass("TRN2", target_bir_lowering=False)

    def din(name, shape):
        return nc.dram_tensor(name, list(shape), F32, kind="ExternalInput").ap()

    def dout(name, shape):
        return nc.dram_tensor(name, list(shape), F32, kind="ExternalOutput").ap()

    xp = din("xp", [SEQ, D])
    xsm = din("xsm", [NS, D])
    ssm_in = din("ssm_in", [NS, 8, 128, 128])
    cst_in = din("cst_in", [NS, 3, 1536])
    sst_in = din("sst_in", [NS, 2, 1024])
    wg = [din("wg1", [D, DFF]), din("wg2", [D, DFF])]
    wu = [din("wu1", [D, DFF]), din("wu2", [D, DFF])]
    wd = [din("wd1", [DFF, D]), din("wd2", [DFF, D])]
    win = din("win", [D, DIN])
    wout = din("wout", [2 * D, D])
    vecs_d = din("vecs", [128, NV])
    rowc_d = din("rowc", [128, NR])
    cmat_d = din("cmat", [128, 512])

    yp = dout("yp", [SEQ, D])
    ysm = dout("ysm", [NS, D])
    o_ssm_p = dout("o_ssm_p", [8, 128, 128])
    o_cv_p = dout("o_cv_p", [3, 1536])
    o_sc_p = dout("o_sc_p", [2, 1024])
    o_ssm_s = dout("o_ssm_s", [NS, 8, 128, 128])
    o_cv_s = dout("o_cv_s", [NS, 3, 1536])
    o_sc_s = dout("o_sc_s", [NS, 2, 1024])

    wg_v = [w.rearrange("(k p) n -> p k n", p=128) for w in wg]
    wu_v = [w.rearrange("(k p) n -> p k n", p=128) for w in wu]
    wd_v = [w.rearrange("(j p) m -> p j m", p=128) for w in wd]
    win_v = win.rearrange("(k p) n -> p k n", p=128)
    wout_v = wout.rearrange("(c p) m -> p c m", p=128)

    NCOL = TT + NS
    AW = 23800

    from contextlib import ExitStack
    with ExitStack() as es:
        def sb(name, shape, dt):
            return es.enter_context(nc.sbuf_tensor("sb_" + name, list(shape), dt))

        def ps(name, shape, dt):
            return es.enter_context(nc.psum_tensor("ps_" + name, list(shape), dt))

        sems = [es.enter_context(nc.semaphore(f"s{i}")) for i in range(64)]
        g = Prog(nc, sems)

        xT = sb("xT", [128, KC, NCOL], F32)
        hT = sb("hT", [128, KC, NCOL], BF16)
        rstd = sb("rstd", [128, NCOL], F32)
        lnv = sb("lnv", [128, NCOL], F32)
        sqb = [sb(f"sq{i}", [128, NCOL], BF16) for i in range(2)]
        sgb = [sb(f"sg{i}", [128, TT], F32) for i in range(2)]
        wA = [sb(f"wA{i}", [128, KC, 256], BF16) for i in range(5)]
        wB = [sb(f"wB{i}", [128, NJ, 128], BF16) for i in range(3)]
        cmat = sb("cmat", [128, 512], F32)
        cbf = sb("cbf", [128, 256], BF16)
        vecs = sb("vecs", [128, NV], F32)
        rowc = sb("rowc", [128, NR], F32)
        smallc = sb("smallc", [128, 256], F32)
        S_f = sb("S_f", [128, 1024], F32)
        S_bf = sb("S_bf", [128, 1024], BF16)
        hist = sb("hist", [128, 12, 3], F32)
        uhist = sb("uhist", [128, 8, 2], F32)
        Wz_t = sb("Wz", [128, KC, 1024], BF16)
        Wdt_t = sb("Wdt", [128, KC, 16], BF16)
        arena_t = sb("arena", [128, AW], F32)
        ar = Arena(arena_t, AW)

        banks = [ps(f"pb{i}", [128, 512], F32) for i in range(7)]
        psbf = ps("psbf", [128, 1024], BF16)
        bank_i = [0]

        def bank():
            i = bank_i[0] % 7
            bank_i[0] += 1
            return ("pb", i), banks[i]

        ident_f = cmat[:, 0:128]
        U_f = cmat[:, 128:256]
        Ms_f = cmat[:, 256:384]
        ones_f = cmat[:, 384:512]
        ident_b = cbf[:, 0:128]
        ones_b = cbf[:, 128:256]
        eps_t = smallc[:, 0:1]
        one_t = smallc[:, 1:2]
        A_bc4 = smallc[:, 64:128]
        A_bc = smallc[:, 64:80]

        R2_OFF = (16 * NCOL) // 2 + KC * NCOL + (4 * NCOL) // 2 + 4 * NS + 2 * (3 + TT) + 2 * (2 + TT) + 2 * TT
        wA_i = [0]
        wB_i = [0]
        wz_loaded = [False]
        sg_i = [0]
        wA_s = [g.stream(f"wA{i}") for i in range(5)]
        wB_s = [g.stream(f"wB{i}") for i in range(3)]

        def wloadA(src_ap, ncols):
            i = wA_i[0] % 5
            wA_i[0] += 1
            key = ("wA", i)
            dst = wA[i][:, :, 0:ncols]
            if CFG.get("dma_skip", False) and (wA_i[0] % 2 == 1):
                return key, wA[i]
            g.dma("pool", lambda e: e.dma_start(out=dst, in_=src_ap), reads=(), writes=[key], stream=wA_s[i])
            return key, wA[i]

        def wloadB(src_ap, nrow):
            i = wB_i[0] % 3
            wB_i[0] += 1
            key = ("wB", i)
            dst = wB[i][:, 0:nrow, :]
            g.dma("pool", lambda e: e.dma_start(out=dst, in_=src_ap), reads=(), writes=[key], stream=wB_s[i])
            return key, wB[i]

        wB_pref = {0: [], 1: []}

        def prefetch_wB(which, n=3):
            assert not wB_pref[which]
            for m in range(n):
                wB_pref[which].append(wloadB(wd_v[which][:, :, m * 128:(m + 1) * 128], NJ))

        def spdma(name, out, in_, reads, writes):
            g.dma("sp", lambda e: e.dma_start(out=out, in_=in_), reads=reads, writes=writes, stream=g.stream(name))

        spdma("c0", cmat[:, :], cmat_d[:, :], (), ["cmat"])
        spdma("c1", vecs[:, :], vecs_d[:, :], (), ["vecs"])
        spdma("c2", rowc[:, :], rowc_d[:, :], (), ["rowc"])
        g.cp("dve", cbf[:, 0:128], cmat[:, 0:128], ["cmat"], ["cbf"])
        g.cp("dve", cbf[:, 128:256], cmat[:, 384:512], ["cmat"], ["cbf"])
        g.memset(smallc[:, 0:1], EPS, ["smallc"])
        g.memset(smallc[:, 1:2], 1.0, ["smallc"])
        g.act(smallc[:, 128:192], rowc[:, R_ALOG4:R_ALOG4 + 64], AF.Exp, ["rowc", "smallc"], ["smallc"])
        g.ts(A_bc4, smallc[:, 128:192], -1.0, ALU.mult, ["smallc"], ["smallc"])
        g.memset(S_f[:, :], 0.0, ["S_f"])
        g.memset(S_bf[:, :], 0.0, ["S_bf"])
        g.memset(hist[:, :, :], 0.0, [("hist", c) for c in range(12)])
        g.memset(uhist[:, :, :], 0.0, [("uhist", c) for c in range(8)])
        CONSTK = ["cmat", "cbf", "vecs", "rowc", "smallc"]

        def segs_of(t):
            return [(0, TT)] + ([(TT, NS)] if t == NTILE - 1 else [])

        def norm_rstd(segs, voff):
            ncol = segs[-1][0] + segs[-1][1]
            pbs = [bank() for _ in segs]
            for k in range(KC):
                sq = sqb[k % 2]
                g.act(sq[:, 0:ncol], xT[:, k, 0:ncol], AF.Square, [("xT", k)], [("sq", k % 2)])
                for si, (c0, n) in enumerate(segs):
                    g.mm(pbs[si][0], pbs[si][1][:, 0:n], ones_b, sq[:, c0:c0 + n],
                         [("sq", k % 2), "cbf"], start=(k == 0), stop=(k == KC - 1), force_inc=True)
            for si, (c0, n) in enumerate(segs):
                g.act(lnv[:, c0:c0 + n], pbs[si][1][:, 0:n], AF.Ln, [pbs[si][0], "smallc"], [("lnv", si)],
                      bias=eps_t, scale=1.0 / D)
                g.act(rstd[:, c0:c0 + n], lnv[:, c0:c0 + n], AF.Exp, [("lnv", si)], [("rstd", si)], scale=-0.5)

        def norm_to_hT(segs, voff):
            norm_rstd(segs, voff)
            ncol = segs[-1][0] + segs[-1][1]
            rk = [("rstd", si) for si in range(len(segs))]
            for k in range(KC):
                g.stt(hT[:, k, 0:ncol], xT[:, k, 0:ncol], vecs[:, voff + k:voff + k + 1], rstd[:, 0:ncol],
                      ALU.mult, ALU.mult, [("xT", k), "vecs"] + rk, [("hT", k)])

        def ffn(t, which, voff):
            g.embed = CFG.get("embed_ffn", True)
            segs = segs_of(t)
            ncol = segs[-1][0] + segs[-1][1]
            ar.reset(R2_OFF)
            acts = ar.alloc([128, NJ, NCOL], BF16)
            norm_to_hT(segs, voff)
            hk = [("hT", k) for k in range(KC)]
            g.sync("dve", [("io", 0), ("io", 1), ("io", 2, 0), ("io", 2, 1), ("io", 3, 0), ("io", 3, 1)])
            for s in range(NJ // 2):
                kg, tg = wloadA(wg_v[which][:, :, s * 256:(s + 1) * 256], 256)
                ku, tu = wloadA(wu_v[which][:, :, s * 256:(s + 1) * 256], 256)
                for jj in range(2):
                    j = 2 * s + jj
                    for si, (c0, n) in enumerate(segs):
                        bg = bank()
                        bu = bank()
                        for k in range(KC):
                            g.mm(bg[0], bg[1][:, 0:n], tg[:, k, jj * 128:(jj + 1) * 128], hT[:, k, c0:c0 + n],
                                 [kg, ("hT", k)], start=(k == 0), stop=(k == KC - 1))
                        for k in range(KC):
                            if CFG.get("skip_up", False) and k < KC - 1:
                                continue
                            g.mm(bu[0], bu[1][:, 0:n], tu[:, k, jj * 128:(jj + 1) * 128], hT[:, k, c0:c0 + n],
                                 [ku, ("hT", k)], start=(k == 0 or CFG.get("skip_up", False)), stop=(k == KC - 1))
                        sg_i[0] += 1
                        sgi = sg_i[0] % 2
                        g.act(sgb[sgi][:, 0:n], bg[1][:, 0:n], AF.Silu, [bg[0]], [("sg", sgi)])
                        g.tt(acts[:, j, c0:c0 + n], sgb[sgi][:, 0:n], bu[1][:, 0:n], ALU.mult,
                             [("sg", sgi), bu[0]], [("acts", j, si)])
            g.act(smallc[:, 200:201], one_t, AF.Exp, ["smallc"], ["tblwarm2"])
            for m in range(KC):
                if wB_pref[which]:
                    kd, td = wB_pref[which].pop(0)
                else:
                    kd, td = wloadB(wd_v[which][:, :, m * 128:(m + 1) * 128], NJ)
                for si, (c0, n) in enumerate(segs):
                    bo = bank()
                    for j in range(NJ):
                        g.mm(bo[0], bo[1][:, 0:n], td[:, j, :], acts[:, j, c0:c0 + n],
                             [kd, ("acts", j, si)], start=(j == 0), stop=(j == NJ - 1))
                    g.stt(xT[:, m, c0:c0 + n], bo[1][:, 0:n], 0.5, xT[:, m, c0:c0 + n], ALU.mult, ALU.add,
                          [bo[0], ("xT", m)], [("xT", m)])
            if (which == 0 and t == NTILE - 1) or CFG.get("all_barriers", False):
                g.barrier()

        def load_tile(t):
            g.embed = CFG.get("embed_io", True)
            ar.reset(R2_OFF)
            xrow = [ar.alloc([128, D], F32) for _ in range(2)]
            for b in range(NBLK):
                r0 = t * TT + b * 128
                xr = xrow[b % 2]
                spdma(f"xrow{b % 2}", xr[:, :], xp[r0:r0 + 128, :], (), [("io", b % 2)])
                for hf in range(0 if (CFG.get("load_dma_only", False) and t == 3) else 2):
                    bk = bank()
                    for q in range(4):
                        k = hf * 4 + q
                        g.tr(bk[0], bk[1][:, q * 128:(q + 1) * 128], xr[:, k * 128:(k + 1) * 128], ident_f,
                             [("io", b % 2), "cmat"], last=(q == 3))
                    g.cp("act", xT[:, hf * 4:(hf + 1) * 4, b * 128:(b + 1) * 128],
                         bk[1][:, :].rearrange("p (q c) -> p q c", q=4), [bk[0]],
                         [("xT", hf * 4 + q) for q in range(4)])
            if t == NTILE - 1 and not CFG.get("skip_xs", False):
                xsr = ar.alloc([NS, D], F32)
                spdma("xsr", xsr[:, :], xsm[:, :], (), [("io", 2, 0), ("io", 2, 1)])
                for hf in range(2):
                    bk = bank()
                    for q in range(4):
                        k = hf * 4 + q
                        g.tr(bk[0], bk[1][:, q * NS:(q + 1) * NS], xsr[:, k * 128:(k + 1) * 128], ident_f[0:NS, 0:NS],
                             [("io", 2, 0), ("io", 2, 1), "cmat"], last=(q == 3))
                    g.cp("act", xT[:, hf * 4:(hf + 1) * 4, TT:TT + NS],
                         bk[1][:, 0:4 * NS].rearrange("p (q c) -> p q c", q=4), [bk[0]],
                         [("xT", hf * 4 + q) for q in range(4)])
            if CFG.get("all_barriers", False):
                g.barrier()

        def store_tile(t):
            g.embed = CFG.get("embed_io", True)
            segs = segs_of(t)
            ar.reset(R2_OFF)
            yTb = [ar.alloc([128, KC, 128], F32) for _ in range(2)]
            yrow = [ar.alloc([128, D], F32) for _ in range(2)]
            if CFG.get("st_norm", True):
                norm_rstd(segs, V_NWF)
            rk = [("rstd", si) for si in range(len(segs))]
            nblk = NBLK + (1 if t == NTILE - 1 else 0)
            for b in range(nblk):
                c0 = b * 128
                n = 128 if b < NBLK else NS
                yt = yTb[b % 2]
                yr = yrow[b % 2]
                for k in range(KC):
                    g.stt(yt[:, k, 0:n], xT[:, k, c0:c0 + n], vecs[:, V_NWF + k:V_NWF + k + 1], rstd[:, c0:c0 + n],
                          ALU.mult, ALU.mult, [("xT", k), "vecs"] + rk, [("io", b % 2)])
                for hf in range(2 if CFG.get("st_tr", True) else 0):
                    bk = bank()
                    for q in range(4):
                        k = hf * 4 + q
                        g.tr(bk[0], bk[1][0:n, q * 128:(q + 1) * 128], yt[:, k, 0:n], ident_f,
                             [("io", b % 2), "cmat"], last=(q == 3))
                    g.cp("act", yr[0:n, hf * 512:(hf + 1) * 512], bk[1][0:n, :], [bk[0]], [("io", 2 + b % 2, hf)])
                if not CFG.get("st_dma", True):
                    continue
                if b < NBLK:
                    r0 = t * TT + b * 128
                    spdma(f"yst{b % 2}", yp[r0:r0 + 128, :], yr[:, :], [("io", 2 + b % 2, 0), ("io", 2 + b % 2, 1)],
                          [("ypd", t, b)])
                else:
                    spdma(f"yst{b % 2}", ysm[:, :], yr[0:NS, :], [("io", 2 + b % 2, 0), ("io", 2 + b % 2, 1)],
                          [("ysd",)])
            if CFG.get("all_barriers", False):
                g.barrier()

        def mixer(t):
            g.embed = CFG.get("embed_mix", False)
            segs = segs_of(t)
            last = (t == NTILE - 1)
            ncol = segs[-1][0] + segs[-1][1]
            norm_to_hT(segs, V_NWM)
            hk = [("hT", k) for k in range(KC)]
            ar.reset(0)
            Wz = Wz_t[:, :, :]
            Wdt = Wdt_t[:, :, :]
            ymixT = ar.alloc([128, 16, NCOL], BF16)
            xsT = ar.alloc([128, KC, NCOL], F32)
            BCT = ar.alloc([128, 4, NCOL], BF16)
            BCs_f = ar.alloc([128, 4, NS], F32)
            rawb = [ar.alloc([128, 3 + TT], F32) for _ in range(2)]
            ubuf = [ar.alloc([128, 2 + TT], F32) for _ in range(2)]
            accb = [ar.alloc([128, TT], F32) for _ in range(2)]
            R2 = ar.off
            assert R2 == R2_OFF, (R2, R2_OFF)

            if not wz_loaded[0]:
                wz_loaded[0] = True
                g.dma("pool", lambda e: e.dma_start(out=Wz, in_=win_v[:, :, O_Z:O_Z + 1024]), (), ["Wz"], g.stream("Wz"))
                g.dma("pool", lambda e: e.dma_start(out=Wdt, in_=win_v[:, :, O_DT:O_DT + 16]), (), ["Wdt"], g.stream("Wdt"))

            if last:
                strow = ar.alloc([48, 1536], F32)
                scrow = ar.alloc([32, 1024], F32)
                hs = ar.alloc([128, 12, 48], F32)
                us = ar.alloc([128, 8, 32], F32)
                smp = ar.alloc([128, 8, NS], F32)
                smp2 = ar.alloc([128, 8, NS], F32)
                cvrow = ar.alloc([NS, 1536], F32)
                urow = ar.alloc([NS, 1024], F32)
                spdma("strow", strow[:, :], cst_in.rearrange("b k c -> (b k) c"), (), ["strow"])
                spdma("scrow", scrow[:, :], sst_in.rearrange("b k c -> (b k) c"), (), ["scrow"])
                for c in range(12):
                    bk = bank()
                    g.tr(bk[0], bk[1][:, 0:48], strow[:, c * 128:(c + 1) * 128], ident_f[0:48, 0:48], ["strow", "cmat"])
                    g.cp("act", hs[:, c, :], bk[1][:, 0:48], [bk[0]], [("hs", c)])
                for c in range(8):
                    bk = bank()
                    g.tr(bk[0], bk[1][:, 0:32], scrow[:, c * 128:(c + 1) * 128], ident_f[0:32, 0:32], ["scrow", "cmat"])
                    g.cp("act", us[:, c, :], bk[1][:, 0:32], [bk[0]], [("us", c)])
                spdma("cvpass", o_cv_s[:, 0:2, :], cst_in[:, 1:3, :], (), ["o_cv_s01"])
                spdma("scpass", o_sc_s[:, 0:1, :], sst_in[:, 1:2, :], (), ["o_sc_s0"])

            for qd in range(4):
                kb, tb = wloadA(win_v[:, :, O_SCB + qd * 256:O_SCB + (qd + 1) * 256], 256)
                kc_, tc_ = wloadA(win_v[:, :, O_SCC + qd * 256:O_SCC + (qd + 1) * 256], 256)
                kh, th = wloadA(win_v[:, :, O_SCH + qd * 256:O_SCH + (qd + 1) * 256], 256)
                for jj in range(2):
                    c = qd * 2 + jj
                    for si, (c0, n) in enumerate(segs):
                        pb_, pc_, ph_ = bank(), bank(), bank()
                        for (pk, tw, kw) in ((pb_, tb, kb), (pc_, tc_, kc_), (ph_, th, kh)):
                            for k in range(KC):
                                g.mm(pk[0], pk[1][:, 0:n], tw[:, k, jj * 128:(jj + 1) * 128], hT[:, k, c0:c0 + n],
                                     [kw, ("hT", k)], start=(k == 0), stop=(k == KC - 1))
                        sg_i[0] += 1
                        sgi = sg_i[0] % 2
                        ct = sgb[sgi]
                        g.cp("act", ct[:, 0:n], pc_[1][:, 0:n], [pc_[0]], [("sg", sgi)])
                        w0 = vecs[:, V_SW + c * 3 + 0:V_SW + c * 3 + 1]
                        w1 = vecs[:, V_SW + c * 3 + 1:V_SW + c * 3 + 2]
                        w2 = vecs[:, V_SW + c * 3 + 2:V_SW + c * 3 + 3]
                        if si == 0:
                            ub = ubuf[c % 2]
                            uk = ("ubuf", c % 2)
                            g.cp("dve", ub[:, 0:2], uhist[:, c, :], [("uhist", c)], [uk])
                            g.tt(ub[:, 2:2 + TT], ct[:, 0:TT], ph_[1][:, 0:TT], ALU.mult, [("sg", sgi), ph_[0]], [uk])
                            ac = accb[c % 2]
                            ak = ("accb", c % 2)
                            g.ts(ac[:, :], ub[:, 0:TT], w0, ALU.mult, [uk, "vecs"], [ak])
                            g.stt(ac[:, :], ub[:, 1:1 + TT], w1, ac[:, :], ALU.mult, ALU.add, [uk, ak], [ak])
                            g.stt(ac[:, :], ub[:, 2:2 + TT], w2, ac[:, :], ALU.mult, ALU.add, [uk, ak], [ak])
                            g.tt(ymixT[:, 8 + c, 0:TT], ac[:, :], pb_[1][:, 0:TT], ALU.mult, [ak, pb_[0]],
                                 [("ymix", 8 + c, "p")])
                            g.cp("dve", uhist[:, c, :], ub[:, TT:TT + 2], [uk], [("uhist", c)])
                        else:
                            u_s = smp[:, c, :]
                            v_s = smp2[:, c, :]
                            g.tt(u_s, ct[:, 0:NS], ph_[1][:, 0:NS], ALU.mult, [("sg", sgi), ph_[0]], [("smp", c)])
                            usv = us[:, c, :].rearrange("p (b k) -> p b k", k=2)
                            g.ts(v_s, usv[:, :, 0], w0, ALU.mult, [("us", c), "vecs"], [("smp2", c)])
                            g.stt(v_s, usv[:, :, 1], w1, v_s, ALU.mult, ALU.add, [("us", c), ("smp2", c)], [("smp2", c)])
                            g.stt(v_s, u_s, w2, v_s, ALU.mult, ALU.add, [("smp", c), ("smp2", c)], [("smp2", c)])
                            g.tt(ymixT[:, 8 + c, TT:TT + NS], v_s, pb_[1][:, 0:NS], ALU.mult, [("smp2", c), pb_[0]],
                                 [("ymix", 8 + c, "s")])
                            bk = bank()
                            g.tr(bk[0], bk[1][0:NS, 0:128], u_s, ident_f, [("smp", c), "cmat"])
                            g.cp("act", urow[:, c * 128:(c + 1) * 128], bk[1][0:NS, 0:128], [bk[0]], [("urow", c)])
            if last:
                spdma("urow", o_sc_s[:, 1:2, :], urow[:, :].rearrange("b (o c) -> b o c", o=1),
                      [("urow", c) for c in range(8)], ["o_sc_s1"])

            pend_b = [None]
            for s in range(6):
                kx, tx = wloadA(win_v[:, :, O_XBC + s * 256:O_XBC + (s + 1) * 256], 256)
                for jj in range(2):
                    c = s * 2 + jj
                    cw = [vecs[:, V_CW + c * 4 + i:V_CW + c * 4 + i + 1] for i in range(4)]
                    cb = vecs[:, V_CB + c:V_CB + c + 1]
                    for si, (c0, n) in enumerate(segs):
                        px = bank()
                        for k in range(KC):
                            g.mm(px[0], px[1][:, 0:n], tx[:, k, jj * 128:(jj + 1) * 128], hT[:, k, c0:c0 + n],
                                 [kx, ("hT", k)], start=(k == 0), stop=(k == KC - 1))
                        if si == 0:
                            rb = rawb[c % 2]
                            rk_ = ("rawb", c % 2)
                            g.cp("act", rb[:, 0:3], hist[:, c, :], [("hist", c)], [rk_])
                            g.cp("act", rb[:, 3:3 + TT], px[1][:, 0:TT], [px[0]], [rk_])
                            ac = accb[c % 2]
                            ak = ("accb", c % 2)
                            g.act(ac[:, :], rb[:, 0:TT], AF.Identity, [rk_, "vecs"], [ak], scale=cw[0])
                            for i in range(1, 4):
                                g.stt(ac[:, :], rb[:, i:i + TT], cw[i], ac[:, :], ALU.mult, ALU.add, [rk_, ak], [ak])
                            g.cp("dve", hist[:, c, :], rb[:, TT:TT + 3], [rk_], [("hist", c)])

                            def stage_b(c=c, ac=ac, ak=ak, cb=cb):
                                if c < 8:
                                    g.act(xsT[:, c, 0:TT], ac[:, :], AF.Silu, [ak, "vecs"], [("xsT", c, "p")], bias=cb)
                                else:
                                    g.act(BCT[:, c - 8, 0:TT], ac[:, :], AF.Silu, [ak, "vecs"], [("BCT", c - 8)], bias=cb)
                            if pend_b[0] is not None:
                                pend_b[0]()
                            pend_b[0] = stage_b
                        else:
                            rs = ar_small_raw[c]
                            g.cp("act", rs, px[1][:, 0:NS], [px[0]], [("rs", c)])
                            hv = hs[:, c, :].rearrange("p (b k) -> p b k", k=3)
                            a_s = ar_small_acc[c]
                            g.ts(a_s, hv[:, :, 0], cw[0], ALU.mult, [("hs", c), "vecs"], [("as", c)])
                            g.stt(a_s, hv[:, :, 1], cw[1], a_s, ALU.mult, ALU.add, [("hs", c), ("as", c)], [("as", c)])
                            g.stt(a_s, hv[:, :, 2], cw[2], a_s, ALU.mult, ALU.add, [("hs", c), ("as", c)], [("as", c)])
                            g.stt(a_s, rs, cw[3], a_s, ALU.mult, ALU.add, [("rs", c), ("as", c)], [("as", c)])
                            if c < 8:
                                g.act(xsT[:, c, TT:TT + NS], a_s, AF.Silu, [("as", c), "vecs"], [("xsT", c, "s")], bias=cb)
                            else:
                                g.act(BCs_f[:, c - 8, :], a_s, AF.Silu, [("as", c), "vecs"], [("BCs", c - 8)], bias=cb)
                            bk = bank()
                            g.tr(bk[0], bk[1][0:NS, 0:128], rs, ident_f, [("rs", c), "cmat"])
                            g.cp("act", cvrow[:, c * 128:(c + 1) * 128], bk[1][0:NS, 0:128], [bk[0]], [("cvrow", c)])
            if pend_b[0] is not None:
                pend_b[0]()
            if CFG.get("wb_prefetch", True) and CFG["ffn2"]:
                prefetch_wB(1)
            if last:
                spdma("cvrow", o_cv_s[:, 2:3, :], cvrow[:, :].rearrange("b (o c) -> b o c", o=1),
                      [("cvrow", c) for c in range(12)], ["o_cv_s2"])
            if last or CFG.get("all_barriers", False):
                g.barrier()
            return dict(Wz=Wz, Wdt=Wdt, ymixT=ymixT, xsT=xsT, BCT=BCT, BCs_f=BCs_f, R2=R2)

        ar_small = sb("ar_small", [128, 24, NS], F32)
        ar_small_raw = [ar_small[:, c, :] for c in range(12)]
        ar_small_acc = [ar_small[:, 12 + c, :] for c in range(12)]

        def ssd_blocks(t, mx):
            g.embed = CFG.get("embed_mix", False)
            Wz, Wdt, ymixT, xsT, BCT = mx["Wz"], mx["Wdt"], mx["ymixT"], mx["xsT"], mx["BCT"]
            ar.reset(mx["R2"])
            zs = ar.alloc([128, 1024], F32)
            sm = ar.alloc([128, 16, 64], F32)
            xs_tm = ar.alloc([128, 1024], F32)
            xdt = ar.alloc([128, 1024], BF16)
            xdtd = ar.alloc([128, 1024], BF16)
            B_tm = ar.alloc([128, 2, 128], BF16)
            CBm = ar.alloc([128, 2, 128], F32)
            rseg = ar.alloc([128, 8, 128], F32)
            Lm = ar.alloc([128, 8, 128], F32)
            G = ar.alloc([128, 16, 128], BF16)
            y1 = ar.alloc([128, 1024], F32)
            yg = ar.alloc([128, 1024], F32)
            yn = ar.alloc([128, 1024], BF16)
            junk2 = ar.alloc([128, 2, 512], F32)
            hk = [("hT", k) for k in range(KC)]
            dtr, dtt, dta, dte, dtl, dtv, av, acum, totb, eacum, dend, etot, dtde, ss2, ln2, rs2 = [sm[:, i, 0:16] for i in range(16)]
            rsegq = [rseg[:, 0:4, :], rseg[:, 4:8, :]]
            Lmq = [Lm[:, 0:4, :], Lm[:, 4:8, :]]

            def z_mm(b):
                cb0_ = b * 128
                bzs = []
                for hf in range(2):
                    bz = bank()
                    for k in range(KC):
                        g.mm(bz[0], bz[1][:, :], hT[:, k, cb0_:cb0_ + 128], Wz[:, k, hf * 512:(hf + 1) * 512],
                             [("hT", k), "Wz"], start=(k == 0), stop=(k == KC - 1))
                    bzs.append(bz)
                return bzs

            def z_act(bzs):
                for hf in range(2):
                    g.act(zs[:, hf * 512:(hf + 1) * 512], bzs[hf][1][:, :], AF.Silu, [bzs[hf][0]], [("zs", hf)])
                g.act(sm[:, 12, 32:33], one_t, AF.Exp, ["smallc"], ["tblwarm"])

            z_act(z_mm(0))
            for b in range(NBLK):
                cb0 = b * 128
                bd = bank()
                for k in range(KC):
                    g.mm(bd[0], bd[1][:, 0:16], hT[:, k, cb0:cb0 + 128], Wdt[:, k, :], [("hT", k), "Wdt"],
                         start=(k == 0), stop=(k == KC - 1))
                g.tt(dtt, bd[1][:, 0:16], rowc[:, R_DTB:R_DTB + 16], ALU.add, [bd[0], "rowc"], ["dtt"])
                g.act(dta, dtt, AF.Abs, ["dtt"], ["dta"])
                g.act(dte, dta, AF.Exp, ["dta"], ["dte"], scale=-1.0)
                g.act(dtl, dte, AF.Ln, ["dte", "smallc"], ["dtl"], bias=one_t)
                for hf in range(2):
                    bx = bank()
                    for q in range(4):
                        c = hf * 4 + q
                        g.tr(bx[0], bx[1][:, q * 128:(q + 1) * 128], xsT[:, c, cb0:cb0 + 128], ident_f,
                             [("xsT", c, "p"), "cmat"], last=(q == 3))
                    sl = slice(hf * 512, (hf + 1) * 512)
                    g.cp("act", xs_tm[:, sl], bx[1][:, :], [bx[0]], [("xs_tm", hf)])
                for gi in range(2):
                    g.tr("psbf", psbf[:, gi * 128:(gi + 1) * 128], BCT[:, gi, cb0:cb0 + 128], ident_b,
                         [("BCT", gi), "cbf"], last=(gi == 1))
                g.cp("act", B_tm[:, :, :], psbf[:, 0:256].rearrange("p (a b) -> p a b", a=2), ["psbf"], ["B_tm"])
                g.stt(dtv, dtt, 0.0, dtl, ALU.max, ALU.add, ["dtt", "dtl"], ["dtv"])
                g.tt(av, dtv, A_bc, ALU.mult, ["dtv", "smallc"], ["av"])
                ba = bank()
                g.mm(ba[0], ba[1][:, 0:16], U_f, av, ["cmat", "av"])
                g.mm(ba[0], ba[1][:, 16:32], ones_f, av, ["cmat", "av"])
                g.cp("dve", acum, ba[1][:, 0:16], [ba[0]], ["acum"])
                g.act(eacum, ba[1][:, 0:16], AF.Exp, [ba[0]], ["eacum"])
                g.act(etot, ba[1][:, 16:32], AF.Exp, [ba[0]], ["etot"])
                g.tt(dend, ba[1][:, 16:32], acum, ALU.subtract, [ba[0], "acum"], ["dend"])
                g.act(dend, dend, AF.Exp, ["dend"], ["dend"])
                g.tt(dtde, dtv, dend, ALU.mult, ["dtv", "dend"], ["dtde"])
                bc = bank()
                for gi in range(2):
                    g.mm(bc[0], bc[1][:, gi * 128:(gi + 1) * 128], BCT[:, gi, cb0:cb0 + 128], BCT[:, 2 + gi, cb0:cb0 + 128],
                         [("BCT", gi), ("BCT", 2 + gi)])
                for gi in range(2):
                    g.tt(CBm[:, gi, :], bc[1][:, gi * 128:(gi + 1) * 128], U_f, ALU.mult, [bc[0], "cmat"], [("CBm", gi)])
                def seg_a(qq):
                    rq, lq = rsegq[qq % 2], Lmq[qq % 2]
                    g.tt(rq, U_f.unsqueeze(1).to_broadcast([128, 4, 128]),
                         av[:, qq * 4:(qq + 1) * 4].unsqueeze(2).to_broadcast([128, 4, 128]), ALU.mult,
                         ["cmat", "av"], [("rseg", qq % 2)])
                    bs = bank()
                    g.mm(bs[0], bs[1][:, :], Ms_f, rq.rearrange("p a b -> p (a b)"), ["cmat", ("rseg", qq % 2)])
                    g.act(lq.rearrange("p a b -> p (a b)"), bs[1][:, :], AF.Exp, [bs[0]], [("Lm", qq % 2)])

                def seg_b(qq):
                    lq = Lmq[qq % 2]
                    g.tt(G[:, qq * 4:(qq + 1) * 4, :], lq, CBm[:, qq // 2, :].unsqueeze(1).to_broadcast([128, 4, 128]), ALU.mult,
                         [("Lm", qq % 2), ("CBm", qq // 2)], [("G", qq)])

                def xmul(hf):
                    sl = slice(hf * 512, (hf + 1) * 512)
                    g.tt(xdt[:, sl].rearrange("p (h d) -> p h d", h=8), xs_tm[:, sl].rearrange("p (h d) -> p h d", h=8),
                         dtv[:, hf * 8:(hf + 1) * 8].unsqueeze(2).to_broadcast([128, 8, 64]), ALU.mult,
                         [("xs_tm", hf), "dtv"], [("xdt", hf)])
                    g.tt(xdtd[:, sl].rearrange("p (h d) -> p h d", h=8), xs_tm[:, sl].rearrange("p (h d) -> p h d", h=8),
                         dtde[:, hf * 8:(hf + 1) * 8].unsqueeze(2).to_broadcast([128, 8, 64]), ALU.mult,
                         [("xs_tm", hf), "dtde"], [("xdtd", hf)])
                    g.tt(yg[:, sl], xs_tm[:, sl], rowc[:, R_D + hf * 512:R_D + (hf + 1) * 512], ALU.mult,
                         [("xs_tm", hf), "rowc"], [("yg", hf)])

                seg_a(0)
                seg_a(1)
                xmul(0)
                seg_b(0)
                seg_a(2)
                xmul(1)
                seg_b(1)
                seg_a(3)
                seg_b(2)
                seg_b(3)
                by = [bank(), bank()]
                for h in range(16):
                    g.mm(by[h // 8][0], by[h // 8][1][:, (h % 8) * 64:(h % 8 + 1) * 64], G[:, h, :],
                         xdt[:, h * 64:(h + 1) * 64], [("G", h // 4), ("xdt", h // 8)])
                g.cp("act", S_bf[:, :], S_f[:, :], ["S_f"], ["S_bf"])
                bo = [bank(), bank()]
                for gi in range(2):
                    g.mm(bo[gi][0], bo[gi][1][:, :], BCT[:, 2 + gi, cb0:cb0 + 128], S_bf[:, gi * 512:(gi + 1) * 512],
                         [("BCT", 2 + gi), "S_bf"])
                for gi in range(2):
                    sl = slice(gi * 512, (gi + 1) * 512)
                    g.cp("act", y1[:, sl], bo[gi][1][:, :], [bo[gi][0]], [("y1", gi)])
                    g.tt(y1[:, sl].rearrange("p (h d) -> p h d", h=8), y1[:, sl].rearrange("p (h d) -> p h d", h=8),
                         eacum[:, gi * 8:(gi + 1) * 8].unsqueeze(2).to_broadcast([128, 8, 64]), ALU.mult,
                         [("y1", gi), "eacum"], [("y1", gi)])
                    g.tt(y1[:, sl], y1[:, sl], by[gi][1][:, :], ALU.add, [("y1", gi), by[gi][0]], [("y1", gi)])
                    g.tt(yg[:, sl], yg[:, sl], y1[:, sl], ALU.add, [("yg", gi), ("y1", gi)], [("yg", gi)])
                    g.tt(yg[:, sl], yg[:, sl], zs[:, sl], ALU.mult, [("yg", gi), ("zs", gi)], [("yg", gi)])
                bzn = z_mm(b + 1) if b + 1 < NBLK else None
                for gi in range(2):
                    sl = slice(gi * 512, (gi + 1) * 512)
                    g.act(junk2[:, gi, :], yg[:, sl], AF.Square, [("yg", gi)], [("junk", gi)])
                    g.op("dve", lambda e, gi=gi, ss2=ss2: e.tensor_reduce(out=ss2[:, gi:gi + 1], in_=junk2[:, gi, :], axis=AX.X, op=ALU.add),
                         reads=[("junk", gi)], writes=[("ss2", gi)])
                g.act(ln2[:, 0:2], ss2[:, 0:2], AF.Ln, [("ss2", 0), ("ss2", 1), "smallc"], ["ln2"], bias=eps_t, scale=1.0 / 512)
                g.act(rs2[:, 0:2], ln2[:, 0:2], AF.Exp, ["ln2"], ["rs2"], scale=-0.5)
                for gi in range(2):
                    sl = slice(gi * 512, (gi + 1) * 512)
                    g.stt(yn[:, sl], yg[:, sl], rs2[:, gi:gi + 1], rowc[:, R_SNW + gi * 512:R_SNW + (gi + 1) * 512],
                          ALU.mult, ALU.mult, [("yg", gi), "rs2", "rowc"], [("yn", gi)])
                if bzn is not None:
                    z_act(bzn)
                for c in range(8):
                    g.tr("psbf", psbf[:, c * 128:(c + 1) * 128], yn[:, c * 128:(c + 1) * 128], ident_b,
                         [("yn", c // 4), "cbf"], last=(c == 7))
                g.cp("act", ymixT[:, 0:8, cb0:cb0 + 128], psbf[:, :].rearrange("p (a b) -> p a b", a=8), ["psbf"],
                     [("ymix", c, "p", b) for c in range(8)])
                bn = [bank(), bank()]
                for gi in range(2):
                    g.mm(bn[gi][0], bn[gi][1][:, :], B_tm[:, gi, :], xdtd[:, gi * 512:(gi + 1) * 512],
                         ["B_tm", ("xdtd", gi)])
                for gi in range(2):
                    sl = slice(gi * 512, (gi + 1) * 512)
                    g.tt(S_f[:, sl].rearrange("p (h d) -> p h d", h=8), S_f[:, sl].rearrange("p (h d) -> p h d", h=8),
                         etot[:, gi * 8:(gi + 1) * 8].unsqueeze(2).to_broadcast([128, 8, 64]), ALU.mult,
                         ["S_f", "etot"], ["S_f"])
                    g.tt(S_f[:, sl], S_f[:, sl], bn[gi][1][:, :], ALU.add, ["S_f", bn[gi][0]], ["S_f"])
            g.barrier()

        def ssd_sample(mx):
            g.embed = CFG.get("embed_mix", False)
            Wz, Wdt, ymixT, xsT, BCs_f = mx["Wz"], mx["Wdt"], mx["ymixT"], mx["xsT"], mx["BCs_f"]
            ar.reset(mx["R2"])
            Sb = [ar.alloc([128, 8, 128], F32) for _ in range(3)]
            tmpb = ar.alloc([128, 8, 128], F32)
            tmp2 = ar.alloc([128, 8, 128], F32)
            BCbb = [ar.alloc([128, 512], F32) for _ in range(2)]
            dtE = ar.alloc([NS, 2, 1024], F32)
            dtF = ar.alloc([128, 2, 8, NS], F32)
            dtxF = ar.alloc([128, 8, NS], F32)
            ysT = ar.alloc([128, 8, NS], F32)
            zT = ar.alloc([128, 8, NS], F32)
            ygT = ar.alloc([128, 8, NS], F32)
            sqT = ar.alloc([128, 8, NS], F32)
            rsb = ar.alloc([128, 2, NS], F32)
            BC_tm = ar.alloc([NS, 4, 128], F32)
            sel = [ar.alloc([NS, 128], F32) for _ in range(2)]
            smt = ar.alloc([NS, 8, 16], F32)
            dtt, dta, dte, dtl, dtv, av, dA = [smt[:, i, :] for i in range(7)]
            sc0, sc1 = TT, TT + NS
            bd = bank()
            for k in range(KC):
                g.mm(bd[0], bd[1][0:NS, 0:16], hT[:, k, sc0:sc1], Wdt[:, k, :], [("hT", k), "Wdt"],
                     start=(k == 0), stop=(k == KC - 1))
            g.tt(dtt, bd[1][0:NS, 0:16], rowc[0:NS, R_DTB:R_DTB + 16], ALU.add, [bd[0], "rowc"], ["s_dtt"])
            g.act(dta, dtt, AF.Abs, ["s_dtt"], ["s_dta"])
            g.act(dte, dta, AF.Exp, ["s_dta"], ["s_dte"], scale=-1.0)
            g.act(dtl, dte, AF.Ln, ["s_dte", "smallc"], ["s_dtl"], bias=one_t[0:NS, :])
            g.stt(dtv, dtt, 0.0, dtl, ALU.max, ALU.add, ["s_dtt", "s_dtl"], ["s_dtv"])
            g.tt(av, dtv, A_bc[0:NS, :], ALU.mult, ["s_dtv", "smallc"], ["s_av"])
            g.act(dA, av, AF.Exp, ["s_av"], ["s_dA"])
            g.cp("dve", dtE[:, 0, :].rearrange("p (h d) -> p h d", h=16), dtv.unsqueeze(2).to_broadcast([NS, 16, 64]),
                 ["s_dtv"], [("dtE", 0)])
            g.cp("dve", dtE[:, 1, :].rearrange("p (h d) -> p h d", h=16), dA.unsqueeze(2).to_broadcast([NS, 16, 64]),
                 ["s_dA"], [("dtE", 1)])
            for w_ in range(2):
                for hf in range(2):
                    bk = bank()
                    for q in range(4):
                        c = hf * 4 + q
                        g.tr(bk[0], bk[1][:, q * NS:(q + 1) * NS], dtE[:, w_, c * 128:(c + 1) * 128], ident_f[0:NS, 0:NS],
                             [("dtE", w_), "cmat"], last=(q == 3))
                    g.cp("act", dtF[:, w_, hf * 4:(hf + 1) * 4, :], bk[1][:, 0:4 * NS].rearrange("p (q c) -> p q c", q=4),
                         [bk[0]], [("dtF", w_, hf)])
            dk = [("dtF", w_, hf) for w_ in range(2) for hf in range(2)]
            xk = [("xsT", c, "s") for c in range(8)]
            g.tt(dtxF[:, :, :], xsT[:, :, sc0:sc1], dtF[:, 0, :, :], ALU.mult, xk + dk, ["dtxF"])
            bk = bank()
            for i in range(4):
                g.tr(bk[0], bk[1][0:NS, i * 128:(i + 1) * 128], BCs_f[:, i, :], ident_f, [("BCs", i), "cmat"], last=(i == 3))
            g.cp("act", BC_tm[:, :, :], bk[1][0:NS, :].rearrange("p (a b) -> p a b", a=4), [bk[0]], ["BC_tm"])
            NSL = 3

            def s_load(b):
                spdma(f"ssmin{b % NSL}", Sb[b % NSL][:, :, :], ssm_in[b].rearrange("c q n -> q c n"), (), [("Sb", b % NSL)])

            def s_stage_a(b):
                sl_ = sel[b % 2]
                g.cp("dve", sl_[:, :], ident_f[0:NS, b:b + 1].to_broadcast([NS, 128]), ["cmat"], [("sel", b % 2)])
                bb = bank()
                g.mm(bb[0], bb[1][:, :], sl_[:, :], BC_tm[:, :, :].rearrange("p a b -> p (a b)"), [("sel", b % 2), "BC_tm"])
                g.cp("act", BCbb[b % 2][:, :], bb[1][:, :], [bb[0]], [("BCb", b % 2)])

            def s_stage_b(b):
                S = Sb[b % NSL]
                sk = ("Sb", b % NSL)
                BCb = BCbb[b % 2]
                g.tt(S[:, :, :], S[:, :, :], dtF[:, 1, :, b:b + 1].to_broadcast([128, 8, 128]), ALU.mult, [sk] + dk, [sk])
                Bv = BCb[:, 0:256].rearrange("p (g n) -> p g n", g=2).unsqueeze(2).to_broadcast([128, 2, 4, 128])
                Cv = BCb[:, 256:512].rearrange("p (g n) -> p g n", g=2).unsqueeze(2).to_broadcast([128, 2, 4, 128])
                g.tt(tmpb[:, :, :].rearrange("p (g q) n -> p g q n", g=2), Bv,
                     dtxF[:, :, b:b + 1].to_broadcast([128, 8, 128]).rearrange("p (g q) n -> p g q n", g=2), ALU.mult,
                     [("BCb", b % 2), "dtxF"], ["tmpb"])
                g.tt(S[:, :, :], S[:, :, :], tmpb[:, :, :], ALU.add, [sk, "tmpb"], [sk])
                spdma(f"ssmout{b % NSL}", o_ssm_s[b].rearrange("c q n -> q c n"), S[:, :, :], [sk], [("o_ssm_s", b)])
                g.tt(tmp2[:, :, :].rearrange("p (g q) n -> p g q n", g=2), Cv,
                     S[:, :, :].rearrange("p (g q) n -> p g q n", g=2), ALU.mult, [("BCb", b % 2), sk], ["tmp2"])
                g.op("dve", lambda e, b=b: e.tensor_reduce(out=ysT[:, :, b], in_=tmp2[:, :, :], axis=AX.X, op=ALU.add),
                     reads=["tmp2"], writes=[("ysT", b)])
                if b + NSL < NS:
                    s_load(b + NSL)

            for b in range(NSL):
                s_load(b)
            s_stage_a(0)
            for b in range(NS):
                if b + 1 < NS:
                    s_stage_a(b + 1)
                s_stage_b(b)
            yk = [("ysT", b) for b in range(NS)]
            g.tt(tmpb[:, 0, 0:8 * NS].rearrange("p (c b) -> p c b", c=8), xsT[:, :, sc0:sc1],
                 vecs[:, V_DFM:V_DFM + 8].unsqueeze(2).to_broadcast([128, 8, NS]), ALU.mult, xk + ["vecs", "tmpb"], ["tmpb"])
            g.tt(ysT[:, :, :], ysT[:, :, :], tmpb[:, 0, 0:8 * NS].rearrange("p (c b) -> p c b", c=8), ALU.add,
                 yk + ["tmpb"], ["ysTall"])
            for hf in range(2):
                bk = bank()
                for q in range(4):
                    c = hf * 4 + q
                    for k in range(KC):
                        g.mm(bk[0], bk[1][:, q * NS:(q + 1) * NS], Wz[:, k, c * 128:(c + 1) * 128], hT[:, k, sc0:sc1],
                             ["Wz", ("hT", k)], start=(k == 0), stop=(k == KC - 1))
                g.act(zT[:, hf * 4:(hf + 1) * 4, :], bk[1][:, 0:4 * NS].rearrange("p (q c) -> p q c", q=4), AF.Silu,
                      [bk[0]], [("zT", hf)])
            g.tt(ygT[:, :, :], ysT[:, :, :], zT[:, :, :], ALU.mult, ["ysTall", ("zT", 0), ("zT", 1)], ["ygT"])
            g.tt(sqT[:, :, :], ygT[:, :, :], ygT[:, :, :], ALU.mult, ["ygT"], ["sqT"])
            bk = bank()
            for gi in range(2):
                for q in range(4):
                    g.mm(bk[0], bk[1][:, gi * NS:(gi + 1) * NS], ones_f, sqT[:, gi * 4 + q, :], ["cmat", "sqT"],
                         start=(q == 0), stop=(q == 3))
            g.act(rsb[:, :, :], bk[1][:, 0:2 * NS].rearrange("p (g b) -> p g b", g=2), AF.Ln, [bk[0], "smallc"], ["rsb"],
                  bias=eps_t, scale=1.0 / 512)
            g.act(rsb[:, :, :], rsb[:, :, :], AF.Exp, ["rsb"], ["rsb"], scale=-0.5)
            for c in range(8):
                g.stt(ymixT[:, c, sc0:sc1], ygT[:, c, :], vecs[:, V_SNW + c:V_SNW + c + 1], rsb[:, c // 4, :],
                      ALU.mult, ALU.mult, ["ygT", "vecs", "rsb"], [("ymix", c, "s")])
            g.barrier()

        def out_proj(t, mx):
            g.embed = CFG.get("embed_ffn", True)
            ymixT = mx["ymixT"]
            segs = segs_of(t)
            for m in range(KC):
                i = wA_i[0] % 5
                wA_i[0] += 1
                key = ("wA", i)
                dst = wA[i][:, :, :].rearrange("p k c -> p (k c)")[:, 0:2048].rearrange("p (c m) -> p c m", c=16)
                src = wout_v[:, :, m * 128:(m + 1) * 128]
                g.dma("pool", lambda e, dst=dst, src=src: e.dma_start(out=dst, in_=src), (), [key], wA_s[i])
                for si, (c0, n) in enumerate(segs):
                    bo = bank()
                    for c in range(16):
                        if si == 1:
                            yk_ = [("ymix", c, "s")]
                        elif c >= 8:
                            yk_ = [("ymix", c, "p")]
                        else:
                            yk_ = [("ymix", c, "p", b_) for b_ in range(NBLK)]
                        g.mm(bo[0], bo[1][:, 0:n], dst[:, c, :], ymixT[:, c, c0:c0 + n], [key] + yk_, start=(c == 0), stop=(c == 15))
                    g.tt(xT[:, m, c0:c0 + n], bo[1][:, 0:n], xT[:, m, c0:c0 + n], ALU.add, [bo[0], ("xT", m)], [("xT", m)])
            if CFG.get("all_barriers", False):
                g.barrier()

        def state_outputs():
            g.embed = False
            ar.reset(0)
            Sout = ar.alloc([128, 8, 128], F32)
            crow = ar.alloc([3, 1536], F32)
            srow = ar.alloc([2, 1024], F32)
            for hf in range(2):
                bk = bank()
                for q in range(4):
                    c = hf * 4 + q
                    g.tr(bk[0], bk[1][:, q * 128:(q + 1) * 128], S_f[:, c * 128:(c + 1) * 128], ident_f, ["S_f", "cmat"],
                         last=(q == 3))
                g.cp("act", Sout[:, hf * 4:(hf + 1) * 4, :], bk[1][:, :].rearrange("p (q c) -> p q c", q=4), [bk[0]],
                     [("Sout", hf)])
            spdma("sout", o_ssm_p.rearrange("c q n -> q c n"), Sout[:, :, :], [("Sout", 0), ("Sout", 1)], ["o_ssm_p"])
            for c in range(12):
                bk = bank()
                g.tr(bk[0], bk[1][0:3, 0:128], hist[:, c, :], ident_f, [("hist", c), "cmat"])
                g.cp("act", crow[:, c * 128:(c + 1) * 128], bk[1][0:3, 0:128], [bk[0]], [("crow", c)])
            spdma("crow", o_cv_p[:, :], crow[:, :], [("crow", c) for c in range(12)], ["o_cv_p"])
            for c in range(8):
                bk = bank()
                g.tr(bk[0], bk[1][0:2, 0:128], uhist[:, c, :], ident_f, [("uhist", c), "cmat"])
                g.cp("act", srow[:, c * 128:(c + 1) * 128], bk[1][0:2, 0:128], [bk[0]], [("srow", c)])
            spdma("srow", o_sc_p[:, :], srow[:, :], [("srow", c) for c in range(8)], ["o_sc_p"])

        ph = [0]
        if CFG.get("wb_prefetch", True):
            prefetch_wB(0)

        def ok():
            ph[0] += 1
            return ph[0] <= CFG.get("max_phase", 10 ** 9)

        for t in CFG["tiles"]:
            if ok():
                load_tile(t)
            if CFG["ffn1"]:
                for _ in range(CFG.get("ffn_rep", 1)):
                    if ok():
                        ffn(t, 0, V_NW1)
            if CFG["mixer"]:
                if ok():
                    mx = mixer(t)
                if CFG["ssd"]:
                    if ok():
                        ssd_blocks(t, mx)
                    if t == NTILE - 1 and CFG["sample"]:
                        if ok():
                            ssd_sample(mx)
                if CFG["outproj"]:
                    if ok():
                        out_proj(t, mx)
            if CFG["ffn2"]:
                for _ in range(CFG.get("ffn_rep", 1)):
                    if ok():
                        ffn(t, 1, V_NW2)
                if CFG.get("wb_prefetch", True) and t != CFG["tiles"][-1]:
                    prefetch_wB(0)
            if CFG.get("store", True):
                if ok():
                    store_tile(t)
        if CFG["stateout"] and ok():
            state_outputs()
        g.barrier(engines=("sp",))

        with nc.Block() as block:
            @block.tensor
            def _(e):
                g.replay("pe", e)

            @block.scalar
            def _(e):
                g.replay("act", e)

            @block.vector
            def _(e):
                g.replay("dve", e)

            @block.gpsimd
            def _(e):
                g.replay("pool", e)

            @block.sync
            def _(e):
                g.replay("sp", e)
    return nc


_NC_CACHE = {}


def _host_layout(inp):
    f = np.float32
    def fm(v, n):
        return np.ascontiguousarray(np.asarray(v, f).reshape(n, 128).T)
    vecs = np.zeros((128, NV), f)
    vecs[:, V_NW1:V_NW1 + 8] = fm(inp["norm_ffn1_w"][0], 8)
    vecs[:, V_NWM:V_NWM + 8] = fm(inp["norm_mix_w"][0], 8)
    vecs[:, V_NW2:V_NW2 + 8] = fm(inp["norm_ffn2_w"][0], 8)
    vecs[:, V_NWF:V_NWF + 8] = fm(inp["final_norm_w"], 8)
    cw = np.asarray(inp["ssd_conv_w"][0], f)
    vecs[:, V_CW:V_CW + 48] = cw.reshape(4, 12, 128).transpose(2, 1, 0).reshape(128, 48)
    vecs[:, V_CB:V_CB + 12] = fm(inp["ssd_conv_b"][0], 12)
    sw = np.asarray(inp["sconv_w"][0], f)
    vecs[:, V_SW:V_SW + 24] = sw.reshape(3, 8, 128).transpose(2, 1, 0).reshape(128, 24)
    dsk = np.repeat(np.asarray(inp["d_skip"][0], f), 64)
    vecs[:, V_DFM:V_DFM + 8] = fm(dsk, 8)
    vecs[:, V_SNW:V_SNW + 8] = fm(inp["ssd_norm_w"][0], 8)
    rowc = np.zeros((128, NR), f)
    rowc[:, R_D:R_D + 1024] = dsk[None, :]
    rowc[:, R_SNW:R_SNW + 1024] = np.asarray(inp["ssd_norm_w"][0], f)[None, :]
    rowc[:, R_ALOG:R_ALOG + 16] = np.asarray(inp["a_log"][0], f)[None, :]
    rowc[:, R_DTB:R_DTB + 16] = np.asarray(inp["dt_bias"][0], f)[None, :]
    rowc[:, R_ALOG4:R_ALOG4 + 64] = np.tile(np.asarray(inp["a_log"][0], f), 4)[None, :]
    rowc[:, R_DTB4:R_DTB4 + 64] = np.tile(np.asarray(inp["dt_bias"][0], f), 4)[None, :]
    cm = np.zeros((128, 512), f)
    i = np.arange(128)
    cm[:, 0:128] = np.eye(128, dtype=f)
    cm[:, 128:256] = (i[:, None] <= i[None, :]).astype(f)
    cm[:, 256:384] = (i[:, None] > i[None, :]).astype(f)
    cm[:, 384:512] = 1.0
    return vecs, rowc, cm


def kernel(**inp):
    f = np.float32
    if "nc" not in _NC_CACHE:
        _NC_CACHE["nc"] = build_program()
    nc = _NC_CACHE["nc"]
    vecs, rowc, cm = _host_layout(inp)
    shared = {
        "wg1": np.ascontiguousarray(inp["ffn1_w_gate"][0], f), "wu1": np.ascontiguousarray(inp["ffn1_w_up"][0], f),
        "wd1": np.ascontiguousarray(inp["ffn1_w_down"][0], f),
        "wg2": np.ascontiguousarray(inp["ffn2_w_gate"][0], f), "wu2": np.ascontiguousarray(inp["ffn2_w_up"][0], f),
        "wd2": np.ascontiguousarray(inp["ffn2_w_down"][0], f),
        "win": np.ascontiguousarray(inp["w_in"][0], f), "wout": np.ascontiguousarray(inp["w_out"][0], f),
        "vecs": vecs, "rowc": rowc, "cmat": cm,
    }
    in_maps = []
    for c in range(NCORES):
        m = dict(shared)
        sl = slice(c * NS, (c + 1) * NS)
        m["xp"] = np.ascontiguousarray(inp["x_prompt"][c], f)
        m["xsm"] = np.ascontiguousarray(inp["x_sample"][sl, 0, :], f)
        m["ssm_in"] = np.ascontiguousarray(inp["state_ssm"][0, sl], f).reshape(NS, 8, 128, 128)
        m["cst_in"] = np.ascontiguousarray(inp["state_ssd_conv"][0, sl], f)
        m["sst_in"] = np.ascontiguousarray(inp["state_sconv"][0, sl], f)
        in_maps.append(m)
    res = run_bass_kernel_spmd(nc, in_maps, core_ids=list(range(NCORES)))
    R = res.results
    y_prompt = np.stack([R[c]["yp"] for c in range(NCORES)], 0).astype(f)
    y_sample = np.concatenate([R[c]["ysm"] for c in range(NCORES)], 0).reshape(NCORES * NS, 1, D).astype(f)
    ssm_p = np.stack([R[c]["o_ssm_p"].reshape(16, 64, 128) for c in range(NCORES)], 0)[None].astype(f)
    cv_p = np.stack([R[c]["o_cv_p"] for c in range(NCORES)], 0)[None].astype(f)
    sc_p = np.stack([R[c]["o_sc_p"] for c in range(NCORES)], 0)[None].astype(f)
    ssm_s = np.concatenate([R[c]["o_ssm_s"].reshape(NS, 16, 64, 128) for c in range(NCORES)], 0)[None].astype(f)
    cv_s = np.concatenate([R[c]["o_cv_s"] for c in range(NCORES)], 0)[None].astype(f)
    sc_s = np.concatenate([R[c]["o_sc_s"] for c in range(NCORES)], 0)[None].astype(f)
    return (y_prompt, y_sample, ssm_p, cv_p, sc_p, ssm_s, cv_s, sc_s)
```

```python
import numpy as np
import concourse.bass as bass
import concourse.mybir as mybir
from concourse.bass_utils import run_bass_kernel_spmd

F32 = mybir.dt.float32
BF16 = mybir.dt.bfloat16
AF = mybir.ActivationFunctionType
ALU = mybir.AluOpType
AX = mybir.AxisListType

NCORES = 8
D = 1024
KC = 8
DFF = 2816
NJ = 22
DIN = 5648
SEQ = 2048
TT = 512
NTILE = 4
NBLK = 4
NS = 16
EPS = 1e-6
O_Z, O_XBC, O_DT, O_SCB, O_SCC, O_SCH = 0, 1024, 2560, 2576, 3600, 4624

V_NW1, V_NWM, V_NW2, V_NWF = 0, 8, 16, 24
V_CW = 32
V_CB = 80
V_SW = 92
V_DFM = 116
V_SNW = 124
NV = 132
R_D, R_SNW, R_ALOG, R_DTB = 0, 1024, 2048, 2064
R_ALOG4, R_DTB4 = 2080, 2144
NR = 2208


CFG = dict(tiles=[0, 1, 2, 3], ffn1=True, mixer=True, ssd=True, sample=True, outproj=True, ffn2=True, stateout=True)


class Sem:
    def __init__(self, h):
        self.h = h
        self.n = 0


class Prog:
    def __init__(self, nc, sem_handles):
        self.nc = nc
        self.free_sems = list(sem_handles)
        self.all_sems = []
        self.q = {k: [] for k in ("pe", "act", "dve", "pool", "sp")}
        self.esem = {k: self.new_sem() for k in ("pe", "act", "dve")}
        self.seen = {k: {} for k in self.q}
        self.w = {}
        self.r = {}
        self.named = {}
        self.embed = False

    def new_sem(self):
        s = Sem(self.free_sems.pop())
        self.all_sems.append(s)
        return s

    def stream(self, name):
        if name not in self.named:
            self.named[name] = self.new_sem()
        return self.named[name]

    def _waits(self, eng, reads, writes):
        deps = {}
        for k in reads:
            for (s, v) in self.w.get(k, ()):
                deps[s] = max(deps.get(s, 0), v)
        for k in writes:
            for (s, v) in self.w.get(k, ()):
                deps[s] = max(deps.get(s, 0), v)
            for (s, v) in self.r.get(k, ()):
                deps[s] = max(deps.get(s, 0), v)
        waits = []
        for s, v in deps.items():
            if eng == "pe" and s is self.esem["pe"]:
                continue
            if self.seen[eng].get(s, 0) >= v:
                continue
            self.seen[eng][s] = v
            waits.append((s, v))
        return waits

    def _mark(self, t, reads, writes):
        for k in reads:
            self.r.setdefault(k, []).append(t)
        for k in writes:
            self.w[k] = [t]
            self.r[k] = []

    def op(self, eng, fn, reads=(), writes=(), inc=True):
        waits = self._waits(eng, reads, writes)
        s = self.esem[eng]
        if inc:
            s.n += 1
            t = (s, s.n)
            self.q[eng].append((waits, fn, s, 1, self.embed))
        else:
            t = (s, s.n + 1)
            self.q[eng].append((waits, fn, None, 0, self.embed))
        self._mark(t, reads, writes)

    def dma(self, queue, fn, reads, writes, stream):
        waits = self._waits(queue, reads, writes)
        stream.n += 16
        t = (stream, stream.n)
        self.q[queue].append((waits, fn, stream, 16, self.embed))
        self._mark(t, reads, writes)

    def sync(self, eng, keys):
        waits = self._waits(eng, (), keys)
        if waits:
            self.q[eng].append((waits, None, None, 0, False))

    def barrier(self, engines=("pe", "act", "dve", "sp")):
        for e in engines:
            waits = []
            for s in self.all_sems:
                if e == "pe" and s is self.esem["pe"]:
                    continue
                if s.n > self.seen[e].get(s, 0):
                    self.seen[e][s] = s.n
                    waits.append((s, s.n))
            if waits:
                self.q[e].append((waits, None, None, 0, False))

    def replay(self, eng, e):
        for (waits, fn, s, inc, emb) in self.q[eng]:
            if fn is None or not emb:
                for (ws, wv) in waits:
                    e.wait_ge(ws.h, wv)
                if fn is None:
                    continue
                ins = fn(e)
                if s is not None:
                    ins.then_inc(s.h, inc)
                continue
            for (ws, wv) in waits[:-1]:
                e.wait_ge(ws.h, wv)
            ins = fn(e)
            if waits:
                ins._wait_ge(waits[-1][0].h, waits[-1][1])
            if s is not None:
                ins.then_inc(s.h, inc)

    def mm(self, okey, out, lhsT, rhs, reads, start=True, stop=True, force_inc=False):
        self.op("pe", lambda e: e.matmul(out, lhsT, rhs, start=start, stop=stop),
                reads=reads, writes=[okey], inc=(stop or force_inc))

    def tr(self, okey, out, in_, ident, reads, last=True):
        self.op("pe", lambda e: e.transpose(out, in_, ident), reads=reads, writes=[okey], inc=last)

    def act(self, out, in_, func, reads, writes, bias=None, scale=None, accum=None):
        kw = {}
        if bias is not None:
            kw["bias"] = bias
        if scale is not None:
            kw["scale"] = scale
        if accum is not None:
            kw["accum_out"] = accum
        self.op("act", lambda e: e.activation(out=out, in_=in_, func=func, **kw), reads=reads, writes=writes)

    def tt(self, out, a, b, op, reads, writes):
        self.op("dve", lambda e: e.tensor_tensor(out=out, in0=a, in1=b, op=op), reads=reads, writes=writes)

    def ts(self, out, a, s1, op0, reads, writes, s2=None, op1=None):
        if op1 is None:
            self.op("dve", lambda e: e.tensor_scalar(out, a, s1, None, op0), reads=reads, writes=writes)
        else:
            self.op("dve", lambda e: e.tensor_scalar(out, a, s1, s2, op0, op1), reads=reads, writes=writes)

    def stt(self, out, in0, scalar, in1, op0, op1, reads, writes):
        self.op("dve", lambda e: e.scalar_tensor_tensor(out=out, in0=in0, scalar=scalar, in1=in1, op0=op0, op1=op1),
                reads=reads, writes=writes)

    def cp(self, eng, out, in_, reads, writes):
        if eng == "act":
            self.op("act", lambda e: e.copy(out=out, in_=in_), reads=reads, writes=writes)
        else:
            self.op("dve", lambda e: e.tensor_copy(out=out, in_=in_), reads=reads, writes=writes)

    def memset(self, ap, val, writes):
        self.op("dve", lambda e: e.memset(ap, val), reads=(), writes=writes)


class Arena:
    def __init__(self, t, nwords):
        self.t = t
        self.nwords = nwords
        self.off = 0

    def reset(self, off=0):
        self.off = off

    def alloc(self, shape, dtype, parts=128):
        n = 1
        for s in shape[1:]:
            n *= s
        words = n if dtype == F32 else (n + 1) // 2
        assert self.off + words <= self.nwords, ("arena overflow", self.off, words, self.nwords)
        ap = self.t[0:shape[0], self.off:self.off + words]
        self.off += words
        if dtype != F32:
            ap = ap.bitcast(dtype)[:, 0:n]
        if len(shape) == 3:
            ap = ap.rearrange("p (a b) -> p a b", a=shape[1])
        elif len(shape) == 4:
            ap = ap.rearrange("p (a b c) -> p a b c", a=shape[1], b=shape[2])
        return ap


def build_program():
    nc = bass.Bass("TRN2", target_bir_lowering=False)

    def din(name, shape):
        return nc.dram_tensor(name, list(shape), F32, kind="ExternalInput").ap()

    def dout(name, shape):
        return nc.dram_tensor(name, list(shape), F32, kind="ExternalOutput").ap()

    xp = din("xp", [SEQ, D])
    xsm = din("xsm", [NS, D])
    ssm_in = din("ssm_in", [NS, 8, 128, 128])
    cst_in = din("cst_in", [NS, 3, 1536])
    sst_in = din("sst_in", [NS, 2, 1024])
    wg = [din("wg1", [D, DFF]), din("wg2", [D, DFF])]
    wu = [din("wu1", [D, DFF]), din("wu2", [D, DFF])]
    wd = [din("wd1", [DFF, D]), din("wd2", [DFF, D])]
    win = din("win", [D, DIN])
    wout = din("wout", [2 * D, D])
    vecs_d = din("vecs", [128, NV])
    rowc_d = din("rowc", [128, NR])
    cmat_d = din("cmat", [128, 512])

    yp = dout("yp", [SEQ, D])
    ysm = dout("ysm", [NS, D])
    o_ssm_p = dout("o_ssm_p", [8, 128, 128])
    o_cv_p = dout("o_cv_p", [3, 1536])
    o_sc_p = dout("o_sc_p", [2, 1024])
    o_ssm_s = dout("o_ssm_s", [NS, 8, 128, 128])
    o_cv_s = dout("o_cv_s", [NS, 3, 1536])
    o_sc_s = dout("o_sc_s", [NS, 2, 1024])

    wg_v = [w.rearrange("(k p) n -> p k n", p=128) for w in wg]
    wu_v = [w.rearrange("(k p) n -> p k n", p=128) for w in wu]
    wd_v = [w.rearrange("(j p) m -> p j m", p=128) for w in wd]
    win_v = win.rearrange("(k p) n -> p k n", p=128)
    wout_v = wout.rearrange("(c p) m -> p c m", p=128)

    NCOL = TT + NS
    AW = 23800

    from contextlib import ExitStack
    with ExitStack() as es:
        def sb(name, shape, dt):
            return es.enter_context(nc.sbuf_tensor("sb_" + name, list(shape), dt))

        def ps(name, shape, dt):
            return es.enter_context(nc.psum_tensor("ps_" + name, list(shape), dt))

        sems = [es.enter_context(nc.semaphore(f"s{i}")) for i in range(64)]
        g = Prog(nc, sems)

        xT = sb("xT", [128, KC, NCOL], F32)
        hT = sb("hT", [128, KC, NCOL], BF16)
        rstd = sb("rstd", [128, NCOL], F32)
        lnv = sb("lnv", [128, NCOL], F32)
        sqb = [sb(f"sq{i}", [128, NCOL], BF16) for i in range(2)]
        sgb = [sb(f"sg{i}", [128, TT], F32) for i in range(2)]
        wA = [sb(f"wA{i}", [128, KC, 256], BF16) for i in range(5)]
        wB = [sb(f"wB{i}", [128, NJ, 128], BF16) for i in range(3)]
        cmat = sb("cmat", [128, 512], F32)
        cbf = sb("cbf", [128, 256], BF16)
        vecs = sb("vecs", [128, NV], F32)
        rowc = sb("rowc", [128, NR], F32)
        smallc = sb("smallc", [128, 256], F32)
        S_f = sb("S_f", [128, 1024], F32)
        S_bf = sb("S_bf", [128, 1024], BF16)
        hist = sb("hist", [128, 12, 3], F32)
        uhist = sb("uhist", [128, 8, 2], F32)
        Wz_t = sb("Wz", [128, KC, 1024], BF16)
        Wdt_t = sb("Wdt", [128, KC, 16], BF16)
        arena_t = sb("arena", [128, AW], F32)
        ar = Arena(arena_t, AW)

        banks = [ps(f"pb{i}", [128, 512], F32) for i in range(7)]
        psbf = ps("psbf", [128, 1024], BF16)
        bank_i = [0]

        def bank():
            i = bank_i[0] % 7
            bank_i[0] += 1
            return ("pb", i), banks[i]

        ident_f = cmat[:, 0:128]
        U_f = cmat[:, 128:256]
        Ms_f = cmat[:, 256:384]
        ones_f = cmat[:, 384:512]
        ident_b = cbf[:, 0:128]
        ones_b = cbf[:, 128:256]
        eps_t = smallc[:, 0:1]
        one_t = smallc[:, 1:2]
        A_bc4 = smallc[:, 64:128]
        A_bc = smallc[:, 64:80]

        R2_OFF = (16 * NCOL) // 2 + KC * NCOL + (4 * NCOL) // 2 + 4 * NS + 2 * (3 + TT) + 2 * (2 + TT) + 2 * TT
        wA_i = [0]
        wB_i = [0]
        wz_loaded = [False]
        sg_i = [0]
        wA_s = [g.stream(f"wA{i}") for i in range(5)]
        wB_s = [g.stream(f"wB{i}") for i in range(3)]

        def wloadA(src_ap, ncols):
            i = wA_i[0] % 5
            wA_i[0] += 1
            key = ("wA", i)
            dst = wA[i][:, :, 0:ncols]
            if CFG.get("dma_skip", False) and (wA_i[0] % 2 == 1):
                return key, wA[i]
            g.dma("pool", lambda e: e.dma_start(out=dst, in_=src_ap), reads=(), writes=[key], stream=wA_s[i])
            return key, wA[i]

        def wloadB(src_ap, nrow):
            i = wB_i[0] % 3
            wB_i[0] += 1
            key = ("wB", i)
            dst = wB[i][:, 0:nrow, :]
            g.dma("pool", lambda e: e.dma_start(out=dst, in_=src_ap), reads=(), writes=[key], stream=wB_s[i])
            return key, wB[i]

        wB_pref = {0: [], 1: []}

        def prefetch_wB(which, n=3):
            assert not wB_pref[which]
            for m in range(n):
                wB_pref[which].append(wloadB(wd_v[which][:, :, m * 128:(m + 1) * 128], NJ))

        def spdma(name, out, in_, reads, writes):
            g.dma("sp", lambda e: e.dma_start(out=out, in_=in_), reads=reads, writes=writes, stream=g.stream(name))

        spdma("c0", cmat[:, :], cmat_d[:, :], (), ["cmat"])
        spdma("c1", vecs[:, :], vecs_d[:, :], (), ["vecs"])
        spdma("c2", rowc[:, :], rowc_d[:, :], (), ["rowc"])
        g.cp("dve", cbf[:, 0:128], cmat[:, 0:128], ["cmat"], ["cbf"])
        g.cp("dve", cbf[:, 128:256], cmat[:, 384:512], ["cmat"], ["cbf"])
        g.memset(smallc[:, 0:1], EPS, ["smallc"])
        g.memset(smallc[:, 1:2], 1.0, ["smallc"])
        g.act(smallc[:, 128:192], rowc[:, R_ALOG4:R_ALOG4 + 64], AF.Exp, ["rowc", "smallc"], ["smallc"])
        g.ts(A_bc4, smallc[:, 128:192], -1.0, ALU.mult, ["smallc"], ["smallc"])
        g.memset(S_f[:, :], 0.0, ["S_f"])
        g.memset(S_bf[:, :], 0.0, ["S_bf"])
        g.memset(hist[:, :, :], 0.0, [("hist", c) for c in range(12)])
        g.memset(uhist[:, :, :], 0.0, [("uhist", c) for c in range(8)])
        CONSTK = ["cmat", "cbf", "vecs", "rowc", "smallc"]

        def segs_of(t):
            return [(0, TT)] + ([(TT, NS)] if t == NTILE - 1 else [])

        def norm_rstd(segs, voff):
            ncol = segs[-1][0] + segs[-1][1]
            pbs = [bank() for _ in segs]
            for k in range(KC):
                sq = sqb[k % 2]
                g.act(sq[:, 0:ncol], xT[:, k, 0:ncol], AF.Square, [("xT", k)], [("sq", k % 2)])
                for si, (c0, n) in enumerate(segs):
                    g.mm(pbs[si][0], pbs[si][1][:, 0:n], ones_b, sq[:, c0:c0 + n],
                         [("sq", k % 2), "cbf"], start=(k == 0), stop=(k == KC - 1), force_inc=True)
            for si, (c0, n) in enumerate(segs):
                g.act(lnv[:, c0:c0 + n], pbs[si][1][:, 0:n], AF.Ln, [pbs[si][0], "smallc"], [("lnv", si)],
                      bias=eps_t, scale=1.0 / D)
                g.act(rstd[:, c0:c0 + n], lnv[:, c0:c0 + n], AF.Exp, [("lnv", si)], [("rstd", si)], scale=-0.5)

        def norm_to_hT(segs, voff):
            norm_rstd(segs, voff)
            ncol = segs[-1][0] + segs[-1][1]
            rk = [("rstd", si) for si in range(len(segs))]
            for k in range(KC):
                g.stt(hT[:, k, 0:ncol], xT[:, k, 0:ncol], vecs[:, voff + k:voff + k + 1], rstd[:, 0:ncol],
                      ALU.mult, ALU.mult, [("xT", k), "vecs"] + rk, [("hT", k)])

        def ffn(t, which, voff):
            g.embed = CFG.get("embed_ffn", True)
            segs = segs_of(t)
            ncol = segs[-1][0] + segs[-1][1]
            ar.reset(R2_OFF)
            acts = ar.alloc([128, NJ, NCOL], BF16)
            norm_to_hT(segs, voff)
            hk = [("hT", k) for k in range(KC)]
            g.sync("dve", [("io", 0), ("io", 1), ("io", 2, 0), ("io", 2, 1), ("io", 3, 0), ("io", 3, 1)])
            for s in range(NJ // 2):
                kg, tg = wloadA(wg_v[which][:, :, s * 256:(s + 1) * 256], 256)
                ku, tu = wloadA(wu_v[which][:, :, s * 256:(s + 1) * 256], 256)
                for jj in range(2):
                    j = 2 * s + jj
                    for si, (c0, n) in enumerate(segs):
                        bg = bank()
                        bu = bank()
                        for k in range(KC):
                            g.mm(bg[0], bg[1][:, 0:n], tg[:, k, jj * 128:(jj + 1) * 128], hT[:, k, c0:c0 + n],
                                 [kg, ("hT", k)], start=(k == 0), stop=(k == KC - 1))
                        for k in range(KC):
                            if CFG.get("skip_up", False) and k < KC - 1:
                                continue
                            g.mm(bu[0], bu[1][:, 0:n], tu[:, k, jj * 128:(jj + 1) * 128], hT[:, k, c0:c0 + n],
                                 [ku, ("hT", k)], start=(k == 0 or CFG.get("skip_up", False)), stop=(k == KC - 1))
                        sg_i[0] += 1
                        sgi = sg_i[0] % 2
                        g.act(sgb[sgi][:, 0:n], bg[1][:, 0:n], AF.Silu, [bg[0]], [("sg", sgi)])
                        g.tt(acts[:, j, c0:c0 + n], sgb[sgi][:, 0:n], bu[1][:, 0:n], ALU.mult,
                             [("sg", sgi), bu[0]], [("acts", j, si)])
            g.act(smallc[:, 200:201], one_t, AF.Exp, ["smallc"], ["tblwarm2"])
            for m in range(KC):
                if wB_pref[which]:
                    kd, td = wB_pref[which].pop(0)
                else:
                    kd, td = wloadB(wd_v[which][:, :, m * 128:(m + 1) * 128], NJ)
                for si, (c0, n) in enumerate(segs):
                    bo = bank()
                    for j in range(NJ):
                        g.mm(bo[0], bo[1][:, 0:n], td[:, j, :], acts[:, j, c0:c0 + n],
                             [kd, ("acts", j, si)], start=(j == 0), stop=(j == NJ - 1))
                    g.stt(xT[:, m, c0:c0 + n], bo[1][:, 0:n], 0.5, xT[:, m, c0:c0 + n], ALU.mult, ALU.add,
                          [bo[0], ("xT", m)], [("xT", m)])
            if (which == 0 and t == NTILE - 1) or CFG.get("all_barriers", False):
                g.barrier()

        def load_tile(t):
            g.embed = CFG.get("embed_io", True)
            ar.reset(R2_OFF)
            xrow = [ar.alloc([128, D], F32) for _ in range(2)]
            for b in range(NBLK):
                r0 = t * TT + b * 128
                xr = xrow[b % 2]
                spdma(f"xrow{b % 2}", xr[:, :], xp[r0:r0 + 128, :], (), [("io", b % 2)])
                for hf in range(0 if (CFG.get("load_dma_only", False) and t == 3) else 2):
                    bk = bank()
                    for q in range(4):
                        k = hf * 4 + q
                        g.tr(bk[0], bk[1][:, q * 128:(q + 1) * 128], xr[:, k * 128:(k + 1) * 128], ident_f,
                             [("io", b % 2), "cmat"], last=(q == 3))
                    g.cp("act", xT[:, hf * 4:(hf + 1) * 4, b * 128:(b + 1) * 128],
                         bk[1][:, :].rearrange("p (q c) -> p q c", q=4), [bk[0]],
                         [("xT", hf * 4 + q) for q in range(4)])
            if t == NTILE - 1 and not CFG.get("skip_xs", False):
                xsr = ar.alloc([NS, D], F32)
                spdma("xsr", xsr[:, :], xsm[:, :], (), [("io", 2, 0), ("io", 2, 1)])
                for hf in range(2):
                    bk = bank()
                    for q in range(4):
                        k = hf * 4 + q
                        g.tr(bk[0], bk[1][:, q * NS:(q + 1) * NS], xsr[:, k * 128:(k + 1) * 128], ident_f[0:NS, 0:NS],
                             [("io", 2, 0), ("io", 2, 1), "cmat"], last=(q == 3))
                    g.cp("act", xT[:, hf * 4:(hf + 1) * 4, TT:TT + NS],
                         bk[1][:, 0:4 * NS].rearrange("p (q c) -> p q c", q=4), [bk[0]],
                         [("xT", hf * 4 + q) for q in range(4)])
            if CFG.get("all_barriers", False):
                g.barrier()

        def store_tile(t):
            g.embed = CFG.get("embed_io", True)
            segs = segs_of(t)
            ar.reset(R2_OFF)
            yTb = [ar.alloc([128, KC, 128], F32) for _ in range(2)]
            yrow = [ar.alloc([128, D], F32) for _ in range(2)]
            if CFG.get("st_norm", True):
                norm_rstd(segs, V_NWF)
            rk = [("rstd", si) for si in range(len(segs))]
            nblk = NBLK + (1 if t == NTILE - 1 else 0)
            for b in range(nblk):
                c0 = b * 128
                n = 128 if b < NBLK else NS
                yt = yTb[b % 2]
                yr = yrow[b % 2]
                for k in range(KC):
                    g.stt(yt[:, k, 0:n], xT[:, k, c0:c0 + n], vecs[:, V_NWF + k:V_NWF + k + 1], rstd[:, c0:c0 + n],
                          ALU.mult, ALU.mult, [("xT", k), "vecs"] + rk, [("io", b % 2)])
                for hf in range(2 if CFG.get("st_tr", True) else 0):
                    bk = bank()
                    for q in range(4):
                        k = hf * 4 + q
                        g.tr(bk[0], bk[1][0:n, q * 128:(q + 1) * 128], yt[:, k, 0:n], ident_f,
                             [("io", b % 2), "cmat"], last=(q == 3))
                    g.cp("act", yr[0:n, hf * 512:(hf + 1) * 512], bk[1][0:n, :], [bk[0]], [("io", 2 + b % 2, hf)])
                if not CFG.get("st_dma", True):
                    continue
                if b < NBLK:
                    r0 = t * TT + b * 128
                    spdma(f"yst{b % 2}", yp[r0:r0 + 128, :], yr[:, :], [("io", 2 + b % 2, 0), ("io", 2 + b % 2, 1)],
                          [("ypd", t, b)])
                else:
                    spdma(f"yst{b % 2}", ysm[:, :], yr[0:NS, :], [("io", 2 + b % 2, 0), ("io", 2 + b % 2, 1)],
                          [("ysd",)])
            if CFG.get("all_barriers", False):
                g.barrier()

        def mixer(t):
            g.embed = CFG.get("embed_mix", False)
            segs = segs_of(t)
            last = (t == NTILE - 1)
            ncol = segs[-1][0] + segs[-1][1]
            norm_to_hT(segs, V_NWM)
            hk = [("hT", k) for k in range(KC)]
            ar.reset(0)
            Wz = Wz_t[:, :, :]
            Wdt = Wdt_t[:, :, :]
            ymixT = ar.alloc([128, 16, NCOL], BF16)
            xsT = ar.alloc([128, KC, NCOL], F32)
            BCT = ar.alloc([128, 4, NCOL], BF16)
            BCs_f = ar.alloc([128, 4, NS], F32)
            rawb = [ar.alloc([128, 3 + TT], F32) for _ in range(2)]
            ubuf = [ar.alloc([128, 2 + TT], F32) for _ in range(2)]
            accb = [ar.alloc([128, TT], F32) for _ in range(2)]
            R2 = ar.off
            assert R2 == R2_OFF, (R2, R2_OFF)

            if not wz_loaded[0]:
                wz_loaded[0] = True
                g.dma("pool", lambda e: e.dma_start(out=Wz, in_=win_v[:, :, O_Z:O_Z + 1024]), (), ["Wz"], g.stream("Wz"))
                g.dma("pool", lambda e: e.dma_start(out=Wdt, in_=win_v[:, :, O_DT:O_DT + 16]), (), ["Wdt"], g.stream("Wdt"))

            if last:
                strow = ar.alloc([48, 1536], F32)
                scrow = ar.alloc([32, 1024], F32)
                hs = ar.alloc([128, 12, 48], F32)
                us = ar.alloc([128, 8, 32], F32)
                smp = ar.alloc([128, 8, NS], F32)
                smp2 = ar.alloc([128, 8, NS], F32)
                cvrow = ar.alloc([NS, 1536], F32)
                urow = ar.alloc([NS, 1024], F32)
                spdma("strow", strow[:, :], cst_in.rearrange("b k c -> (b k) c"), (), ["strow"])
                spdma("scrow", scrow[:, :], sst_in.rearrange("b k c -> (b k) c"), (), ["scrow"])
                for c in range(12):
                    bk = bank()
                    g.tr(bk[0], bk[1][:, 0:48], strow[:, c * 128:(c + 1) * 128], ident_f[0:48, 0:48], ["strow", "cmat"])
                    g.cp("act", hs[:, c, :], bk[1][:, 0:48], [bk[0]], [("hs", c)])
                for c in range(8):
                    bk = bank()
                    g.tr(bk[0], bk[1][:, 0:32], scrow[:, c * 128:(c + 1) * 128], ident_f[0:32, 0:32], ["scrow", "cmat"])
                    g.cp("act", us[:, c, :], bk[1][:, 0:32], [bk[0]], [("us", c)])
                spdma("cvpass", o_cv_s[:, 0:2, :], cst_in[:, 1:3, :], (), ["o_cv_s01"])
                spdma("scpass", o_sc_s[:, 0:1, :], sst_in[:, 1:2, :], (), ["o_sc_s0"])

            for qd in range(4):
                kb, tb = wloadA(win_v[:, :, O_SCB + qd * 256:O_SCB + (qd + 1) * 256], 256)
                kc_, tc_ = wloadA(win_v[:, :, O_SCC + qd * 256:O_SCC + (qd + 1) * 256], 256)
                kh, th = wloadA(win_v[:, :, O_SCH + qd * 256:O_SCH + (qd + 1) * 256], 256)
                for jj in range(2):
                    c = qd * 2 + jj
                    for si, (c0, n) in enumerate(segs):
                        pb_, pc_, ph_ = bank(), bank(), bank()
                        for (pk, tw, kw) in ((pb_, tb, kb), (pc_, tc_, kc_), (ph_, th, kh)):
                            for k in range(KC):
                                g.mm(pk[0], pk[1][:, 0:n], tw[:, k, jj * 128:(jj + 1) * 128], hT[:, k, c0:c0 + n],
                                     [kw, ("hT", k)], start=(k == 0), stop=(k == KC - 1))
                        sg_i[0] += 1
                        sgi = sg_i[0] % 2
                        ct = sgb[sgi]
                        g.cp("act", ct[:, 0:n], pc_[1][:, 0:n], [pc_[0]], [("sg", sgi)])
                        w0 = vecs[:, V_SW + c * 3 + 0:V_SW + c * 3 + 1]
                        w1 = vecs[:, V_SW + c * 3 + 1:V_SW + c * 3 + 2]
                        w2 = vecs[:, V_SW + c * 3 + 2:V_SW + c * 3 + 3]
                        if si == 0:
                            ub = ubuf[c % 2]
                            uk = ("ubuf", c % 2)
                            g.cp("dve", ub[:, 0:2], uhist[:, c, :], [("uhist", c)], [uk])
                            g.tt(ub[:, 2:2 + TT], ct[:, 0:TT], ph_[1][:, 0:TT], ALU.mult, [("sg", sgi), ph_[0]], [uk])
                            ac = accb[c % 2]
                            ak = ("accb", c % 2)
                            g.ts(ac[:, :], ub[:, 0:TT], w0, ALU.mult, [uk, "vecs"], [ak])
                            g.stt(ac[:, :], ub[:, 1:1 + TT], w1, ac[:, :], ALU.mult, ALU.add, [uk, ak], [ak])
                            g.stt(ac[:, :], ub[:, 2:2 + TT], w2, ac[:, :], ALU.mult, ALU.add, [uk, ak], [ak])
                            g.tt(ymixT[:, 8 + c, 0:TT], ac[:, :], pb_[1][:, 0:TT], ALU.mult, [ak, pb_[0]],
                                 [("ymix", 8 + c, "p")])
                            g.cp("dve", uhist[:, c, :], ub[:, TT:TT + 2], [uk], [("uhist", c)])
                        else:
                            u_s = smp[:, c, :]
                            v_s = smp2[:, c, :]
                            g.tt(u_s, ct[:, 0:NS], ph_[1][:, 0:NS], ALU.mult, [("sg", sgi), ph_[0]], [("smp", c)])
                            usv = us[:, c, :].rearrange("p (b k) -> p b k", k=2)
                            g.ts(v_s, usv[:, :, 0], w0, ALU.mult, [("us", c), "vecs"], [("smp2", c)])
                            g.stt(v_s, usv[:, :, 1], w1, v_s, ALU.mult, ALU.add, [("us", c), ("smp2", c)], [("smp2", c)])
                            g.stt(v_s, u_s, w2, v_s, ALU.mult, ALU.add, [("smp", c), ("smp2", c)], [("smp2", c)])
                            g.tt(ymixT[:, 8 + c, TT:TT + NS], v_s, pb_[1][:, 0:NS], ALU.mult, [("smp2", c), pb_[0]],
                                 [("ymix", 8 + c, "s")])
                            bk = bank()
                            g.tr(bk[0], bk[1][0:NS, 0:128], u_s, ident_f, [("smp", c), "cmat"])
                            g.cp("act", urow[:, c * 128:(c + 1) * 128], bk[1][0:NS, 0:128], [bk[0]], [("urow", c)])
            if last:
                spdma("urow", o_sc_s[:, 1:2, :], urow[:, :].rearrange("b (o c) -> b o c", o=1),
                      [("urow", c) for c in range(8)], ["o_sc_s1"])

            pend_b = [None]
            for s in range(6):
                kx, tx = wloadA(win_v[:, :, O_XBC + s * 256:O_XBC + (s + 1) * 256], 256)
                for jj in range(2):
                    c = s * 2 + jj
                    cw = [vecs[:, V_CW + c * 4 + i:V_CW + c * 4 + i + 1] for i in range(4)]
                    cb = vecs[:, V_CB + c:V_CB + c + 1]
                    for si, (c0, n) in enumerate(segs):
                        px = bank()
                        for k in range(KC):
                            g.mm(px[0], px[1][:, 0:n], tx[:, k, jj * 128:(jj + 1) * 128], hT[:, k, c0:c0 + n],
                                 [kx, ("hT", k)], start=(k == 0), stop=(k == KC - 1))
                        if si == 0:
                            rb = rawb[c % 2]
                            rk_ = ("rawb", c % 2)
                            g.cp("act", rb[:, 0:3], hist[:, c, :], [("hist", c)], [rk_])
                            g.cp("act", rb[:, 3:3 + TT], px[1][:, 0:TT], [px[0]], [rk_])
                            ac = accb[c % 2]
                            ak = ("accb", c % 2)
                            g.act(ac[:, :], rb[:, 0:TT], AF.Identity, [rk_, "vecs"], [ak], scale=cw[0])
                            for i in range(1, 4):
                                g.stt(ac[:, :], rb[:, i:i + TT], cw[i], ac[:, :], ALU.mult, ALU.add, [rk_, ak], [ak])
                            g.cp("dve", hist[:, c, :], rb[:, TT:TT + 3], [rk_], [("hist", c)])

                            def stage_b(c=c, ac=ac, ak=ak, cb=cb):
                                if c < 8:
                                    g.act(xsT[:, c, 0:TT], ac[:, :], AF.Silu, [ak, "vecs"], [("xsT", c, "p")], bias=cb)
                                else:
                                    g.act(BCT[:, c - 8, 0:TT], ac[:, :], AF.Silu, [ak, "vecs"], [("BCT", c - 8)], bias=cb)
                            if pend_b[0] is not None:
                                pend_b[0]()
                            pend_b[0] = stage_b
                        else:
                            rs = ar_small_raw[c]
                            g.cp("act", rs, px[1][:, 0:NS], [px[0]], [("rs", c)])
                            hv = hs[:, c, :].rearrange("p (b k) -> p b k", k=3)
                            a_s = ar_small_acc[c]
                            g.ts(a_s, hv[:, :, 0], cw[0], ALU.mult, [("hs", c), "vecs"], [("as", c)])
                            g.stt(a_s, hv[:, :, 1], cw[1], a_s, ALU.mult, ALU.add, [("hs", c), ("as", c)], [("as", c)])
                            g.stt(a_s, hv[:, :, 2], cw[2], a_s, ALU.mult, ALU.add, [("hs", c), ("as", c)], [("as", c)])
                            g.stt(a_s, rs, cw[3], a_s, ALU.mult, ALU.add, [("rs", c), ("as", c)], [("as", c)])
                            if c < 8:
                                g.act(xsT[:, c, TT:TT + NS], a_s, AF.Silu, [("as", c), "vecs"], [("xsT", c, "s")], bias=cb)
                            else:
                                g.act(BCs_f[:, c - 8, :], a_s, AF.Silu, [("as", c), "vecs"], [("BCs", c - 8)], bias=cb)
                            bk = bank()
                            g.tr(bk[0], bk[1][0:NS, 0:128], rs, ident_f, [("rs", c), "cmat"])
                            g.cp("act", cvrow[:, c * 128:(c + 1) * 128], bk[1][0:NS, 0:128], [bk[0]], [("cvrow", c)])
            if pend_b[0] is not None:
                pend_b[0]()
            if CFG.get("wb_prefetch", True) and CFG["ffn2"]:
                prefetch_wB(1)
            if last:
                spdma("cvrow", o_cv_s[:, 2:3, :], cvrow[:, :].rearrange("b (o c) -> b o c", o=1),
                      [("cvrow", c) for c in range(12)], ["o_cv_s2"])
            if last or CFG.get("all_barriers", False):
                g.barrier()
            return dict(Wz=Wz, Wdt=Wdt, ymixT=ymixT, xsT=xsT, BCT=BCT, BCs_f=BCs_f, R2=R2)

        ar_small = sb("ar_small", [128, 24, NS], F32)
        ar_small_raw = [ar_small[:, c, :] for c in range(12)]
        ar_small_acc = [ar_small[:, 12 + c, :] for c in range(12)]

        def ssd_blocks(t, mx):
            g.embed = CFG.get("embed_mix", False)
            Wz, Wdt, ymixT, xsT, BCT = mx["Wz"], mx["Wdt"], mx["ymixT"], mx["xsT"], mx["BCT"]
            ar.reset(mx["R2"])
            zs = ar.alloc([128, 1024], F32)
            sm = ar.alloc([128, 16, 64], F32)
            xs_tm = ar.alloc([128, 1024], F32)
            xdt = ar.alloc([128, 1024], BF16)
            xdtd = ar.alloc([128, 1024], BF16)
            B_tm = ar.alloc([128, 2, 128], BF16)
            CBm = ar.alloc([128, 2, 128], F32)
            rseg = ar.alloc([128, 8, 128], F32)
            Lm = ar.alloc([128, 8, 128], F32)
            G = ar.alloc([128, 16, 128], BF16)
            y1 = ar.alloc([128, 1024], F32)
            yg = ar.alloc([128, 1024], F32)
            yn = ar.alloc([128, 1024], BF16)
            junk2 = ar.alloc([128, 2, 512], F32)
            hk = [("hT", k) for k in range(KC)]
            dtr, dtt, dta, dte, dtl, dtv, av, acum, totb, eacum, dend, etot, dtde, ss2, ln2, rs2 = [sm[:, i, 0:16] for i in range(16)]
            rsegq = [rseg[:, 0:4, :], rseg[:, 4:8, :]]
            Lmq = [Lm[:, 0:4, :], Lm[:, 4:8, :]]

            def z_mm(b):
                cb0_ = b * 128
                bzs = []
                for hf in range(2):
                    bz = bank()
                    for k in range(KC):
                        g.mm(bz[0], bz[1][:, :], hT[:, k, cb0_:cb0_ + 128], Wz[:, k, hf * 512:(hf + 1) * 512],
                             [("hT", k), "Wz"], start=(k == 0), stop=(k == KC - 1))
                    bzs.append(bz)
                return bzs

            def z_act(bzs):
                for hf in range(2):
                    g.act(zs[:, hf * 512:(hf + 1) * 512], bzs[hf][1][:, :], AF.Silu, [bzs[hf][0]], [("zs", hf)])
                g.act(sm[:, 12, 32:33], one_t, AF.Exp, ["smallc"], ["tblwarm"])

            z_act(z_mm(0))
            for b in range(NBLK):
                cb0 = b * 128
                bd = bank()
                for k in range(KC):
                    g.mm(bd[0], bd[1][:, 0:16], hT[:, k, cb0:cb0 + 128], Wdt[:, k, :], [("hT", k), "Wdt"],
                         start=(k == 0), stop=(k == KC - 1))
                g.tt(dtt, bd[1][:, 0:16], rowc[:, R_DTB:R_DTB + 16], ALU.add, [bd[0], "rowc"], ["dtt"])
                g.act(dta, dtt, AF.Abs, ["dtt"], ["dta"])
                g.act(dte, dta, AF.Exp, ["dta"], ["dte"], scale=-1.0)
                g.act(dtl, dte, AF.Ln, ["dte", "smallc"], ["dtl"], bias=one_t)
                for hf in range(2):
                    bx = bank()
                    for q in range(4):
                        c = hf * 4 + q
                        g.tr(bx[0], bx[1][:, q * 128:(q + 1) * 128], xsT[:, c, cb0:cb0 + 128], ident_f,
                             [("xsT", c, "p"), "cmat"], last=(q == 3))
                    sl = slice(hf * 512, (hf + 1) * 512)
                    g.cp("act", xs_tm[:, sl], bx[1][:, :], [bx[0]], [("xs_tm", hf)])
                for gi in range(2):
                    g.tr("psbf", psbf[:, gi * 128:(gi + 1) * 128], BCT[:, gi, cb0:cb0 + 128], ident_b,
                         [("BCT", gi), "cbf"], last=(gi == 1))
                g.cp("act", B_tm[:, :, :], psbf[:, 0:256].rearrange("p (a b) -> p a b", a=2), ["psbf"], ["B_tm"])
                g.stt(dtv, dtt, 0.0, dtl, ALU.max, ALU.add, ["dtt", "dtl"], ["dtv"])
                g.tt(av, dtv, A_bc, ALU.mult, ["dtv", "smallc"], ["av"])
                ba = bank()
                g.mm(ba[0], ba[1][:, 0:16], U_f, av, ["cmat", "av"])
                g.mm(ba[0], ba[1][:, 16:32], ones_f, av, ["cmat", "av"])
                g.cp("dve", acum, ba[1][:, 0:16], [ba[0]], ["acum"])
                g.act(eacum, ba[1][:, 0:16], AF.Exp, [ba[0]], ["eacum"])
                g.act(etot, ba[1][:, 16:32], AF.Exp, [ba[0]], ["etot"])
                g.tt(dend, ba[1][:, 16:32], acum, ALU.subtract, [ba[0], "acum"], ["dend"])
                g.act(dend, dend, AF.Exp, ["dend"], ["dend"])
                g.tt(dtde, dtv, dend, ALU.mult, ["dtv", "dend"], ["dtde"])
                bc = bank()
                for gi in range(2):
                    g.mm(bc[0], bc[1][:, gi * 128:(gi + 1) * 128], BCT[:, gi, cb0:cb0 + 128], BCT[:, 2 + gi, cb0:cb0 + 128],
                         [("BCT", gi), ("BCT", 2 + gi)])
                for gi in range(2):
                    g.tt(CBm[:, gi, :], bc[1][:, gi * 128:(gi + 1) * 128], U_f, ALU.mult, [bc[0], "cmat"], [("CBm", gi)])
                def seg_a(qq):
                    rq, lq = rsegq[qq % 2], Lmq[qq % 2]
                    g.tt(rq, U_f.unsqueeze(1).to_broadcast([128, 4, 128]),
                         av[:, qq * 4:(qq + 1) * 4].unsqueeze(2).to_broadcast([128, 4, 128]), ALU.mult,
                         ["cmat", "av"], [("rseg", qq % 2)])
                    bs = bank()
                    g.mm(bs[0], bs[1][:, :], Ms_f, rq.rearrange("p a b -> p (a b)"), ["cmat", ("rseg", qq % 2)])
                    g.act(lq.rearrange("p a b -> p (a b)"), bs[1][:, :], AF.Exp, [bs[0]], [("Lm", qq % 2)])

                def seg_b(qq):
                    lq = Lmq[qq % 2]
                    g.tt(G[:, qq * 4:(qq + 1) * 4, :], lq, CBm[:, qq // 2, :].unsqueeze(1).to_broadcast([128, 4, 128]), ALU.mult,
                         [("Lm", qq % 2), ("CBm", qq // 2)], [("G", qq)])

                def xmul(hf):
                    sl = slice(hf * 512, (hf + 1) * 512)
                    g.tt(xdt[:, sl].rearrange("p (h d) -> p h d", h=8), xs_tm[:, sl].rearrange("p (h d) -> p h d", h=8),
                         dtv[:, hf * 8:(hf + 1) * 8].unsqueeze(2).to_broadcast([128, 8, 64]), ALU.mult,
                         [("xs_tm", hf), "dtv"], [("xdt", hf)])
                    g.tt(xdtd[:, sl].rearrange("p (h d) -> p h d", h=8), xs_tm[:, sl].rearrange("p (h d) -> p h d", h=8),
                         dtde[:, hf * 8:(hf + 1) * 8].unsqueeze(2).to_broadcast([128, 8, 64]), ALU.mult,
                         [("xs_tm", hf), "dtde"], [("xdtd", hf)])
                    g.tt(yg[:, sl], xs_tm[:, sl], rowc[:, R_D + hf * 512:R_D + (hf + 1) * 512], ALU.mult,
                         [("xs_tm", hf), "rowc"], [("yg", hf)])

                seg_a(0)
                seg_a(1)
                xmul(0)
                seg_b(0)
                seg_a(2)
                xmul(1)
                seg_b(1)
                seg_a(3)
                seg_b(2)
                seg_b(3)
                by = [bank(), bank()]
                for h in range(16):
                    g.mm(by[h // 8][0], by[h // 8][1][:, (h % 8) * 64:(h % 8 + 1) * 64], G[:, h, :],
                         xdt[:, h * 64:(h + 1) * 64], [("G", h // 4), ("xdt", h // 8)])
                g.cp("act", S_bf[:, :], S_f[:, :], ["S_f"], ["S_bf"])
                bo = [bank(), bank()]
                for gi in range(2):
                    g.mm(bo[gi][0], bo[gi][1][:, :], BCT[:, 2 + gi, cb0:cb0 + 128], S_bf[:, gi * 512:(gi + 1) * 512],
                         [("BCT", 2 + gi), "S_bf"])
                for gi in range(2):
                    sl = slice(gi * 512, (gi + 1) * 512)
                    g.cp("act", y1[:, sl], bo[gi][1][:, :], [bo[gi][0]], [("y1", gi)])
                    g.tt(y1[:, sl].rearrange("p (h d) -> p h d", h=8), y1[:, sl].rearrange("p (h d) -> p h d", h=8),
                         eacum[:, gi * 8:(gi + 1) * 8].unsqueeze(2).to_broadcast([128, 8, 64]), ALU.mult,
                         [("y1", gi), "eacum"], [("y1", gi)])
                    g.tt(y1[:, sl], y1[:, sl], by[gi][1][:, :], ALU.add, [("y1", gi), by[gi][0]], [("y1", gi)])
                    g.tt(yg[:, sl], yg[:, sl], y1[:, sl], ALU.add, [("yg", gi), ("y1", gi)], [("yg", gi)])
                    g.tt(yg[:, sl], yg[:, sl], zs[:, sl], ALU.mult, [("yg", gi), ("zs", gi)], [("yg", gi)])
                bzn = z_mm(b + 1) if b + 1 < NBLK else None
                for gi in range(2):
                    sl = slice(gi * 512, (gi + 1) * 512)
                    g.act(junk2[:, gi, :], yg[:, sl], AF.Square, [("yg", gi)], [("junk", gi)])
                    g.op("dve", lambda e, gi=gi, ss2=ss2: e.tensor_reduce(out=ss2[:, gi:gi + 1], in_=junk2[:, gi, :], axis=AX.X, op=ALU.add),
                         reads=[("junk", gi)], writes=[("ss2", gi)])
                g.act(ln2[:, 0:2], ss2[:, 0:2], AF.Ln, [("ss2", 0), ("ss2", 1), "smallc"], ["ln2"], bias=eps_t, scale=1.0 / 512)
                g.act(rs2[:, 0:2], ln2[:, 0:2], AF.Exp, ["ln2"], ["rs2"], scale=-0.5)
                for gi in range(2):
                    sl = slice(gi * 512, (gi + 1) * 512)
                    g.stt(yn[:, sl], yg[:, sl], rs2[:, gi:gi + 1], rowc[:, R_SNW + gi * 512:R_SNW + (gi + 1) * 512],
                          ALU.mult, ALU.mult, [("yg", gi), "rs2", "rowc"], [("yn", gi)])
                if bzn is not None:
                    z_act(bzn)
                for c in range(8):
                    g.tr("psbf", psbf[:, c * 128:(c + 1) * 128], yn[:, c * 128:(c + 1) * 128], ident_b,
                         [("yn", c // 4), "cbf"], last=(c == 7))
                g.cp("act", ymixT[:, 0:8, cb0:cb0 + 128], psbf[:, :].rearrange("p (a b) -> p a b", a=8), ["psbf"],
                     [("ymix", c, "p", b) for c in range(8)])
                bn = [bank(), bank()]
                for gi in range(2):
                    g.mm(bn[gi][0], bn[gi][1][:, :], B_tm[:, gi, :], xdtd[:, gi * 512:(gi + 1) * 512],
                         ["B_tm", ("xdtd", gi)])
                for gi in range(2):
                    sl = slice(gi * 512, (gi + 1) * 512)
                    g.tt(S_f[:, sl].rearrange("p (h d) -> p h d", h=8), S_f[:, sl].rearrange("p (h d) -> p h d", h=8),
                         etot[:, gi * 8:(gi + 1) * 8].unsqueeze(2).to_broadcast([128, 8, 64]), ALU.mult,
                         ["S_f", "etot"], ["S_f"])
                    g.tt(S_f[:, sl], S_f[:, sl], bn[gi][1][:, :], ALU.add, ["S_f", bn[gi][0]], ["S_f"])
            g.barrier()

        def ssd_sample(mx):
            g.embed = CFG.get("embed_mix", False)
            Wz, Wdt, ymixT, xsT, BCs_f = mx["Wz"], mx["Wdt"], mx["ymixT"], mx["xsT"], mx["BCs_f"]
            ar.reset(mx["R2"])
            Sb = [ar.alloc([128, 8, 128], F32) for _ in range(3)]
            tmpb = ar.alloc([128, 8, 128], F32)
            tmp2 = ar.alloc([128, 8, 128], F32)
            BCbb = [ar.alloc([128, 512], F32) for _ in range(2)]
            dtE = ar.alloc([NS, 2, 1024], F32)
            dtF = ar.alloc([128, 2, 8, NS], F32)
            dtxF = ar.alloc([128, 8, NS], F32)
            ysT = ar.alloc([128, 8, NS], F32)
            zT = ar.alloc([128, 8, NS], F32)
            ygT = ar.alloc([128, 8, NS], F32)
            sqT = ar.alloc([128, 8, NS], F32)
            rsb = ar.alloc([128, 2, NS], F32)
            BC_tm = ar.alloc([NS, 4, 128], F32)
            sel = [ar.alloc([NS, 128], F32) for _ in range(2)]
            smt = ar.alloc([NS, 8, 16], F32)
            dtt, dta, dte, dtl, dtv, av, dA = [smt[:, i, :] for i in range(7)]
            sc0, sc1 = TT, TT + NS
            bd = bank()
            for k in range(KC):
                g.mm(bd[0], bd[1][0:NS, 0:16], hT[:, k, sc0:sc1], Wdt[:, k, :], [("hT", k), "Wdt"],
                     start=(k == 0), stop=(k == KC - 1))
            g.tt(dtt, bd[1][0:NS, 0:16], rowc[0:NS, R_DTB:R_DTB + 16], ALU.add, [bd[0], "rowc"], ["s_dtt"])
            g.act(dta, dtt, AF.Abs, ["s_dtt"], ["s_dta"])
            g.act(dte, dta, AF.Exp, ["s_dta"], ["s_dte"], scale=-1.0)
            g.act(dtl, dte, AF.Ln, ["s_dte", "smallc"], ["s_dtl"], bias=one_t[0:NS, :])
            g.stt(dtv, dtt, 0.0, dtl, ALU.max, ALU.add, ["s_dtt", "s_dtl"], ["s_dtv"])
            g.tt(av, dtv, A_bc[0:NS, :], ALU.mult, ["s_dtv", "smallc"], ["s_av"])
            g.act(dA, av, AF.Exp, ["s_av"], ["s_dA"])
            g.cp("dve", dtE[:, 0, :].rearrange("p (h d) -> p h d", h=16), dtv.unsqueeze(2).to_broadcast([NS, 16, 64]),
                 ["s_dtv"], [("dtE", 0)])
            g.cp("dve", dtE[:, 1, :].rearrange("p (h d) -> p h d", h=16), dA.unsqueeze(2).to_broadcast([NS, 16, 64]),
                 ["s_dA"], [("dtE", 1)])
            for w_ in range(2):
                for hf in range(2):
                    bk = bank()
                    for q in range(4):
                        c = hf * 4 + q
                        g.tr(bk[0], bk[1][:, q * NS:(q + 1) * NS], dtE[:, w_, c * 128:(c + 1) * 128], ident_f[0:NS, 0:NS],
                             [("dtE", w_), "cmat"], last=(q == 3))
                    g.cp("act", dtF[:, w_, hf * 4:(hf + 1) * 4, :], bk[1][:, 0:4 * NS].rearrange("p (q c) -> p q c", q=4),
                         [bk[0]], [("dtF", w_, hf)])
            dk = [("dtF", w_, hf) for w_ in range(2) for hf in range(2)]
            xk = [("xsT", c, "s") for c in range(8)]
            g.tt(dtxF[:, :, :], xsT[:, :, sc0:sc1], dtF[:, 0, :, :], ALU.mult, xk + dk, ["dtxF"])
            bk = bank()
            for i in range(4):
                g.tr(bk[0], bk[1][0:NS, i * 128:(i + 1) * 128], BCs_f[:, i, :], ident_f, [("BCs", i), "cmat"], last=(i == 3))
            g.cp("act", BC_tm[:, :, :], bk[1][0:NS, :].rearrange("p (a b) -> p a b", a=4), [bk[0]], ["BC_tm"])
            NSL = 3

            def s_load(b):
                spdma(f"ssmin{b % NSL}", Sb[b % NSL][:, :, :], ssm_in[b].rearrange("c q n -> q c n"), (), [("Sb", b % NSL)])

            def s_stage_a(b):
                sl_ = sel[b % 2]
                g.cp("dve", sl_[:, :], ident_f[0:NS, b:b + 1].to_broadcast([NS, 128]), ["cmat"], [("sel", b % 2)])
                bb = bank()
                g.mm(bb[0], bb[1][:, :], sl_[:, :], BC_tm[:, :, :].rearrange("p a b -> p (a b)"), [("sel", b % 2), "BC_tm"])
                g.cp("act", BCbb[b % 2][:, :], bb[1][:, :], [bb[0]], [("BCb", b % 2)])

            def s_stage_b(b):
                S = Sb[b % NSL]
                sk = ("Sb", b % NSL)
                BCb = BCbb[b % 2]
                g.tt(S[:, :, :], S[:, :, :], dtF[:, 1, :, b:b + 1].to_broadcast([128, 8, 128]), ALU.mult, [sk] + dk, [sk])
                Bv = BCb[:, 0:256].rearrange("p (g n) -> p g n", g=2).unsqueeze(2).to_broadcast([128, 2, 4, 128])
                Cv = BCb[:, 256:512].rearrange("p (g n) -> p g n", g=2).unsqueeze(2).to_broadcast([128, 2, 4, 128])
                g.tt(tmpb[:, :, :].rearrange("p (g q) n -> p g q n", g=2), Bv,
                     dtxF[:, :, b:b + 1].to_broadcast([128, 8, 128]).rearrange("p (g q) n -> p g q n", g=2), ALU.mult,
                     [("BCb", b % 2), "dtxF"], ["tmpb"])
                g.tt(S[:, :, :], S[:, :, :], tmpb[:, :, :], ALU.add, [sk, "tmpb"], [sk])
                spdma(f"ssmout{b % NSL}", o_ssm_s[b].rearrange("c q n -> q c n"), S[:, :, :], [sk], [("o_ssm_s", b)])
                g.tt(tmp2[:, :, :].rearrange("p (g q) n -> p g q n", g=2), Cv,
                     S[:, :, :].rearrange("p (g q) n -> p g q n", g=2), ALU.mult, [("BCb", b % 2), sk], ["tmp2"])
                g.op("dve", lambda e, b=b: e.tensor_reduce(out=ysT[:, :, b], in_=tmp2[:, :, :], axis=AX.X, op=ALU.add),
                     reads=["tmp2"], writes=[("ysT", b)])
                if b + NSL < NS:
                    s_load(b + NSL)

            for b in range(NSL):
                s_load(b)
            s_stage_a(0)
            for b in range(NS):
                if b + 1 < NS:
                    s_stage_a(b + 1)
                s_stage_b(b)
            yk = [("ysT", b) for b in range(NS)]
            g.tt(tmpb[:, 0, 0:8 * NS].rearrange("p (c b) -> p c b", c=8), xsT[:, :, sc0:sc1],
                 vecs[:, V_DFM:V_DFM + 8].unsqueeze(2).to_broadcast([128, 8, NS]), ALU.mult, xk + ["vecs", "tmpb"], ["tmpb"])
            g.tt(ysT[:, :, :], ysT[:, :, :], tmpb[:, 0, 0:8 * NS].rearrange("p (c b) -> p c b", c=8), ALU.add,
                 yk + ["tmpb"], ["ysTall"])
            for hf in range(2):
                bk = bank()
                for q in range(4):
                    c = hf * 4 + q
                    for k in range(KC):
                        g.mm(bk[0], bk[1][:, q * NS:(q + 1) * NS], Wz[:, k, c * 128:(c + 1) * 128], hT[:, k, sc0:sc1],
                             ["Wz", ("hT", k)], start=(k == 0), stop=(k == KC - 1))
                g.act(zT[:, hf * 4:(hf + 1) * 4, :], bk[1][:, 0:4 * NS].rearrange("p (q c) -> p q c", q=4), AF.Silu,
                      [bk[0]], [("zT", hf)])
            g.tt(ygT[:, :, :], ysT[:, :, :], zT[:, :, :], ALU.mult, ["ysTall", ("zT", 0), ("zT", 1)], ["ygT"])
            g.tt(sqT[:, :, :], ygT[:, :, :], ygT[:, :, :], ALU.mult, ["ygT"], ["sqT"])
            bk = bank()
            for gi in range(2):
                for q in range(4):
                    g.mm(bk[0], bk[1][:, gi * NS:(gi + 1) * NS], ones_f, sqT[:, gi * 4 + q, :], ["cmat", "sqT"],
                         start=(q == 0), stop=(q == 3))
            g.act(rsb[:, :, :], bk[1][:, 0:2 * NS].rearrange("p (g b) -> p g b", g=2), AF.Ln, [bk[0], "smallc"], ["rsb"],
                  bias=eps_t, scale=1.0 / 512)
            g.act(rsb[:, :, :], rsb[:, :, :], AF.Exp, ["rsb"], ["rsb"], scale=-0.5)
            for c in range(8):
                g.stt(ymixT[:, c, sc0:sc1], ygT[:, c, :], vecs[:, V_SNW + c:V_SNW + c + 1], rsb[:, c // 4, :],
                      ALU.mult, ALU.mult, ["ygT", "vecs", "rsb"], [("ymix", c, "s")])
            g.barrier()

        def out_proj(t, mx):
            g.embed = CFG.get("embed_ffn", True)
            ymixT = mx["ymixT"]
            segs = segs_of(t)
            for m in range(KC):
                i = wA_i[0] % 5
                wA_i[0] += 1
                key = ("wA", i)
                dst = wA[i][:, :, :].rearrange("p k c -> p (k c)")[:, 0:2048].rearrange("p (c m) -> p c m", c=16)
                src = wout_v[:, :, m * 128:(m + 1) * 128]
                g.dma("pool", lambda e, dst=dst, src=src: e.dma_start(out=dst, in_=src), (), [key], wA_s[i])
                for si, (c0, n) in enumerate(segs):
                    bo = bank()
                    for c in range(16):
                        if si == 1:
                            yk_ = [("ymix", c, "s")]
                        elif c >= 8:
                            yk_ = [("ymix", c, "p")]
                        else:
                            yk_ = [("ymix", c, "p", b_) for b_ in range(NBLK)]
                        g.mm(bo[0], bo[1][:, 0:n], dst[:, c, :], ymixT[:, c, c0:c0 + n], [key] + yk_, start=(c == 0), stop=(c == 15))
                    g.tt(xT[:, m, c0:c0 + n], bo[1][:, 0:n], xT[:, m, c0:c0 + n], ALU.add, [bo[0], ("xT", m)], [("xT", m)])
            if CFG.get("all_barriers", False):
                g.barrier()

        def state_outputs():
            g.embed = False
            ar.reset(0)
            Sout = ar.alloc([128, 8, 128], F32)
            crow = ar.alloc([3, 1536], F32)
            srow = ar.alloc([2, 1024], F32)
            for hf in range(2):
                bk = bank()
                for q in range(4):
                    c = hf * 4 + q
                    g.tr(bk[0], bk[1][:, q * 128:(q + 1) * 128], S_f[:, c * 128:(c + 1) * 128], ident_f, ["S_f", "cmat"],
                         last=(q == 3))
                g.cp("act", Sout[:, hf * 4:(hf + 1) * 4, :], bk[1][:, :].rearrange("p (q c) -> p q c", q=4), [bk[0]],
                     [("Sout", hf)])
            spdma("sout", o_ssm_p.rearrange("c q n -> q c n"), Sout[:, :, :], [("Sout", 0), ("Sout", 1)], ["o_ssm_p"])
            for c in range(12):
                bk = bank()
                g.tr(bk[0], bk[1][0:3, 0:128], hist[:, c, :], ident_f, [("hist", c), "cmat"])
                g.cp("act", crow[:, c * 128:(c + 1) * 128], bk[1][0:3, 0:128], [bk[0]], [("crow", c)])
            spdma("crow", o_cv_p[:, :], crow[:, :], [("crow", c) for c in range(12)], ["o_cv_p"])
            for c in range(8):
                bk = bank()
                g.tr(bk[0], bk[1][0:2, 0:128], uhist[:, c, :], ident_f, [("uhist", c), "cmat"])
                g.cp("act", srow[:, c * 128:(c + 1) * 128], bk[1][0:2, 0:128], [bk[0]], [("srow", c)])
            spdma("srow", o_sc_p[:, :], srow[:, :], [("srow", c) for c in range(8)], ["o_sc_p"])

        ph = [0]
        state_done = [False]
        if CFG.get("wb_prefetch", True):
            prefetch_wB(0)

        def ok():
            ph[0] += 1
            return ph[0] <= CFG.get("max_phase", 10 ** 9)

        for t in CFG["tiles"]:
            if ok():
                load_tile(t)
            if CFG["ffn1"]:
                for _ in range(CFG.get("ffn_rep", 1)):
                    if ok():
                        ffn(t, 0, V_NW1)
            if CFG["mixer"]:
                if ok():
                    mx = mixer(t)
                if CFG["ssd"]:
                    if ok():
                        ssd_blocks(t, mx)
                    if t == NTILE - 1 and CFG["sample"]:
                        if ok():
                            ssd_sample(mx)
                if CFG["outproj"]:
                    if ok():
                        out_proj(t, mx)
                if CFG["stateout"] and t == CFG["tiles"][-1] and CFG["ssd"] and not state_done[0]:
                    state_done[0] = True
                    state_outputs()
            if CFG["ffn2"]:
                for _ in range(CFG.get("ffn_rep", 1)):
                    if ok():
                        ffn(t, 1, V_NW2)
                if CFG.get("wb_prefetch", True) and t != CFG["tiles"][-1]:
                    prefetch_wB(0)
            if CFG.get("store", True):
                if ok():
                    store_tile(t)
        if CFG["stateout"] and not state_done[0] and ok():
            state_outputs()
        g.barrier(engines=("sp",))

        with nc.Block() as block:
            @block.tensor
            def _(e):
                g.replay("pe", e)

            @block.scalar
            def _(e):
                g.replay("act", e)

            @block.vector
            def _(e):
                g.replay("dve", e)

            @block.gpsimd
            def _(e):
                g.replay("pool", e)

            @block.sync
            def _(e):
                g.replay("sp", e)
    return nc


_NC_CACHE = {}


def _host_layout(inp):
    f = np.float32
    def fm(v, n):
        return np.ascontiguousarray(np.asarray(v, f).reshape(n, 128).T)
    vecs = np.zeros((128, NV), f)
    vecs[:, V_NW1:V_NW1 + 8] = fm(inp["norm_ffn1_w"][0], 8)
    vecs[:, V_NWM:V_NWM + 8] = fm(inp["norm_mix_w"][0], 8)
    vecs[:, V_NW2:V_NW2 + 8] = fm(inp["norm_ffn2_w"][0], 8)
    vecs[:, V_NWF:V_NWF + 8] = fm(inp["final_norm_w"], 8)
    cw = np.asarray(inp["ssd_conv_w"][0], f)
    vecs[:, V_CW:V_CW + 48] = cw.reshape(4, 12, 128).transpose(2, 1, 0).reshape(128, 48)
    vecs[:, V_CB:V_CB + 12] = fm(inp["ssd_conv_b"][0], 12)
    sw = np.asarray(inp["sconv_w"][0], f)
    vecs[:, V_SW:V_SW + 24] = sw.reshape(3, 8, 128).transpose(2, 1, 0).reshape(128, 24)
    dsk = np.repeat(np.asarray(inp["d_skip"][0], f), 64)
    vecs[:, V_DFM:V_DFM + 8] = fm(dsk, 8)
    vecs[:, V_SNW:V_SNW + 8] = fm(inp["ssd_norm_w"][0], 8)
    rowc = np.zeros((128, NR), f)
    rowc[:, R_D:R_D + 1024] = dsk[None, :]
    rowc[:, R_SNW:R_SNW + 1024] = np.asarray(inp["ssd_norm_w"][0], f)[None, :]
    rowc[:, R_ALOG:R_ALOG + 16] = np.asarray(inp["a_log"][0], f)[None, :]
    rowc[:, R_DTB:R_DTB + 16] = np.asarray(inp["dt_bias"][0], f)[None, :]
    rowc[:, R_ALOG4:R_ALOG4 + 64] = np.tile(np.asarray(inp["a_log"][0], f), 4)[None, :]
    rowc[:, R_DTB4:R_DTB4 + 64] = np.tile(np.asarray(inp["dt_bias"][0], f), 4)[None, :]
    cm = np.zeros((128, 512), f)
    i = np.arange(128)
    cm[:, 0:128] = np.eye(128, dtype=f)
    cm[:, 128:256] = (i[:, None] <= i[None, :]).astype(f)
    cm[:, 256:384] = (i[:, None] > i[None, :]).astype(f)
    cm[:, 384:512] = 1.0
    return vecs, rowc, cm


def kernel(**inp):
    f = np.float32
    if "nc" not in _NC_CACHE:
        _NC_CACHE["nc"] = build_program()
    nc = _NC_CACHE["nc"]
    vecs, rowc, cm = _host_layout(inp)
    shared = {
        "wg1": np.ascontiguousarray(inp["ffn1_w_gate"][0], f), "wu1": np.ascontiguousarray(inp["ffn1_w_up"][0], f),
        "wd1": np.ascontiguousarray(inp["ffn1_w_down"][0], f),
        "wg2": np.ascontiguousarray(inp["ffn2_w_gate"][0], f), "wu2": np.ascontiguousarray(inp["ffn2_w_up"][0], f),
        "wd2": np.ascontiguousarray(inp["ffn2_w_down"][0], f),
        "win": np.ascontiguousarray(inp["w_in"][0], f), "wout": np.ascontiguousarray(inp["w_out"][0], f),
        "vecs": vecs, "rowc": rowc, "cmat": cm,
    }
    in_maps = []
    for c in range(NCORES):
        m = dict(shared)
        sl = slice(c * NS, (c + 1) * NS)
        m["xp"] = np.ascontiguousarray(inp["x_prompt"][c], f)
        m["xsm"] = np.ascontiguousarray(inp["x_sample"][sl, 0, :], f)
        m["ssm_in"] = np.ascontiguousarray(inp["state_ssm"][0, sl], f).reshape(NS, 8, 128, 128)
        m["cst_in"] = np.ascontiguousarray(inp["state_ssd_conv"][0, sl], f)
        m["sst_in"] = np.ascontiguousarray(inp["state_sconv"][0, sl], f)
        in_maps.append(m)
    res = run_bass_kernel_spmd(nc, in_maps, core_ids=list(range(NCORES)))
    R = res.results
    y_prompt = np.stack([R[c]["yp"] for c in range(NCORES)], 0).astype(f)
    y_sample = np.concatenate([R[c]["ysm"] for c in range(NCORES)], 0).reshape(NCORES * NS, 1, D).astype(f)
    ssm_p = np.stack([R[c]["o_ssm_p"].reshape(16, 64, 128) for c in range(NCORES)], 0)[None].astype(f)
    cv_p = np.stack([R[c]["o_cv_p"] for c in range(NCORES)], 0)[None].astype(f)
    sc_p = np.stack([R[c]["o_sc_p"] for c in range(NCORES)], 0)[None].astype(f)
    ssm_s = np.concatenate([R[c]["o_ssm_s"].reshape(NS, 16, 64, 128) for c in range(NCORES)], 0)[None].astype(f)
    cv_s = np.concatenate([R[c]["o_cv_s"] for c in range(NCORES)], 0)[None].astype(f)
    sc_s = np.concatenate([R[c]["o_sc_s"] for c in range(NCORES)], 0)[None].astype(f)
    return (y_prompt, y_sample, ssm_p, cv_p, sc_p, ssm_s, cv_s, sc_s)
```
